# Optimizing a Trainium2 kernel written in Bass

```python
import math
import jax, jax.numpy as jnp
from jax import lax
import numpy as np

D_MODEL = 1024
BATCH = 8
SEQ = 8192
DEPTH = 2
DEC_BATCH = 32
DEC_SEQ = 16
PAST_LEN = 2048

CHUNK = 64
Q_BLOCK = 128
DA_HEADS = 8
DA_HEAD_DIM = 64
DA_K_ROW = 2 * DA_HEAD_DIM
DA_V_DIM = 2 * DA_HEAD_DIM
ROPE_DIM = DA_HEAD_DIM // 4
ROPE_THETA = 500000.0
GLA_HEADS = 4
GLA_DK = D_MODEL // 2 // GLA_HEADS
GLA_DV = D_MODEL // GLA_HEADS
GLA_RANK = 16
GLA_GATE_NORM = 16.0
D_FF = 2816
CONV_W = 3
EPS = 1e-6
DA_QK = DA_HEADS * 2 * DA_HEAD_DIM
DA_V = DA_HEADS * DA_V_DIM
GLA_QK = GLA_HEADS * GLA_DK
GLA_V = GLA_HEADS * GLA_DV
D_IN = 2 * DA_QK + DA_V + 2 * GLA_QK + 2 * GLA_V + GLA_RANK + 2 * D_MODEL

kernel_name = "diffattn_gla_gated_merge_streaming_step"


def _rmsnorm(x, w):
    xf = x.astype(jnp.float32)
    y = xf * lax.rsqrt(jnp.mean(xf * xf, axis=-1, keepdims=True) + EPS)
    return (y * w.astype(jnp.float32)).astype(x.dtype)


def _rope(x, pos):
    half = ROPE_DIM // 2
    inv = ROPE_THETA ** (-jnp.arange(half, dtype=jnp.float32) * 2.0 / ROPE_DIM)
    ang = pos.astype(jnp.float32)[:, None] * inv[None, :]
    cos = jnp.cos(ang)[None, :, None, None, :].astype(x.dtype)
    sin = jnp.sin(ang)[None, :, None, None, :].astype(x.dtype)
    x1 = x[..., :half]
    x2 = x[..., half:ROPE_DIM]
    return jnp.concatenate([x1 * cos - x2 * sin, x2 * cos + x1 * sin, x[..., ROPE_DIM:]], axis=-1)


def _diff_weights(s, lam):
    p = jax.nn.softmax(s, axis=-1)
    return p[:, :, 0] - lam * p[:, :, 1]


def _diff_attn_prompt(q, k, v, lam):
    B, S = q.shape[0], q.shape[1]
    nb = S // Q_BLOCK
    qb = jnp.moveaxis(q.reshape(B, nb, Q_BLOCK, DA_HEADS, 2, DA_HEAD_DIM), 1, 0)
    kpos = jnp.arange(S)
    vf = v.astype(jnp.float32)
    scale = DA_HEAD_DIM ** -0.5

    def block(args):
        qi, bi = args
        s = jnp.einsum('bqhcd,bkhcd->bhcqk', qi, k).astype(jnp.float32) * scale
        qpos = bi * Q_BLOCK + jnp.arange(Q_BLOCK)
        limit = (qpos // CHUNK + 1) * CHUNK
        mask = kpos[None, :] < limit[:, None]
        s = jnp.where(mask, s, -jnp.inf)
        w = _diff_weights(s, lam)
        return jnp.einsum('bhqk,bkhd->bqhd', w, vf)

    o = lax.map(block, (qb, jnp.arange(nb)))
    return jnp.moveaxis(o, 0, 1).reshape(B, S, DA_HEADS, DA_V_DIM)


def _diff_attn_sample(q, k_all, v_all, lam):
    s = jnp.einsum('bqhcd,bkhcd->bhcqk', q, k_all).astype(jnp.float32) * (DA_HEAD_DIM ** -0.5)
    w = _diff_weights(s, lam)
    return jnp.einsum('bhqk,bkhd->bqhd', w, v_all.astype(jnp.float32))


def _gla_chunk(S, inp):
    q, k, v, g = inp
    C = q.shape[2]
    b = jnp.cumsum(g, axis=2)
    o_inter = jnp.einsum('bhtk,bhkv->bhtv', q * jnp.exp(b), S)
    causal = jnp.tril(jnp.ones((C, C), dtype=bool))
    rel = b[:, :, :, None, :] - b[:, :, None, :, :]
    decay = jnp.exp(jnp.where(causal[:, :, None], rel, -jnp.inf))
    A = jnp.sum(q[:, :, :, None, :] * k[:, :, None, :, :] * decay, axis=-1)
    o = o_inter + jnp.einsum('bhts,bhsv->bhtv', A, v)
    b_last = b[:, :, -1:, :]
    S_new = (jnp.exp(b_last[:, :, 0, :])[..., None] * S
             + jnp.einsum('bhsk,bhsv->bhkv', k * jnp.exp(b_last - b), v))
    return S_new, o


def _gla(q, k, v, g, S0):
    B, H, L = q.shape[0], q.shape[1], q.shape[2]
    C = CHUNK if L % CHUNK == 0 else L
    n = L // C

    def split(t):
        return jnp.moveaxis(t.reshape(B, H, n, C, t.shape[-1]), 2, 0)

    S, o = lax.scan(_gla_chunk, S0, (split(q), split(k), split(v), split(g)))
    o = jnp.moveaxis(o, 0, 2).reshape(B, H, L, GLA_DV)
    return o, S


def _layer(x, pos, kv_past, S0, conv_past, li, w_in, w_gk2, b_gk2, lq1, lk1, lq2, lk2,
           da_norm_w, gla_norm_w, w_o, pre_mix_w, post_mix_w, pre_ffn_w, post_ffn_w,
           w_up, conv_w, conv_b, w_down):
    B, L = x.shape[0], x.shape[1]
    f32 = jnp.float32
    xn = _rmsnorm(x, pre_mix_w)
    proj = xn @ w_in
    sizes = (DA_QK, DA_QK, DA_V, GLA_QK, GLA_QK, GLA_V, GLA_V, GLA_RANK, D_MODEL)
    cuts = []
    acc = 0
    for sz in sizes:
        acc += sz
        cuts.append(acc)
    qa, ka, va, qg, kg, vg, rg, lr, ga, gb = jnp.split(proj, cuts, axis=-1)

    qa = _rope(qa.reshape(B, L, DA_HEADS, 2, DA_HEAD_DIM), pos)
    ka = _rope(ka.reshape(B, L, DA_HEADS, 2, DA_HEAD_DIM), pos)
    va = va.reshape(B, L, DA_HEADS, DA_V_DIM)
    lam_init = 0.8 - 0.6 * math.exp(-0.3 * li)
    lam = (jnp.exp(jnp.sum(lq1.astype(f32) * lk1.astype(f32)))
           - jnp.exp(jnp.sum(lq2.astype(f32) * lk2.astype(f32))) + lam_init)
    if kv_past is None:
        oa = _diff_attn_prompt(qa, ka, va, lam)
    else:
        k_past, v_past = kv_past
        P = k_past.shape[1]
        k_all = jnp.concatenate(
            [k_past.reshape(B, P, DA_HEADS, 2, DA_HEAD_DIM).astype(ka.dtype), ka], axis=1)
        v_all = jnp.concatenate([v_past.astype(va.dtype), va], axis=1)
        oa = _diff_attn_sample(qa, k_all, v_all, lam)
    oa = (_rmsnorm(oa, da_norm_w) * (1.0 - lam_init)).reshape(B, L, D_MODEL).astype(x.dtype)
    new_k = ka.reshape(B, L, DA_HEADS, DA_K_ROW)
    new_v = va

    def heads(t, d):
        return t.reshape(B, L, GLA_HEADS, d).transpose(0, 2, 1, 3).astype(f32)

    gk = jax.nn.log_sigmoid((lr @ w_gk2 + b_gk2).astype(f32)) / GLA_GATE_NORM
    og, S_new = _gla(heads(qg, GLA_DK) * (GLA_DK ** -0.5), heads(kg, GLA_DK),
                     heads(vg, GLA_DV), heads(gk, GLA_DK), S0.astype(f32))
    og = og.transpose(0, 2, 1, 3)
    og = _rmsnorm(og, gla_norm_w) * jax.nn.silu(rg.astype(f32)).reshape(B, L, GLA_HEADS, GLA_DV)
    ob = og.reshape(B, L, D_MODEL).astype(x.dtype)

    merged = jax.nn.sigmoid(ga) * oa + jax.nn.sigmoid(gb) * ob
    h = x + _rmsnorm(merged @ w_o, post_mix_w)

    hn = _rmsnorm(h, pre_ffn_w)
    u, g = jnp.split(hn @ w_up, [D_FF], axis=-1)
    gpad = jnp.concatenate([conv_past.astype(g.dtype), g], axis=1)
    gc = conv_b + sum(conv_w[j] * gpad[:, j:j + L] for j in range(CONV_W))
    ffn = (jax.nn.gelu(gc, approximate=True) * u) @ w_down
    out = h + _rmsnorm(ffn, post_ffn_w)
    return out, new_k, new_v, S_new, gpad[:, -(CONV_W - 1):]


def setup_inputs(seed: int = 0) -> dict:
    key = jax.random.key(seed)
    ks = jax.random.split(key, 32)
    nrm = jax.random.normal
    f32 = jnp.float32
    return {
        "x_prompt": nrm(ks[0], (BATCH, SEQ, D_MODEL), f32),
        "x_sample": nrm(ks[1], (DEC_BATCH, DEC_SEQ, D_MODEL), f32),
        "cache_k": nrm(ks[2], (DEPTH, DEC_BATCH, PAST_LEN, DA_HEADS, DA_K_ROW), f32),
        "cache_v": nrm(ks[3], (DEPTH, DEC_BATCH, PAST_LEN, DA_HEADS, DA_V_DIM), f32),
        "state_gla": nrm(ks[4], (DEPTH, DEC_BATCH, GLA_HEADS, GLA_DK, GLA_DV), f32),
        "state_conv": nrm(ks[5], (DEPTH, DEC_BATCH, CONV_W - 1, D_FF), f32),
        "w_in": nrm(ks[6], (DEPTH, D_MODEL, D_IN), f32) * D_MODEL ** -0.5,
        "w_gk2": nrm(ks[7], (DEPTH, GLA_RANK, GLA_QK), f32) * GLA_RANK ** -0.5,
        "b_gk2": nrm(ks[8], (DEPTH, GLA_QK), f32) * 0.1,
        "lambda_q1": nrm(ks[9], (DEPTH, DA_HEAD_DIM), f32) * 0.1,
        "lambda_k1": nrm(ks[10], (DEPTH, DA_HEAD_DIM), f32) * 0.1,
        "lambda_q2": nrm(ks[11], (DEPTH, DA_HEAD_DIM), f32) * 0.1,
        "lambda_k2": nrm(ks[12], (DEPTH, DA_HEAD_DIM), f32) * 0.1,
        "da_norm_w": 1.0 + 0.05 * nrm(ks[13], (DEPTH, DA_V_DIM), f32),
        "gla_norm_w": 1.0 + 0.05 * nrm(ks[14], (DEPTH, GLA_DV), f32),
        "w_o": nrm(ks[15], (DEPTH, D_MODEL, D_MODEL), f32) * D_MODEL ** -0.5,
        "pre_mix_w": 1.0 + 0.05 * nrm(ks[16], (DEPTH, D_MODEL), f32),
        "post_mix_w": 1.0 + 0.05 * nrm(ks[17], (DEPTH, D_MODEL), f32),
        "pre_ffn_w": 1.0 + 0.05 * nrm(ks[18], (DEPTH, D_MODEL), f32),
        "post_ffn_w": 1.0 + 0.05 * nrm(ks[19], (DEPTH, D_MODEL), f32),
        "w_up": nrm(ks[20], (DEPTH, D_MODEL, 2 * D_FF), f32) * D_MODEL ** -0.5,
        "conv_w": nrm(ks[21], (DEPTH, CONV_W, D_FF), f32) * CONV_W ** -0.5,
        "conv_b": nrm(ks[22], (DEPTH, D_FF), f32) * 0.02,
        "w_down": nrm(ks[23], (DEPTH, D_FF, D_MODEL), f32) * D_FF ** -0.5,
    }


def reference(x_prompt, x_sample, cache_k, cache_v, state_gla, state_conv,
              w_in, w_gk2, b_gk2, lambda_q1, lambda_k1, lambda_q2, lambda_k2,
              da_norm_w, gla_norm_w, w_o, pre_mix_w, post_mix_w, pre_ffn_w, post_ffn_w,
              w_up, conv_w, conv_b, w_down):
    B, S = x_prompt.shape[0], x_prompt.shape[1]
    L = x_sample.shape[1]
    P = cache_k.shape[2]
    pos_p = jnp.arange(S)
    pos_s = P + jnp.arange(L)
    hp, hs = x_prompt, x_sample
    kp_l, vp_l, sp_l, cp_l = [], [], [], []
    ks_l, vs_l, ss_l, cs_l = [], [], [], []
    for li in range(DEPTH):
        params = (w_in[li], w_gk2[li], b_gk2[li], lambda_q1[li], lambda_k1[li],
                  lambda_q2[li], lambda_k2[li], da_norm_w[li], gla_norm_w[li], w_o[li],
                  pre_mix_w[li], post_mix_w[li], pre_ffn_w[li], post_ffn_w[li],
                  w_up[li], conv_w[li], conv_b[li], w_down[li])
        S0_p = jnp.zeros((B, GLA_HEADS, GLA_DK, GLA_DV), jnp.float32)
        conv0_p = jnp.zeros((B, CONV_W - 1, D_FF), x_prompt.dtype)
        hp, kp, vp, sp, cp = _layer(hp, pos_p, None, S0_p, conv0_p, li, *params)
        hs, kn, vn, sn, cn = _layer(hs, pos_s, (cache_k[li], cache_v[li]),
                                    state_gla[li], state_conv[li], li, *params)
        kp_l.append(kp); vp_l.append(vp); sp_l.append(sp); cp_l.append(cp)
        ks_l.append(kn); vs_l.append(vn); ss_l.append(sn.astype(state_gla.dtype)); cs_l.append(cn)
    return (hp, hs,
            jnp.stack(kp_l), jnp.stack(vp_l), jnp.stack(sp_l), jnp.stack(cp_l),
            jnp.stack(ks_l), jnp.stack(vs_l), jnp.stack(ss_l), jnp.stack(cs_l))
```

```python
import math
import numpy as np
from contextlib import ExitStack
import concourse.bass as bass
import concourse.mybir as mybir
from concourse.bass_utils import run_bass_kernel_spmd

F32 = mybir.dt.float32
BF16 = mybir.dt.bfloat16
AF = mybir.ActivationFunctionType
ALU = mybir.AluOpType

D = 1024
T_FULL = 8192
NS = 64
NSEQ = 4
LS = 16
PAST = 2048
DIN = 8208
DFF = 2816
NJ = DFF // 128
EPS = 1e-6
DEPTH = 2


class Buf:
    __slots__ = ("w", "r")

    def __init__(self):
        self.w = None
        self.r = []


class Eng:
    def __init__(self, name, key):
        self.name = name
        self.key = key
        self.cnt = 0
        self.seen = {}
        self.prog = []
        self.dsems = []
        self.dnext = 0
        self.pending = None


class Rot:
    def __init__(self, items):
        self.items = items
        self.i = 0

    def next(self):
        it = self.items[self.i % len(self.items)]
        self.i += 1
        return it


class FW:
    def __init__(self, nc, stack, n_dma_sems=24):
        self.nc = nc
        self.gstack = stack
        self.stack = stack
        self.sems = {}
        self.semval = {}
        self.engs = {}
        for name in ("pe", "act", "dve", "pool", "sp"):
            key = None
            if name != "sp":
                key = "c_" + name
                self.sems[key] = stack.enter_context(nc.semaphore(key))
                self.semval[key] = 0
            self.engs[name] = Eng(name, key)
        for name in ("sp", "pool"):
            E = self.engs[name]
            for i in range(n_dma_sems if name == "sp" else 8):
                key = "d_%s_%d" % (name, i)
                self.sems[key] = stack.enter_context(nc.semaphore(key))
                self.semval[key] = 0
                E.dsems.append(key)
        self.n_ops = 0
        self.uid = 0

    def sb(self, name, shape, dt):
        self.uid += 1
        return self.stack.enter_context(self.nc.sbuf_tensor("%s_%d" % (name, self.uid), list(shape), dt))

    def ps(self, name, shape, dt=F32):
        self.uid += 1
        return self.stack.enter_context(self.nc.psum_tensor("%s_%d" % (name, self.uid), list(shape), dt))

    def tiles(self, name, n, shape, dt):
        return Rot([(self.sb(name + str(i), shape, dt), Buf()) for i in range(n)])

    def _deps(self, E, reads, writes):
        need = {}
        pe = E.name == "pe"

        def add(t, same_ok):
            if t is None:
                return
            k, v = t
            if k == E.key and not same_ok:
                return
            if need.get(k, 0) < v:
                need[k] = v

        for b in reads:
            add(b.w, not pe)
        for b in writes:
            add(b.w, not pe)
            for t in b.r:
                add(t, False)
        for k, v in need.items():
            if E.seen.get(k, 0) < v:
                E.seen[k] = v
                if k.startswith("c_"):
                    X = self.engs[k[2:]]
                    if v > X.cnt:
                        assert X.pending is not None and v == X.cnt + 1, (k, v, X.cnt)
                        X.pending["sig"] = True
                        X.pending = None
                        X.cnt += 1
                        self.semval[k] = X.cnt
                E.prog.append(lambda h, sem=self.sems[k], v=v: h.wait_ge(sem, v))

    def op(self, en, meth, *args, R=(), W=(), signal=True, **kw):
        E = self.engs[en]
        self._deps(E, R, W)
        ent = {"meth": meth, "args": args, "kw": kw, "sig": signal, "sem": self.sems[E.key]}
        E.prog.append(ent)
        if signal:
            E.cnt += 1
            self.semval[E.key] = E.cnt
            E.pending = None
            t = (E.key, E.cnt)
        else:
            E.pending = ent
            t = (E.key, E.cnt + 1)
        for b in R:
            b.r.append(t)
        for b in W:
            b.w = t
            b.r = []
        self.n_ops += 1

    def dma(self, en, out, in_, R=(), W=(), **kw):
        E = self.engs[en]
        self._deps(E, R, W)
        key = E.dsems[E.dnext % len(E.dsems)]
        E.dnext += 1
        prev = self.semval[key]
        if E.seen.get(key, 0) < prev:
            E.seen[key] = prev
            E.prog.append(lambda h, sem=self.sems[key], v=prev: h.wait_ge(sem, v))
        self.semval[key] = prev + 16
        sem = self.sems[key]
        E.prog.append(lambda h, out=out, in_=in_, sem=sem, kw=kw: h.dma_start(out=out, in_=in_, **kw).then_inc(sem, 16))
        t = (key, prev + 16)
        for b in R:
            b.r.append(t)
        for b in W:
            b.w = t
            b.r = []
        self.n_ops += 1

    def flush(self, final=False):
        for E in self.engs.values():
            if E.name == "sp" or final is False:
                for k, v in self.semval.items():
                    if v > 0 and k != E.key and E.seen.get(k, 0) < v:
                        E.seen[k] = v
                        E.prog.append(lambda h, sem=self.sems[k], v=v: h.wait_ge(sem, v))
        progs = {n: E.prog for n, E in self.engs.items()}
        for E in self.engs.values():
            E.prog = []
            E.pending = None

        def run(h, prog):
            for f in prog:
                if isinstance(f, dict):
                    ins = getattr(h, f["meth"])(*f["args"], **f["kw"])
                    if f["sig"]:
                        ins.then_inc(f["sem"], 1)
                else:
                    f(h)
        with self.nc.Block() as block:
            @block.tensor
            def _(h):
                run(h, progs["pe"])

            @block.scalar
            def _(h):
                run(h, progs["act"])

            @block.vector
            def _(h):
                run(h, progs["dve"])

            @block.gpsimd
            def _(h):
                run(h, progs["pool"])

            @block.sync
            def _(h):
                run(h, progs["sp"])


class _Stop(Exception):
    pass


DBG_STOP = None
DBG_OUT = False


def _cp(i):
    if DBG_STOP is not None and i >= DBG_STOP:
        raise _Stop()


def build(T=T_FULL, layers=DEPTH, phases="ABCD", sample=True):
    nc = bass.Bass("TRN2", target_bir_lowering=False)
    TT = T + NS
    NT = T // 128
    G = T // 512
    SG = G

    def din(name, shape):
        return nc.dram_tensor(name, list(shape), F32, kind="ExternalInput").ap()

    def dout(name, shape):
        return nc.dram_tensor(name, list(shape), F32, kind="ExternalOutput").ap()

    def dscr(name, shape, dt):
        return nc.dram_tensor(name, list(shape), dt, kind=("ExternalOutput" if (DBG_OUT and name in ("OA", "OG", "HT")) else "Internal")).ap()

    x_p = din("x_p", [T_FULL, D])
    x_s = din("x_s", [NS, D])
    ck = din("ck", [DEPTH, NSEQ, PAST, D])
    cv = din("cv", [DEPTH, NSEQ, PAST, D])
    sg = din("sg", [DEPTH, NSEQ, 4, 128, 256])
    sc = din("sc", [DEPTH, NSEQ, 2, DFF])
    w_in = din("w_in", [DEPTH, D, DIN])
    w_gk2 = din("w_gk2", [DEPTH, 16, 512])
    b_gk2 = din("b_gk2", [DEPTH, 512])
    lq1 = din("lq1", [DEPTH, 64]); lk1 = din("lk1", [DEPTH, 64])
    lq2 = din("lq2", [DEPTH, 64]); lk2 = din("lk2", [DEPTH, 64])
    da_norm_w = din("da_norm_w", [DEPTH, 128])
    gla_norm_w = din("gla_norm_w", [DEPTH, 256])
    w_o = din("w_o", [DEPTH, D, D])
    pre_mix_w = din("pre_mix_w", [DEPTH, D]); post_mix_w = din("post_mix_w", [DEPTH, D])
    pre_ffn_w = din("pre_ffn_w", [DEPTH, D]); post_ffn_w = din("post_ffn_w", [DEPTH, D])
    w_up = din("w_up", [DEPTH, D, 2 * DFF])
    conv_w = din("conv_w", [DEPTH, 3, DFF]); conv_b = din("conv_b", [DEPTH, DFF])
    w_down = din("w_down", [DEPTH, DFF, D])
    c_ident = din("c_ident", [128, 128])
    c_U = din("c_U", [128, 128]); c_L = din("c_L", [128, 128])
    c_Us = din("c_Us", [NS, NS]); c_Ls = din("c_Ls", [NS, NS])
    c_ind = din("c_ind", [NS, NSEQ])
    c_cos = din("c_cos", [T_FULL, 8]); c_sin = din("c_sin", [T_FULL, 8])
    c_cos_s = din("c_cos_s", [NS, 8]); c_sin_s = din("c_sin_s", [NS, 8])

    y_p = dout("y_p", [T_FULL, D]); y_s = dout("y_s", [NS, D])
    k_p = dout("k_p", [DEPTH, T_FULL, D]); v_p = dout("v_p", [DEPTH, T_FULL, D])
    gla_p = dout("gla_p", [DEPTH, 4, 128, 256]); conv_p = dout("conv_p", [DEPTH, 2, DFF])
    k_s = dout("k_s", [DEPTH, NS, D]); v_s = dout("v_s", [DEPTH, NS, D])
    gla_s = dout("gla_s", [DEPTH, NSEQ, 4, 128, 256]); conv_s = dout("conv_s", [DEPTH, NSEQ, 2, DFF])

    XT = dscr("XT", [8, 128, TT], F32)
    XN = dscr("XN", [8, 128, TT], BF16)
    QT = dscr("QT", [8, 128, TT], BF16)
    KT = dscr("KT", [8, 128, TT], BF16)
    V16 = dscr("V16", [TT, D], BF16)
    OA = dscr("OA", [8, 128, TT], BF16)
    OG = dscr("OG", [8, 128, TT], BF16)
    HT = dscr("HT", [8, 128, TT], F32)
    XMID = dscr("XMID", [TT, D], F32)

    def bufs(n):
        return [Buf() for _ in range(n)]

    B_XT = bufs(G + 1); B_XN = bufs(G + 1); B_QT = bufs(G + 1); B_KT = bufs(G + 1)
    B_V16 = bufs(G + 1); B_OG = bufs(G + 1); B_HT = bufs(G + 1)
    B_OA = [bufs(G + 1) for _ in range(8)]
    B_XMID = bufs(2 * G + 1)
    B_OUT = Buf()

    with ExitStack() as gst:
        fw = FW(nc, gst)

        def ACT(func, out, in_, R, W, **kw):
            fw.op("act", "activation", out=out, in_=in_, func=func, R=R, W=W, **kw)

        def DVE(meth, *a, R, W, **kw):
            fw.op("dve", meth, *a, R=R, W=W, **kw)

        def MM(out, lhsT, rhs, start, stop, R, W, signal=True):
            fw.op("pe", "matmul", out, lhsT, rhs, start=start, stop=stop, R=R, W=W, signal=signal)

        def TRP(out, in_, ident, R, W, signal=True):
            fw.op("pe", "transpose", out, in_, ident, R=R, W=W, signal=signal)

        SLOW = dict(allow_slow_non_contiguous=True)

        CB = Buf()
        id32 = fw.sb("id32", [128, 128], F32); id16 = fw.sb("id16", [128, 128], BF16)
        ones32 = fw.sb("ones32", [128, 128], F32); ones16 = fw.sb("ones16", [128, 128], BF16)
        U32 = fw.sb("U32", [128, 128], F32); L32 = fw.sb("L32", [128, 128], F32)
        Us32 = fw.sb("Us32", [NS, NS], F32); Ls32 = fw.sb("Ls32", [NS, NS], F32)
        ind32 = fw.sb("ind32", [NS, NSEQ], F32)
        cosT = fw.sb("cosT", [128, T_FULL // 128, 8], F32); sinT = fw.sb("sinT", [128, T_FULL // 128, 8], F32)
        cosS = fw.sb("cosS", [NS, 8], F32); sinS = fw.sb("sinS", [NS, 8], F32)
        fw.dma("sp", id32[:], c_ident, W=[CB]); fw.dma("pool", id16[:], c_ident, W=[CB])
        fw.dma("sp", U32[:], c_U, W=[CB]); fw.dma("sp", L32[:], c_L, W=[CB])
        fw.dma("sp", Us32[:], c_Us, W=[CB]); fw.dma("sp", Ls32[:], c_Ls, W=[CB])
        fw.dma("sp", ind32[:], c_ind, W=[CB])
        fw.dma("sp", cosT[:], c_cos.rearrange("(j p) e -> p j e", p=128), W=[CB])
        fw.dma("sp", sinT[:], c_sin.rearrange("(j p) e -> p j e", p=128), W=[CB])
        fw.dma("sp", cosS[:], c_cos_s, W=[CB]); fw.dma("sp", sinS[:], c_sin_s, W=[CB])
        fw.op("dve", "memset", ones32[:], 1.0, W=[CB]); fw.op("dve", "memset", ones16[:], 1.0, W=[CB])

        PRM = []
        for l in range(DEPTH):
            p = {}
            for nm, src in (("wpre", pre_mix_w), ("wpost", post_mix_w), ("wpreffn", pre_ffn_w), ("wpostffn", post_ffn_w)):
                p[nm] = fw.sb(nm, [128, 8], F32)
                fw.dma("sp", p[nm][:], src[l].rearrange("(k p) -> p k", p=128), W=[CB], **SLOW)
            p["wda"] = fw.sb("wda", [128, 1], F32)
            fw.dma("sp", p["wda"][:], da_norm_w[l].rearrange("(p o) -> p o", o=1), W=[CB], **SLOW)
            lam_init = 0.8 - 0.6 * math.exp(-0.3 * l)
            fw.op("dve", "tensor_scalar", p["wda"][:], p["wda"][:], 1.0 - lam_init, None, op0=ALU.mult, R=[CB], W=[CB])
            p["wgla"] = fw.sb("wgla", [128, 2], F32)
            fw.dma("sp", p["wgla"][:], gla_norm_w[l].rearrange("(e p) -> p e", p=128), W=[CB], **SLOW)
            p["cw"] = fw.sb("cw", [128, 3, NJ], F32)
            for j3 in range(3):
                fw.dma("sp", p["cw"][:, j3, :], conv_w[l, j3].rearrange("(c p) -> p c", p=128), W=[CB], **SLOW)
            p["cb"] = fw.sb("cb", [128, NJ], F32)
            fw.dma("sp", p["cb"][:], conv_b[l].rearrange("(c p) -> p c", p=128), W=[CB], **SLOW)
            p["bg2"] = fw.sb("bg2", [128, 512], F32)
            fw.dma("sp", p["bg2"][:], b_gk2[l].partition_broadcast(128), W=[CB])
            p["Wg2"] = fw.sb("Wg2", [16, 512], BF16)
            fw.dma("pool", p["Wg2"][:], w_gk2[l], W=[CB])
            lt = [fw.sb("lt%d" % i, [128, 64], F32) for i in range(4)]
            for t_, src in zip(lt, (lq1, lk1, lq2, lk2)):
                fw.dma("sp", t_[:], src[l].partition_broadcast(128), W=[CB])
            s1 = fw.sb("s1", [128, 2], F32)
            fw.op("dve", "memset", s1[:], 0.0, W=[CB])
            fw.op("dve", "tensor_tensor", lt[0][:], lt[0][:], lt[1][:], op=ALU.mult, R=[CB], W=[CB])
            fw.op("dve", "tensor_tensor", lt[2][:], lt[2][:], lt[3][:], op=ALU.mult, R=[CB], W=[CB])
            fw.op("dve", "reduce_sum", s1[:, 0:1], lt[0][:], axis=mybir.AxisListType.X, R=[CB], W=[CB])
            fw.op("dve", "reduce_sum", s1[:, 1:2], lt[2][:], axis=mybir.AxisListType.X, R=[CB], W=[CB])
            ACT(AF.Exp, s1[:], s1[:], R=[CB], W=[CB])
            p["neglam"] = fw.sb("neglam", [128, 1], F32)
            fw.op("dve", "tensor_tensor", p["neglam"][:], s1[:, 1:2], s1[:, 0:1], op=ALU.subtract, R=[CB], W=[CB])
            fw.op("dve", "tensor_scalar", p["neglam"][:], p["neglam"][:], -lam_init, None, op0=ALU.add, R=[CB], W=[CB])
            PRM.append(p)
        fw.flush()

        def rstd_from_ssq(out, ssq_ap, n_feat, lnv, R, W, WL):
            ACT(AF.Ln, lnv, ssq_ap, R=R, W=[WL], scale=1.0 / n_feat, bias=epsc[:lnv.shape[0], 0:1])
            ACT(AF.Exp, out, lnv, R=[WL], W=W, scale=-0.5)

        epsc = fw.sb("epsc", [128, 1], F32)
        fw.op("dve", "memset", epsc[:], EPS, W=[CB])

        def phase_A(l):
            P = PRM[l]
            with ExitStack() as ph:
                fw.stack = ph
                WB = Buf()
                Wa = fw.sb("Wa", [128, 8, 5120], BF16)
                Wlr = fw.sb("Wlr", [128, 8, 16], BF16)
                src = w_in[l].rearrange("(k p) n -> p k n", p=128)
                for k in range(8):
                    for hh in range(2):
                        fw.dma("pool", Wa[:, k, hh * 2560:(hh + 1) * 2560], src[:, k, hh * 2560:(hh + 1) * 2560], W=[WB])
                fw.dma("pool", Wlr[:], src[:, :, 6144:6160], W=[WB])
                xrot = fw.tiles("x", 2, [128, D], F32)
                junk = fw.sb("junk", [128, D], BF16); JB = Buf()
                ssq = fw.sb("ssq", [128, 1], F32); lnv = fw.sb("lnv", [128, 1], F32); rstd = fw.sb("rstd", [128, 1], F32)
                SB_ = Buf(); LB = Buf(); RB = Buf()
                xn = fw.sb("xn", [128, D], BF16); XNB = Buf()
                xTrot = fw.tiles("xTs", 2, [128, 8, 128], F32)
                xnTrot = fw.tiles("xnT", 2, [128, 8, 128], BF16)
                qrot16 = fw.sb("qrot16", [128, D], BF16); QRB = Buf()
                krot32 = fw.sb("krot32", [128, D], F32); KRB = Buf()
                krot16 = fw.sb("krot16", [128, D], BF16); KR16B = Buf()
                v32 = fw.sb("v32", [128, D], F32); V32B = Buf()
                v16 = fw.sb("v16", [128, D], BF16); V16B = Buf()
                QTs = fw.sb("QTs", [128, 8, 512], BF16); QTSB = Buf()
                KTs = fw.sb("KTs", [128, 8, 512], BF16); KTSB = Buf()
                OGs = fw.sb("OGs", [128, 8, 512], BF16); OGSB = Buf()
                rt = [fw.sb("rt%d" % i, [128, 16, 8], F32) for i in range(4)]; RTB = [Buf() for _ in range(4)]
                lrT = fw.sb("lrT", [16, 128], BF16); LRB = Buf()
                ge = fw.sb("ge", [128, 512], F32); GEB = Buf()
                g32 = fw.sb("g32", [128, 512], F32); G32B = Buf()
                E1 = fw.sb("E1", [128, 512], F32); E2 = fw.sb("E2", [128, 512], F32); E3 = fw.sb("E3", [128, 512], F32)
                E1B = Buf(); E2B = Buf(); E3B = Buf()
                ebl = fw.sb("ebl", [128, 16], F32); EBLB = Buf()
                qt16 = fw.sb("qt16", [128, 512], BF16); kt16 = fw.sb("kt16", [128, 512], BF16)
                kh16 = fw.sb("kh16", [128, 512], BF16); khm = fw.sb("khm", [128, 512], BF16)
                QT16B = Buf(); KT16B = Buf(); KH16B = Buf(); KHMB = Buf()
                vg16 = fw.sb("vg16", [128, D], BF16); VG16B = Buf()
                qkT = fw.sb("qkT", [128, 8, 128], BF16); QKTB = Buf()
                AT16 = fw.sb("AT16", [128, 4, 128], BF16); ATB = Buf()
                ogi = fw.sb("ogi", [128, 8, 128], F32); OGIB = Buf()
                S32 = fw.sb("S32", [128, 4, 256], F32); S32B = Buf()
                S16 = fw.sb("S16", [128, 4, 256], BF16); S16B = Buf()
                P2 = Rot([(fw.ps("P2_%d" % i, [128, 1024], F32), Buf()) for i in range(3)])
                P1 = Rot([(fw.ps("P1_%d" % i, [128, 512], F32), Buf()) for i in range(1)])
                P1b = Rot([(fw.ps("P1b_%d" % i, [128, 1024], BF16), Buf()) for i in range(1)])

                fw.op("dve", "memset", S32[:], 0.0, W=[S32B])
                fw.op("dve", "memset", S16[:], 0.0, W=[S16B])

                def tile(xsrc, XSB, n, nseq, t0, g, ti, cos_ap, sin_ap, kdst, vdst, last_in_group, ncols):
                    Lq = n // nseq
                    Uc = U32 if nseq == 1 else Us32
                    Lc = L32 if nseq == 1 else Ls32
                    xt, XB = xrot.next()
                    fw.dma("sp", xt[:n], xsrc, R=[XSB] if XSB is not None else [], W=[XB])
                    fw.op("dve", "memset", ssq[:n], 0.0, W=[SB_])
                    ACT(AF.Square, junk[:n], xt[:n], R=[XB], W=[JB, SB_], accum_out=ssq[:n, 0:1])
                    rstd_from_ssq(rstd[:n], ssq[:n], D, lnv[:n], R=[SB_], W=[RB], WL=LB)
                    DVE("tensor_scalar", xn[:n], xt[:n], rstd[:n, 0:1], None, op0=ALU.mult, R=[XB, RB], W=[XNB])
                    _cp(1)
                    pX, PXB = P2.next()
                    pXv = pX[:].rearrange("p (k t) -> p k t", t=128)
                    for k in range(8):
                        TRP(pXv[:, k, :n], xt[:n, k * 128:(k + 1) * 128], id32[:n, :n], R=[XB, CB], W=[PXB], signal=(k == 7))
                    xTs, XTSB = xTrot.next()
                    ACT(AF.Copy, xTs[:, :, :n], pXv[:, :, :n], R=[PXB], W=[XTSB])
                    fw.dma("sp", XT.rearrange("k p t -> p k t")[:, :, t0:t0 + n], xTs[:, :, :n], R=[XTSB], W=[B_XT[g]])
                    _cp(2)
                    pN, PNB = P1b.next()
                    pNv = pN[:].rearrange("p (k t) -> p k t", t=128)
                    for k in range(8):
                        TRP(pNv[:, k, :n], xn[:n, k * 128:(k + 1) * 128], id16[:n, :n], R=[XNB, CB], W=[PNB], signal=(k == 7))
                    xnT, XNTB = xnTrot.next()
                    DVE("tensor_tensor", xnT[:, :, :n], pNv[:, 0:8, :n], P["wpre"][:, :].unsqueeze(2).to_broadcast([128, 8, n]),
                        op=ALU.mult, R=[PNB, CB], W=[XNTB])
                    fw.dma("sp", XN.rearrange("k p t -> p k t")[:, :, t0:t0 + n], xnT[:, :, :n], R=[XNTB], W=[B_XN[g]])

                    _cp(3)
                    def proj(c0):
                        pt, PB = P2.next()
                        for half in range(2):
                            for k in range(8):
                                MM(pt[:n, half * 512:(half + 1) * 512], xnT[:, k, :n], Wa[:, k, c0 + half * 512:c0 + (half + 1) * 512],
                                   start=(k == 0), stop=(k == 7), R=[XNTB, WB], W=[PB], signal=(k == 7 and half == 1))
                        return pt, PB

                    def rope(pt, PB, out, OB):
                        v = pt[:n].rearrange("p (g d) -> p g d", d=64)
                        o = out[:n].rearrange("p (g d) -> p g d", d=64)
                        cb_ = cos_ap.unsqueeze(1).to_broadcast([n, 16, 8])
                        sb_ = sin_ap.unsqueeze(1).to_broadcast([n, 16, 8])
                        DVE("tensor_tensor", rt[0][:n], v[:, :, 0:8], cb_, op=ALU.mult, R=[PB, CB], W=[RTB[0]])
                        DVE("tensor_tensor", rt[1][:n], v[:, :, 8:16], sb_, op=ALU.mult, R=[PB, CB], W=[RTB[1]])
                        DVE("tensor_tensor", rt[2][:n], v[:, :, 8:16], cb_, op=ALU.mult, R=[PB, CB], W=[RTB[2]])
                        DVE("tensor_tensor", rt[3][:n], v[:, :, 0:8], sb_, op=ALU.mult, R=[PB, CB], W=[RTB[3]])
                        DVE("tensor_copy", o[:, :, 16:64], v[:, :, 16:64], R=[PB], W=[OB])
                        DVE("tensor_tensor", o[:, :, 0:8], rt[0][:n], rt[1][:n], op=ALU.subtract, R=[RTB[0], RTB[1]], W=[OB])
                        DVE("tensor_tensor", o[:, :, 8:16], rt[2][:n], rt[3][:n], op=ALU.add, R=[RTB[2], RTB[3]], W=[OB])

                    pq, PQB = proj(0)
                    _cp(3.2)
                    rope(pq, PQB, qrot16, QRB)
                    _cp(3.5)
                    pT, PTB = P1b.next()
                    pTv = pT[:].rearrange("p (k t) -> p k t", t=128)
                    for h in range(8):
                        TRP(pTv[:, h, :n], qrot16[:n, h * 128:(h + 1) * 128], id16[:n, :n], R=[QRB, CB], W=[PTB], signal=(h == 7))
                    _cp(3.8)
                    DVE("tensor_copy", QTs[:, :, ti * 128:ti * 128 + n], pTv[:, 0:8, :n], R=[PTB], W=[QTSB])
                    _cp(4)
                    pk, PKB = proj(1024)
                    rope(pk, PKB, krot32, KRB)
                    fw.dma("sp", kdst, krot32[:n], R=[KRB], W=[B_OUT])
                    fw.op("act", "copy", krot16[:n], krot32[:n], R=[KRB], W=[KR16B])
                    pT, PTB = P1b.next()
                    pTv = pT[:].rearrange("p (k t) -> p k t", t=128)
                    for h in range(8):
                        TRP(pTv[:, h, :n], krot16[:n, h * 128:(h + 1) * 128], id16[:n, :n], R=[KR16B, CB], W=[PTB], signal=(h == 7))
                    DVE("tensor_copy", KTs[:, :, ti * 128:ti * 128 + n], pTv[:, 0:8, :n], R=[PTB], W=[KTSB])
                    _cp(5)
                    pv, PVB = proj(2048)
                    _cp(5.2)
                    fw.op("act", "copy", v32[:n], pv[:n], R=[PVB], W=[V32B])
                    _cp(5.4)
                    DVE("tensor_copy", v16[:n], v32[:n], R=[V32B], W=[V16B])
                    _cp(5.6)
                    fw.dma("sp", vdst, v32[:n], R=[V32B], W=[B_OUT])
                    _cp(5.8)
                    fw.dma("sp", V16[t0:t0 + n, :], v16[:n], R=[V16B], W=[B_V16[g]])
                    _cp(6)
                    if last_in_group:
                        fw.dma("sp", QT.rearrange("h p t -> p h t")[:, :, g * 512:g * 512 + ncols], QTs[:, :, :ncols], R=[QTSB], W=[B_QT[g]])
                        fw.dma("sp", KT.rearrange("h p t -> p h t")[:, :, g * 512:g * 512 + ncols], KTs[:, :, :ncols], R=[KTSB], W=[B_KT[g]])

                    _cp(7)
                    pl, PLB = P1.next()
                    for k in range(8):
                        MM(pl[0:16, :n], Wlr[:, k, :], xnT[:, k, :n], start=(k == 0), stop=(k == 7), R=[XNTB, WB], W=[PLB], signal=(k == 7))
                    fw.op("act", "copy", lrT[:, :n], pl[0:16, :n], R=[PLB], W=[LRB])
                    pqk, PQKB = proj(3072)
                    pvg, PVGB = proj(4096)
                    fw.op("act", "copy", vg16[:n], pvg[:n], R=[PVGB], W=[VG16B])
                    pg, PGB = P1.next()
                    MM(pg[:n, :], lrT[:, :n], P["Wg2"][:, :], start=True, stop=True, R=[LRB, CB], W=[PGB])
                    DVE("tensor_tensor", ge[:n], pg[:n, :], P["bg2"][:n], op=ALU.add, R=[PGB, CB], W=[GEB])
                    ACT(AF.Exp, ge[:n], ge[:n], R=[GEB], W=[GEB], scale=-1.0)
                    ACT(AF.Ln, ge[:n], ge[:n], R=[GEB], W=[GEB], bias=1.0)
                    DVE("tensor_scalar", g32[:n], ge[:n], -1.0 / 16.0, None, op0=ALU.mult, R=[GEB], W=[G32B])
                    pb, PBB = P2.next()
                    MM(pb[:n, 0:512], Uc[:n, :n], g32[:n, :], start=True, stop=True, R=[CB, G32B], W=[PBB], signal=False)
                    MM(pb[:n, 512:1024], Lc[:n, :n], g32[:n, :], start=True, stop=True, R=[CB, G32B], W=[PBB])
                    ACT(AF.Exp, E1[:n], pb[:n, 0:512], R=[PBB], W=[E1B])
                    ACT(AF.Exp, E2[:n], pb[:n, 0:512], R=[PBB], W=[E2B], scale=-1.0)
                    ACT(AF.Exp, E3[:n], pb[:n, 512:1024], R=[PBB], W=[E3B])
                    pbl, PBLB = P1.next()
                    for h in range(4):
                        MM(pbl[:, h * nseq:(h + 1) * nseq], g32[:n, h * 128:(h + 1) * 128],
                           (ones32[:n, 0:1] if nseq == 1 else ind32[:n, :nseq]), start=True, stop=True, R=[G32B, CB], W=[PBLB], signal=(h == 3))
                    ACT(AF.Exp, ebl[:, :4 * nseq], pbl[:, :4 * nseq], R=[PBLB], W=[EBLB])
                    DVE("scalar_tensor_tensor", qt16[:n], pqk[:n, 0:512], 128.0 ** -0.5, E1[:n], op0=ALU.mult, op1=ALU.mult, R=[PQKB, E1B], W=[QT16B])
                    DVE("tensor_tensor", kt16[:n], pqk[:n, 512:1024], E2[:n], op=ALU.mult, R=[PQKB, E2B], W=[KT16B])
                    DVE("tensor_tensor", kh16[:n], pqk[:n, 512:1024], E3[:n], op=ALU.mult, R=[PQKB, E3B], W=[KH16B])
                    _cp(8)
                    pT2, PT2B = P1b.next()
                    pT2v = pT2[:].rearrange("p (k t) -> p k t", t=128)
                    for h in range(4):
                        TRP(pT2v[:, h, :n], qt16[:n, h * 128:(h + 1) * 128], id16[:n, :n], R=[QT16B, CB], W=[PT2B], signal=False)
                        TRP(pT2v[:, 4 + h, :n], kt16[:n, h * 128:(h + 1) * 128], id16[:n, :n], R=[KT16B, CB], W=[PT2B], signal=(h == 3))
                    DVE("tensor_copy", qkT[:, :, :n], pT2v[:, 0:8, :n], R=[PT2B], W=[QKTB])
                    pA, PAB = P1.next()
                    pAv = pA[:].rearrange("p (h t) -> p h t", t=128)
                    for h in range(4):
                        MM(pAv[:n, h, :n], qkT[:, 4 + h, :n], qkT[:, h, :n], start=True, stop=True, R=[QKTB], W=[PAB], signal=(h == 3))
                    DVE("tensor_tensor", AT16[:n, :, :n], pAv[:n, :, :n], Uc[:n, :n].unsqueeze(1).to_broadcast([n, 4, n]),
                        op=ALU.mult, R=[PAB, CB], W=[ATB])
                    _cp(9)
                    pO, POB = P2.next()
                    pOv = pO[:].rearrange("p (c t) -> p c t", t=128)
                    for h in range(4):
                        for e in range(2):
                            MM(pOv[:, h * 2 + e, :n], vg16[:n, h * 256 + e * 128:h * 256 + (e + 1) * 128], AT16[:n, h, :n],
                               start=True, stop=True, R=[VG16B, ATB], W=[POB], signal=(h == 3 and e == 1))
                    pI, PIB = P2.next()
                    pIv = pI[:].rearrange("p (c t) -> p c t", t=128)
                    for s in range(nseq):
                        if nseq > 1:
                            fw.dma("sp", S32[:], sg[l, s].rearrange("h p v -> p h v"), W=[S32B])
                            fw.op("act", "copy", S16[:], S32[:], R=[S32B], W=[S16B])
                        for h in range(4):
                            for e in range(2):
                                MM(pIv[:, h * 2 + e, s * Lq:(s + 1) * Lq], S16[:, h, e * 128:(e + 1) * 128], qkT[:, h, s * Lq:(s + 1) * Lq],
                                   start=True, stop=True, R=[S16B, QKTB], W=[PIB], signal=(h == 3 and e == 1))
                        if nseq > 1:
                            DVE("tensor_scalar", khm[:n], kh16[:n], ind32[:n, s:s + 1], None, op0=ALU.mult, R=[KH16B, CB], W=[KHMB])
                            khs, KHSB = khm, KHMB
                        else:
                            khs, KHSB = kh16, KH16B
                        for hp in range(2):
                            pS, PSB = P1.next()
                            pSv = pS[:].rearrange("p (h v) -> p h v", v=256)
                            for hh in range(2):
                                h = hp * 2 + hh
                                MM(pSv[:, hh, :], khs[:n, h * 128:(h + 1) * 128], vg16[:n, h * 256:(h + 1) * 256], start=True, stop=True,
                                   R=[KHSB, VG16B], W=[PSB], signal=(hh == 1))
                            for hh in range(2):
                                h = hp * 2 + hh
                                DVE("scalar_tensor_tensor", S32[:, h, :], S32[:, h, :], ebl[:, h * nseq + s:h * nseq + s + 1], pSv[:, hh, :],
                                    op0=ALU.mult, op1=ALU.add, R=[S32B, EBLB, PSB], W=[S32B])
                        fw.op("act", "copy", S16[:], S32[:], R=[S32B], W=[S16B])
                        if nseq > 1:
                            fw.dma("sp", gla_s[l, s].rearrange("h p v -> p h v"), S32[:], R=[S32B], W=[B_OUT])
                    _cp(11)
                    fw.op("act", "copy", ogi[:, :, :n], pIv[:, :, :n], R=[PIB], W=[OGIB])
                    DVE("tensor_tensor", OGs[:, :, ti * 128:ti * 128 + n], pOv[:, :, :n], ogi[:, :, :n], op=ALU.add, R=[POB, OGIB], W=[OGSB])
                    if last_in_group:
                        fw.dma("sp", OG.rearrange("c p t -> p c t")[:, :, g * 512:g * 512 + ncols], OGs[:, :, :ncols], R=[OGSB], W=[B_OG[g]])

                for i in range(NT):
                    g, ti = divmod(i, 4)
                    if l == 0:
                        xsrc, XSB = x_p[i * 128:(i + 1) * 128, :], None
                    else:
                        xsrc, XSB = XMID[i * 128:(i + 1) * 128, :], B_XMID[i // 2]
                    try:
                        tile(xsrc, XSB, 128, 1, i * 128, g, ti, cosT[:, i, :], sinT[:, i, :],
                             k_p[l, i * 128:(i + 1) * 128, :], v_p[l, i * 128:(i + 1) * 128, :], ti == 3, 512)
                    except _Stop:
                        pass
                fw.dma("sp", gla_p[l].rearrange("h p v -> p h v"), S32[:], R=[S32B], W=[B_OUT])
                if sample:
                    if l == 0:
                        xsrc, XSB = x_s[:, :], None
                    else:
                        xsrc, XSB = XMID[T:T + NS, :], B_XMID[2 * G]
                    tile(xsrc, XSB, NS, NSEQ, T, SG, 0, cosS[:, :], sinS[:, :], k_s[l], v_s[l], True, NS)
                fw.flush()
            fw.stack = fw.gstack

        def phase_B(l):
            P = PRM[l]
            with ExitStack() as ph:
                fw.stack = ph
                KTh = fw.tiles("KTh", 2, [128, T], BF16)
                QTh = fw.tiles("QTh", 2, [128, T], BF16)
                Vh = fw.tiles("Vh", 2, [128, NT, 128], BF16)
                PTr = fw.tiles("PT", 4, [128, 512], BF16)
                r1 = fw.sb("r1", [128, 512], F32); r2 = fw.sb("r2", [128, 512], F32)
                o1 = fw.sb("o1", [128, 512], F32); o2 = fw.sb("o2", [128, 512], F32)
                oa = fw.sb("oa", [128, 512], F32); sq = fw.sb("sq", [128, 512], F32)
                lnv = fw.sb("lnvB", [128, 512], F32); rstd = fw.sb("rstdB", [128, 512], F32)
                R1B = Buf(); R2B = Buf(); O1B = Buf(); O2B = Buf(); OAB = Buf(); SQB = Buf(); LNB = Buf(); RSB = Buf()
                oanr = fw.tiles("oan", 2, [128, 512], BF16)
                STr = Rot([(fw.ps("ST%d" % i, [128, 512], F32), Buf()) for i in range(3)])
                PTb = Rot([(fw.ps("PTb%d" % i, [128, 1024], BF16), Buf()) for i in range(1)])
                Ops = [(fw.ps("O%d" % i, [128, 512], F32), Buf()) for i in range(2)]
                Lps = [(fw.ps("L%d" % i, [128, 512], F32), Buf()) for i in range(2)]

                def epilogue(h, g, N, c0):
                    DVE("reciprocal", r1[:, :N], Lps[0][0][:, :N], R=[Lps[0][1]], W=[R1B])
                    DVE("reciprocal", r2[:, :N], Lps[1][0][:, :N], R=[Lps[1][1]], W=[R2B])
                    DVE("tensor_tensor", o1[:, :N], Ops[0][0][:, :N], r1[:, :N], op=ALU.mult, R=[Ops[0][1], R1B], W=[O1B])
                    DVE("tensor_tensor", o2[:, :N], Ops[1][0][:, :N], r2[:, :N], op=ALU.mult, R=[Ops[1][1], R2B], W=[O2B])
                    DVE("scalar_tensor_tensor", oa[:, :N], o2[:, :N], P["neglam"][:, 0:1], o1[:, :N], op0=ALU.mult, op1=ALU.add,
                        R=[O1B, O2B, CB], W=[OAB])
                    ACT(AF.Square, sq[:, :N], oa[:, :N], R=[OAB], W=[SQB])
                    st, STB = STr.next()
                    MM(st[:, :N], ones32[:, :], sq[:, :N], start=True, stop=True, R=[CB, SQB], W=[STB])
                    rstd_from_ssq(rstd[:, :N], st[:, :N], 128, lnv[:, :N], R=[STB], W=[RSB], WL=LNB)
                    oan, OANB = oanr.next()
                    DVE("scalar_tensor_tensor", oan[:, :N], oa[:, :N], P["wda"][:, 0:1], rstd[:, :N], op0=ALU.mult, op1=ALU.mult,
                        R=[OAB, CB, RSB], W=[OANB])
                    fw.dma("sp", OA[h, :, c0:c0 + N], oan[:, :N], R=[OANB], W=[B_OA[h][g]])

                for h in range(8):
                    kth, KB = KTh.next(); qth, QB = QTh.next(); vh, VB = Vh.next()
                    fw.dma("sp", kth[:, :], KT[h, :, 0:T], R=B_KT[:G], W=[KB])
                    fw.dma("sp", qth[:, :], QT[h, :, 0:T], R=B_QT[:G], W=[QB])
                    fw.dma("sp", vh[:, :, :], V16[0:T, h * 128:(h + 1) * 128].rearrange("(j p) d -> p j d", p=128), R=B_V16[:G], W=[VB])
                    for g in range(G):
                        q0 = g * 512
                        items = [(j, c) for j in range(4 * (g + 1)) for c in range(2)]
                        nit = len(items)
                        pts = [None] * nit

                        def qk(i):
                            j, c = items[i]
                            off = max(j - 4 * g, 0) * 128
                            st, STB = STr.next()
                            MM(st[:, off:512], kth[c * 64:(c + 1) * 64, j * 128:(j + 1) * 128], qth[c * 64:(c + 1) * 64, q0 + off:q0 + 512],
                               start=True, stop=True, R=[KB, QB], W=[STB])
                            pt, PTB = PTr.next()
                            ACT(AF.Exp, pt[:, off:512], st[:, off:512], R=[STB], W=[PTB], scale=0.125)
                            if j >= 4 * g:
                                fw.op("pool", "memset", pt[64:128, off:off + 64], 0.0, W=[PTB])
                            pts[i] = (pt, PTB, off)

                        def pv(i):
                            j, c = items[i]
                            pt, PTB, off = pts[i]
                            first = (j == 0); last = (j == 4 * (g + 1) - 1)
                            MM(Ops[c][0][:, off:512], vh[:, j, :], pt[:, off:512], start=first, stop=last, R=[VB, PTB], W=[Ops[c][1]], signal=last)
                            MM(Lps[c][0][:, off:512], ones16[:, :], pt[:, off:512], start=first, stop=last, R=[CB, PTB], W=[Lps[c][1]], signal=last)

                        qk(0); qk(1)
                        for i in range(nit):
                            if i + 2 < nit:
                                qk(i + 2)
                            pv(i)
                        epilogue(h, g, 512, q0)

                if sample:
                    vs16 = fw.sb("vs16", [NS, D], BF16); VSB = Buf()
                    fw.dma("sp", vs16[:, :], V16[T:T + NS, :], R=[B_V16[SG]], W=[VSB])
                    kcr = fw.tiles("kc", 2, [128, 16, 128], BF16)
                    vcr = fw.tiles("vc", 2, [128, 16, 128], BF16)
                    ktc = fw.sb("ktc", [128, PAST], BF16); KTCB = Buf()
                    qs = fw.sb("qs", [128, NS], BF16); ks_ = fw.sb("ks", [128, NS], BF16); QSB = Buf(); KSB = Buf()
                    ptsr = fw.tiles("pts", 2, [128, 17 * 16], BF16)
                    for h in range(8):
                        fw.dma("sp", qs[:, :], QT[h, :, T:T + NS], R=[B_QT[SG]], W=[QSB])
                        fw.dma("sp", ks_[:, :], KT[h, :, T:T + NS], R=[B_KT[SG]], W=[KSB])
                        for s in range(NSEQ):
                            kc, KCB = kcr.next(); vc, VCB = vcr.next()
                            fw.dma("pool", kc[:, :, :], ck[l, s, :, h * 128:(h + 1) * 128].rearrange("(j p) d -> p j d", p=128), W=[KCB])
                            fw.dma("pool", vc[:, :, :], cv[l, s, :, h * 128:(h + 1) * 128].rearrange("(j p) d -> p j d", p=128), W=[VCB])
                            for half in range(2):
                                pT, PTB_ = PTb.next()
                                pTv = pT[:].rearrange("p (k t) -> p k t", t=128)
                                for jj in range(8):
                                    TRP(pTv[:, jj, :], kc[:, half * 8 + jj, :], id16[:, :], R=[KCB, CB], W=[PTB_], signal=(jj == 7))
                                DVE("tensor_copy", ktc[:, half * 1024:(half + 1) * 1024], pT[:], R=[PTB_], W=[KTCB])
                            for c in range(2):
                                st, STB = STr.next()
                                for j in range(16):
                                    MM(st[:, j * 16:(j + 1) * 16], ktc[c * 64:(c + 1) * 64, j * 128:(j + 1) * 128], qs[c * 64:(c + 1) * 64, s * 16:(s + 1) * 16],
                                       start=True, stop=True, R=[KTCB, QSB], W=[STB], signal=False)
                                MM(st[0:NS, 256:272], ks_[c * 64:(c + 1) * 64, :], qs[c * 64:(c + 1) * 64, s * 16:(s + 1) * 16],
                                   start=True, stop=True, R=[KSB, QSB], W=[STB])
                                pt, PTB = ptsr.next()
                                ACT(AF.Exp, pt[:, 0:256], st[:, 0:256], R=[STB], W=[PTB], scale=0.125)
                                ACT(AF.Exp, pt[0:NS, 256:272], st[0:NS, 256:272], R=[STB], W=[PTB], scale=0.125)
                                DVE("tensor_scalar", pt[0:NS, 256:272], pt[0:NS, 256:272], ind32[:, s:s + 1], None, op0=ALU.mult, R=[PTB, CB], W=[PTB])
                                oc = Ops[c][0][:, s * 16:(s + 1) * 16]; lc = Lps[c][0][:, s * 16:(s + 1) * 16]
                                for j in range(16):
                                    MM(oc, vc[:, j, :], pt[:, j * 16:(j + 1) * 16], start=(j == 0), stop=False, R=[VCB, PTB], W=[Ops[c][1]], signal=False)
                                MM(oc, vs16[:, h * 128:(h + 1) * 128], pt[0:NS, 256:272], start=False, stop=True, R=[VSB, PTB], W=[Ops[c][1]])
                                for j in range(16):
                                    MM(lc, ones16[:, :], pt[:, j * 16:(j + 1) * 16], start=(j == 0), stop=False, R=[CB, PTB], W=[Lps[c][1]], signal=False)
                                MM(lc, ones16[0:NS, :], pt[0:NS, 256:272], start=False, stop=True, R=[CB, PTB], W=[Lps[c][1]])
                        epilogue(h, SG, NS, T)
                fw.flush()
            fw.stack = fw.gstack

        def phase_D1(l):
            P = PRM[l]
            with ExitStack() as ph:
                fw.stack = ph
                WB = Buf()
                Wg = fw.sb("Wg", [128, 8, 3072], BF16)
                Wo = fw.sb("Wo", [128, 8, D], BF16)
                src = w_in[l].rearrange("(k p) n -> p k n", p=128)
                for k in range(8):
                    fw.dma("pool", Wg[:, k, 0:1024], src[:, k, 5120:6144], W=[WB])
                    fw.dma("pool", Wg[:, k, 1024:3072], src[:, k, 6160:8208], W=[WB])
                fw.dma("pool", Wo[:], w_o[l].rearrange("(k p) n -> p k n", p=128), W=[WB])
                xnr = fw.tiles("xnT", 1, [128, 8, 512], BF16)
                oar = fw.tiles("oaT", 1, [128, 8, 512], BF16)
                ogr = fw.tiles("ogT", 1, [128, 8, 512], BF16)
                xtr = fw.tiles("xT", 1, [128, 8, 512], F32)
                gate = fw.sb("gate", [128, 24, 512], BF16); GB = [Buf() for _ in range(24)]
                sqr = fw.tiles("sq", 3, [128, 512], BF16)
                lnv = fw.sb("lnvD", [128, 512], F32); LNB = Buf()
                rsr = fw.tiles("rs", 2, [128, 512], F32)
                t1r = fw.tiles("t1", 2, [128, 512], F32)
                t2r = fw.tiles("t2", 2, [128, 512], F32)
                merged = fw.sb("merged", [128, 8, 512], BF16); MB = [Buf() for _ in range(8)]
                AB = [Buf() for _ in range(8)]
                hT = fw.sb("hT", [128, 8, 512], F32)
                PS = Rot([(fw.ps("PD%d" % i, [128, 512], F32), Buf()) for i in range(6)])
                PSQ = Rot([(fw.ps("PQ%d" % i, [128, 512], F32), Buf()) for i in range(2)])

                def group(g, N, c0):
                    xn, XNB = xnr.next(); oaT, OAB_ = oar.next(); ogT, OGB_ = ogr.next(); xT, XTB = xtr.next()
                    fw.dma("sp", xn[:, :, :N], XN.rearrange("k p t -> p k t")[:, :, c0:c0 + N], R=[B_XN[g]], W=[XNB])
                    fw.dma("sp", oaT[:, :, :N], OA.rearrange("k p t -> p k t")[:, :, c0:c0 + N], R=[B_OA[h][g] for h in range(8)], W=[OAB_])
                    fw.dma("sp", ogT[:, :, :N], OG.rearrange("k p t -> p k t")[:, :, c0:c0 + N], R=[B_OG[g]], W=[OGB_])
                    fw.dma("sp", xT[:, :, :N], XT.rearrange("k p t -> p k t")[:, :, c0:c0 + N], R=[B_XT[g]], W=[XTB])
                    for c in range(24):
                        ps, PB = PS.next()
                        for k in range(8):
                            MM(ps[:, :N], Wg[:, k, c * 128:(c + 1) * 128], xn[:, k, :N], start=(k == 0), stop=(k == 7), R=[WB, XNB], W=[PB], signal=(k == 7))
                        ACT(AF.Silu if c < 8 else AF.Sigmoid, gate[:, c, :N], ps[:, :N], R=[PB], W=[GB[c]])
                    for hh in range(4):
                        pq, PQB = PSQ.next()
                        for e in range(2):
                            sq, SQB = sqr.next()
                            ACT(AF.Square, sq[:, :N], ogT[:, hh * 2 + e, :N], R=[OGB_], W=[SQB])
                            MM(pq[:, :N], ones16[:, :], sq[:, :N], start=(e == 0), stop=(e == 1), R=[CB, SQB], W=[PQB], signal=(e == 1))
                        rs, RSB = rsr.next()
                        rstd_from_ssq(rs[:, :N], pq[:, :N], 256, lnv[:, :N], R=[PQB], W=[RSB], WL=LNB)
                        for e in range(2):
                            c = hh * 2 + e
                            t1, T1B = t1r.next(); t2, T2B = t2r.next()
                            DVE("scalar_tensor_tensor", t1[:, :N], ogT[:, c, :N], P["wgla"][:, e:e + 1], rs[:, :N], op0=ALU.mult, op1=ALU.mult,
                                R=[OGB_, CB, RSB], W=[T1B])
                            DVE("tensor_tensor", t1[:, :N], t1[:, :N], gate[:, c, :N], op=ALU.mult, R=[T1B, GB[c]], W=[T1B])
                            DVE("tensor_tensor", t1[:, :N], t1[:, :N], gate[:, 16 + c, :N], op=ALU.mult, R=[T1B, GB[16 + c]], W=[T1B])
                            fw.op("pool", "tensor_tensor", t2[:, :N], oaT[:, c, :N], gate[:, 8 + c, :N], op=ALU.mult, R=[OAB_, GB[8 + c]], W=[T2B])
                            DVE("tensor_tensor", merged[:, c, :N], t1[:, :N], t2[:, :N], op=ALU.add, R=[T1B, T2B], W=[MB[c]])
                    pq, PQB = PSQ.next()
                    for c in range(8):
                        ps, PB = PS.next()
                        for k in range(8):
                            MM(ps[:, :N], Wo[:, k, c * 128:(c + 1) * 128], merged[:, k, :N], start=(k == 0), stop=(k == 7), R=[WB, MB[k]], W=[PB], signal=(k == 7))
                        fw.op("act", "copy", hT[:, c, :N], ps[:, :N], R=[PB], W=[AB[c]])
                        sq, SQB = sqr.next()
                        ACT(AF.Square, sq[:, :N], ps[:, :N], R=[PB], W=[SQB])
                        MM(pq[:, :N], ones16[:, :], sq[:, :N], start=(c == 0), stop=(c == 7), R=[CB, SQB], W=[PQB], signal=(c == 7))
                    rs, RSB = rsr.next()
                    rstd_from_ssq(rs[:, :N], pq[:, :N], D, lnv[:, :N], R=[PQB], W=[RSB], WL=LNB)
                    for c in range(8):
                        t1, T1B = t1r.next()
                        DVE("scalar_tensor_tensor", t1[:, :N], hT[:, c, :N], P["wpost"][:, c:c + 1], rs[:, :N], op0=ALU.mult, op1=ALU.mult,
                            R=[AB[c], CB, RSB], W=[T1B])
                        fw.op("pool", "tensor_tensor", hT[:, c, :N], t1[:, :N], xT[:, c, :N], op=ALU.add, R=[T1B, XTB], W=[AB[c]])
                    fw.dma("sp", HT.rearrange("k p t -> p k t")[:, :, c0:c0 + N], hT[:, :, :N], R=AB, W=[B_HT[g]])

                for g in range(G):
                    group(g, 512, g * 512)
                if sample:
                    group(SG, NS, T)
                fw.flush()
            fw.stack = fw.gstack

        def phase_D2(l):
            P = PRM[l]
            last = (l == DEPTH - 1)
            NG = 256
            with ExitStack() as ph:
                fw.stack = ph
                WB = Buf()
                Wu = fw.sb("Wu", [128, 8, 2 * DFF], BF16)
                Wd = fw.sb("Wd", [128, NJ, D], BF16)
                srcu = w_up[l].rearrange("(k p) n -> p k n", p=128)
                import os
                for k in range(8 if not os.environ.get("D2SKIPW") else 0):
                    for hh in range(2):
                        fw.dma("pool", Wu[:, k, hh * DFF:(hh + 1) * DFF], srcu[:, k, hh * DFF:(hh + 1) * DFF], W=[WB])
                srcd = w_down[l].rearrange("(j p) n -> p j n", p=128)
                for j in range(0, NJ if not os.environ.get("D2SKIPW") else 0, 2):
                    fw.dma("pool", Wd[:, j:j + 2, :], srcd[:, j:j + 2, :], W=[WB])
                hTr = fw.tiles("hT", 1, [128, 8, NG], F32)
                hn = fw.sb("hn", [128, 8, NG], BF16); HNB = Buf()
                sqr = fw.tiles("sq", 3, [128, NG], BF16)
                lnv = fw.sb("lnvE", [128, NG], F32); LNB = Buf()
                rsr = fw.tiles("rs", 2, [128, NG], F32)
                gbr = fw.tiles("gb", 2, [128, NG + 8], F32)
                gprev = fw.sb("gprev", [128, NJ, 2], F32); GPB = [Buf() for _ in range(NJ)]
                accr = fw.tiles("acc", 2, [128, NG], F32)
                ger = fw.tiles("gel", 2, [128, NG], F32)
                hid = fw.sb("hid", [128, NJ, NG], BF16); HB = [Buf() for _ in range(NJ)]
                oT = fw.sb("oT", [128, 8, NG], F32); OTB = [Buf() for _ in range(8)]
                t1r = fw.tiles("t1", 2, [128, NG], F32)
                ytr = fw.tiles("yt", 1, [128, D], F32)
                scs = fw.sb("scs", [128, NSEQ, 2, NJ], F32); SCB = Buf()
                PS = Rot([(fw.ps("UA%d" % i, [128, 512], F32), Buf()) for i in range(5)])
                PSQ = Rot([(fw.ps("UB%d" % i, [128, 512], F32), Buf()) for i in range(1)])
                PY = Rot([(fw.ps("UC%d" % i, [128, 1024], F32), Buf()) for i in range(1)])
                fw.op("dve", "memset", gprev[:], 0.0, W=GPB)

                def group(g512, N, c0, nseq, ydst_fn, xm_buf):
                    Lq = N // nseq
                    hT, HTB = hTr.next()
                    fw.dma("sp", hT[:, :, :N], HT.rearrange("k p t -> p k t")[:, :, c0:c0 + N], R=[B_HT[g512]], W=[HTB])
                    _cp(19.5)
                    pq, PQB = PSQ.next()
                    for c in range(8):
                        sq, SQB = sqr.next()
                        ACT(AF.Square, sq[:, :N], hT[:, c, :N], R=[HTB], W=[SQB])
                        MM(pq[:, :N], ones16[:, :], sq[:, :N], start=(c == 0), stop=(c == 7), R=[CB, SQB], W=[PQB], signal=(c == 7))
                    _cp(19.7)
                    rs, RSB = rsr.next()
                    rstd_from_ssq(rs[:, :N], pq[:, :N], D, lnv[:, :N], R=[PQB], W=[RSB], WL=LNB)
                    _cp(19.8)
                    for c in range(8):
                        DVE("scalar_tensor_tensor", hn[:, c, :N], hT[:, c, :N], P["wpreffn"][:, c:c + 1], rs[:, :N], op0=ALU.mult, op1=ALU.mult,
                            R=[HTB, CB, RSB], W=[HNB])
                    _cp(20)
                    if nseq > 1:
                        for s_ in range(NSEQ):
                            for t_ in range(2):
                                fw.dma("sp", scs[:, s_, t_, :], sc[l, s_, t_].rearrange("(c p) -> p c", p=128), W=[SCB], **SLOW)
                    for j in range(NJ):
                        pu, PUB = PS.next()
                        for k in range(8):
                            MM(pu[:, :N], Wu[:, k, j * 128:(j + 1) * 128], hn[:, k, :N], start=(k == 0), stop=(k == 7), R=[WB, HNB], W=[PUB], signal=(k == 7))
                        pg, PGB = PS.next()
                        for k in range(8):
                            MM(pg[:, :N], Wu[:, k, DFF + j * 128:DFF + (j + 1) * 128], hn[:, k, :N], start=(k == 0), stop=(k == 7), R=[WB, HNB], W=[PGB], signal=(k == 7))
                        gb, GBB = gbr.next()
                        gbv = gb[:, :nseq * (Lq + 2)].rearrange("p (s t) -> p s t", t=Lq + 2)
                        if nseq == 1:
                            fw.op("act", "copy", gbv[:, :, 0:2], gprev[:, j:j + 1, :], R=[GPB[j]], W=[GBB])
                        else:
                            fw.op("act", "copy", gbv[:, :, 0:2], scs[:, :, :, j], R=[SCB], W=[GBB])
                        fw.op("act", "copy", gbv[:, :, 2:Lq + 2], pg[:, :N].rearrange("p (s t) -> p s t", t=Lq), R=[PGB], W=[GBB])
                        acc, ACB = accr.next()
                        av = acc[:, :N].rearrange("p (s t) -> p s t", t=Lq)
                        DVE("tensor_scalar", av, gbv[:, :, 2:Lq + 2], P["cw"][:, 2, j:j + 1], P["cb"][:, j:j + 1], op0=ALU.mult, op1=ALU.add, R=[GBB, CB], W=[ACB])
                        DVE("scalar_tensor_tensor", av, gbv[:, :, 1:Lq + 1], P["cw"][:, 1, j:j + 1], av, op0=ALU.mult, op1=ALU.add, R=[GBB, CB, ACB], W=[ACB])
                        DVE("scalar_tensor_tensor", av, gbv[:, :, 0:Lq], P["cw"][:, 0, j:j + 1], av, op0=ALU.mult, op1=ALU.add, R=[GBB, CB, ACB], W=[ACB])
                        if nseq == 1:
                            fw.op("pool", "tensor_copy", gprev[:, j:j + 1, :], gbv[:, :, Lq:Lq + 2], R=[GBB], W=[GPB[j]])
                        else:
                            fw.op("pool", "tensor_copy", scs[:, :, :, j], gbv[:, :, Lq:Lq + 2], R=[GBB], W=[SCB])
                        gel, GLB = ger.next()
                        ACT(AF.Gelu_apprx_tanh, gel[:, :N], acc[:, :N], R=[ACB], W=[GLB])
                        DVE("tensor_tensor", hid[:, j, :N], gel[:, :N], pu[:, :N], op=ALU.mult, R=[GLB, PUB], W=[HB[j]])
                        _cp(21)
                    if nseq > 1:
                        for s_ in range(NSEQ):
                            for t_ in range(2):
                                fw.dma("sp", conv_s[l, s_, t_].rearrange("(c p) -> p c", p=128), scs[:, s_, t_, :], R=[SCB], W=[B_OUT], **SLOW)
                    _cp(22)
                    pq, PQB = PSQ.next()
                    for c in range(8):
                        ps, PB = PS.next()
                        for j in range(NJ):
                            MM(ps[:, :N], Wd[:, j, c * 128:(c + 1) * 128], hid[:, j, :N], start=(j == 0), stop=(j == NJ - 1), R=[WB, HB[j]], W=[PB], signal=(j == NJ - 1))
                        fw.op("act", "copy", oT[:, c, :N], ps[:, :N], R=[PB], W=[OTB[c]])
                        sq, SQB = sqr.next()
                        ACT(AF.Square, sq[:, :N], ps[:, :N], R=[PB], W=[SQB])
                        MM(pq[:, :N], ones16[:, :], sq[:, :N], start=(c == 0), stop=(c == 7), R=[CB, SQB], W=[PQB], signal=(c == 7))
                    rs, RSB = rsr.next()
                    rstd_from_ssq(rs[:, :N], pq[:, :N], D, lnv[:, :N], R=[PQB], W=[RSB], WL=LNB)
                    for c in range(8):
                        t1, T1B = t1r.next()
                        DVE("scalar_tensor_tensor", t1[:, :N], oT[:, c, :N], P["wpostffn"][:, c:c + 1], rs[:, :N], op0=ALU.mult, op1=ALU.mult,
                            R=[OTB[c], CB, RSB], W=[T1B])
                        fw.op("pool", "tensor_tensor", oT[:, c, :N], t1[:, :N], hT[:, c, :N], op=ALU.add, R=[T1B, HTB], W=[OTB[c]])
                    _cp(23)
                    for tt in range((N + 127) // 128):
                        n = min(128, N - tt * 128)
                        py, PYB = PY.next()
                        for c in range(8):
                            TRP(py[:n, c * 128:(c + 1) * 128], oT[:, c, tt * 128:tt * 128 + n], id32[:, :], R=[OTB[c], CB], W=[PYB], signal=(c == 7))
                        yt, YTB = ytr.next()
                        fw.op("act", "copy", yt[:n, :], py[:n, :], R=[PYB], W=[YTB])
                        dst, DB = ydst_fn(tt, n)
                        fw.dma("sp", dst, yt[:n, :], R=[YTB], W=[DB])

                for g in range(T // NG):
                    c0 = g * NG

                    def ydst(tt, n, c0=c0, g=g):
                        if last:
                            return y_p[c0 + tt * 128:c0 + tt * 128 + n, :], B_OUT
                        return XMID[c0 + tt * 128:c0 + tt * 128 + n, :], B_XMID[g]
                    try:
                        group(c0 // 512, NG, c0, 1, ydst, None)
                    except _Stop:
                        pass
                import os
                for t_ in range(2):
                    if os.environ.get("NOCONVP"):
                        break
                    fw.dma("sp", conv_p[l, t_].rearrange("(c p) -> p c", p=128), gprev[:, :, t_], R=GPB, W=[B_OUT], **SLOW)
                if sample:
                    def ydst_s(tt, n):
                        if last:
                            return y_s[0:n, :], B_OUT
                        return XMID[T:T + n, :], B_XMID[2 * G]
                    group(SG, NS, T, NSEQ, ydst_s, None)
                fw.flush()
            fw.stack = fw.gstack

        for l in range(layers):
            if "A" in phases:
                phase_A(l)
            if "B" in phases:
                phase_B(l)
            if "C" in phases:
                phase_D1(l)
            if "D" in phases:
                phase_D2(l)
        fw.flush(final=True)
        n_ops = fw.n_ops
    return nc, n_ops


def _consts():
    c = {}
    c["c_ident"] = np.eye(128, dtype=np.float32)
    s = np.arange(128)
    c["c_U"] = (s[:, None] <= s[None, :]).astype(np.float32)
    c["c_L"] = (s[:, None] > s[None, :]).astype(np.float32)
    r = np.arange(NS)
    same = (r[:, None] // LS) == (r[None, :] // LS)
    c["c_Us"] = (same & (r[:, None] <= r[None, :])).astype(np.float32)
    c["c_Ls"] = (same & (r[:, None] > r[None, :])).astype(np.float32)
    c["c_ind"] = ((r[:, None] // LS) == np.arange(NSEQ)[None, :]).astype(np.float32)
    half = 8
    inv = (np.float32(500000.0) ** (-np.arange(half, dtype=np.float32) * np.float32(2.0) / np.float32(16))).astype(np.float32)
    pos = np.arange(T_FULL, dtype=np.float32)
    ang = (pos[:, None] * inv[None, :]).astype(np.float32)
    c["c_cos"] = np.cos(ang).astype(np.float32); c["c_sin"] = np.sin(ang).astype(np.float32)
    pos_s = (PAST + (r % LS)).astype(np.float32)
    ang_s = (pos_s[:, None] * inv[None, :]).astype(np.float32)
    c["c_cos_s"] = np.cos(ang_s).astype(np.float32); c["c_sin_s"] = np.sin(ang_s).astype(np.float32)
    return c


def make_in_maps(inp, n_cores=8):
    f = lambda a: np.ascontiguousarray(np.asarray(a, dtype=np.float32))
    shared = {
        "w_in": f(inp["w_in"]), "w_gk2": f(inp["w_gk2"]), "b_gk2": f(inp["b_gk2"]),
        "lq1": f(inp["lambda_q1"]), "lk1": f(inp["lambda_k1"]), "lq2": f(inp["lambda_q2"]), "lk2": f(inp["lambda_k2"]),
        "da_norm_w": f(inp["da_norm_w"]), "gla_norm_w": f(inp["gla_norm_w"]), "w_o": f(inp["w_o"]),
        "pre_mix_w": f(inp["pre_mix_w"]), "post_mix_w": f(inp["post_mix_w"]),
        "pre_ffn_w": f(inp["pre_ffn_w"]), "post_ffn_w": f(inp["post_ffn_w"]),
        "w_up": f(inp["w_up"]), "conv_w": f(inp["conv_w"]), "conv_b": f(inp["conv_b"]), "w_down": f(inp["w_down"]),
    }
    shared.update(_consts())
    maps = []
    for b in range(n_cores):
        m = dict(shared)
        m["x_p"] = f(inp["x_prompt"][b])
        m["x_s"] = f(inp["x_sample"][4 * b:4 * b + 4]).reshape(NS, D)
        m["ck"] = f(inp["cache_k"][:, 4 * b:4 * b + 4]).reshape(DEPTH, NSEQ, PAST, D)
        m["cv"] = f(inp["cache_v"][:, 4 * b:4 * b + 4]).reshape(DEPTH, NSEQ, PAST, D)
        m["sg"] = f(inp["state_gla"][:, 4 * b:4 * b + 4])
        m["sc"] = f(inp["state_conv"][:, 4 * b:4 * b + 4])
        maps.append(m)
    return maps


_NC_CACHE = {}


def kernel(**inputs):
    n = 8
    if "nc" not in _NC_CACHE:
        _NC_CACHE["nc"] = build()[0]
    nc = _NC_CACHE["nc"]
    in_maps = make_in_maps(inputs, n)
    res = run_bass_kernel_spmd(nc, in_maps, core_ids=list(range(n))).results
    B = n
    y_prompt = np.stack([res[b]["y_p"] for b in range(B)]).astype(np.float32)
    y_sample = np.concatenate([res[b]["y_s"].reshape(NSEQ, LS, D) for b in range(B)], axis=0).astype(np.float32)
    k_prompt = np.stack([res[b]["k_p"].reshape(DEPTH, T_FULL, 8, 128) for b in range(B)], axis=1).astype(np.float32)
    v_prompt = np.stack([res[b]["v_p"].reshape(DEPTH, T_FULL, 8, 128) for b in range(B)], axis=1).astype(np.float32)
    gla_prompt = np.stack([res[b]["gla_p"] for b in range(B)], axis=1).astype(np.float32)
    conv_prompt = np.stack([res[b]["conv_p"] for b in range(B)], axis=1).astype(np.float32)
    k_sample = np.concatenate([res[b]["k_s"].reshape(DEPTH, NSEQ, LS, 8, 128) for b in range(B)], axis=1).astype(np.float32)
    v_sample = np.concatenate([res[b]["v_s"].reshape(DEPTH, NSEQ, LS, 8, 128) for b in range(B)], axis=1).astype(np.float32)
    gla_sample = np.concatenate([res[b]["gla_s"] for b in range(B)], axis=1).astype(np.float32)
    conv_sample = np.concatenate([res[b]["conv_s"] for b in range(B)], axis=1).astype(np.float32)
    return (y_prompt, y_sample, k_prompt, v_prompt, gla_prompt, conv_prompt, k_sample, v_sample, gla_sample, conv_sample)
```

```python
import math
import numpy as np
from contextlib import ExitStack
import concourse.bass as bass
import concourse.mybir as mybir
from concourse.bass_utils import run_bass_kernel_spmd

F32 = mybir.dt.float32
BF16 = mybir.dt.bfloat16
AF = mybir.ActivationFunctionType
ALU = mybir.AluOpType

D = 1024
T_FULL = 8192
NS = 64
NSEQ = 4
LS = 16
PAST = 2048
DIN = 8208
DFF = 2816
NJ = DFF // 128
EPS = 1e-6
DEPTH = 2


class Buf:
    __slots__ = ("w", "r")

    def __init__(self):
        self.w = None
        self.r = []


class Eng:
    def __init__(self, name, key):
        self.name = name
        self.key = key
        self.cnt = 0
        self.seen = {}
        self.prog = []
        self.dsems = []
        self.dnext = 0
        self.pending = None


class Rot:
    def __init__(self, items):
        self.items = items
        self.i = 0

    def next(self):
        it = self.items[self.i % len(self.items)]
        self.i += 1
        return it


class FW:
    def __init__(self, nc, stack, n_dma_sems=24):
        self.nc = nc
        self.gstack = stack
        self.stack = stack
        self.sems = {}
        self.semval = {}
        self.engs = {}
        for name in ("pe", "act", "dve", "pool", "sp"):
            key = None
            if name != "sp":
                key = "c_" + name
                self.sems[key] = stack.enter_context(nc.semaphore(key))
                self.semval[key] = 0
            self.engs[name] = Eng(name, key)
        for name in ("sp", "pool"):
            E = self.engs[name]
            for i in range(n_dma_sems if name == "sp" else 8):
                key = "d_%s_%d" % (name, i)
                self.sems[key] = stack.enter_context(nc.semaphore(key))
                self.semval[key] = 0
                E.dsems.append(key)
        self.n_ops = 0
        self.uid = 0

    def sb(self, name, shape, dt):
        self.uid += 1
        return self.stack.enter_context(self.nc.sbuf_tensor("%s_%d" % (name, self.uid), list(shape), dt))

    def ps(self, name, shape, dt=F32):
        self.uid += 1
        return self.stack.enter_context(self.nc.psum_tensor("%s_%d" % (name, self.uid), list(shape), dt))

    def tiles(self, name, n, shape, dt):
        return Rot([(self.sb(name + str(i), shape, dt), Buf()) for i in range(n)])

    def _deps(self, E, reads, writes):
        need = {}
        pe = E.name == "pe"

        def add(t, same_ok):
            if t is None:
                return
            k, v = t
            if k == E.key and not same_ok:
                return
            if need.get(k, 0) < v:
                need[k] = v

        for b in reads:
            add(b.w, not pe)
        for b in writes:
            add(b.w, not pe)
            for t in b.r:
                add(t, False)
        for k, v in need.items():
            if E.seen.get(k, 0) < v:
                E.seen[k] = v
                if k.startswith("c_"):
                    X = self.engs[k[2:]]
                    if v > X.cnt:
                        assert X.pending is not None and v == X.cnt + 1, (k, v, X.cnt)
                        X.pending["sig"] = True
                        X.pending = None
                        X.cnt += 1
                        self.semval[k] = X.cnt
                E.prog.append(lambda h, sem=self.sems[k], v=v: h.wait_ge(sem, v))

    def op(self, en, meth, *args, R=(), W=(), signal=True, **kw):
        E = self.engs[en]
        self._deps(E, R, W)
        ent = {"meth": meth, "args": args, "kw": kw, "sig": signal, "sem": self.sems[E.key]}
        E.prog.append(ent)
        if signal:
            E.cnt += 1
            self.semval[E.key] = E.cnt
            E.pending = None
            t = (E.key, E.cnt)
        else:
            E.pending = ent
            t = (E.key, E.cnt + 1)
        for b in R:
            b.r.append(t)
        for b in W:
            b.w = t
            b.r = []
        self.n_ops += 1

    def dma(self, en, out, in_, R=(), W=(), **kw):
        E = self.engs[en]
        self._deps(E, R, W)
        key = E.dsems[E.dnext % len(E.dsems)]
        E.dnext += 1
        prev = self.semval[key]
        if E.seen.get(key, 0) < prev:
            E.seen[key] = prev
            E.prog.append(lambda h, sem=self.sems[key], v=prev: h.wait_ge(sem, v))
        self.semval[key] = prev + 16
        sem = self.sems[key]
        E.prog.append(lambda h, out=out, in_=in_, sem=sem, kw=kw: h.dma_start(out=out, in_=in_, **kw).then_inc(sem, 16))
        t = (key, prev + 16)
        for b in R:
            b.r.append(t)
        for b in W:
            b.w = t
            b.r = []
        self.n_ops += 1

    def flush(self, final=False):
        for E in self.engs.values():
            if E.name == "sp" or final is False:
                for k, v in self.semval.items():
                    if v > 0 and k != E.key and E.seen.get(k, 0) < v:
                        E.seen[k] = v
                        E.prog.append(lambda h, sem=self.sems[k], v=v: h.wait_ge(sem, v))
        progs = {n: E.prog for n, E in self.engs.items()}
        for E in self.engs.values():
            E.prog = []
            E.pending = None

        def run(h, prog):
            for f in prog:
                if isinstance(f, dict):
                    ins = getattr(h, f["meth"])(*f["args"], **f["kw"])
                    if f["sig"]:
                        ins.then_inc(f["sem"], 1)
                else:
                    f(h)
        with self.nc.Block() as block:
            @block.tensor
            def _(h):
                run(h, progs["pe"])

            @block.scalar
            def _(h):
                run(h, progs["act"])

            @block.vector
            def _(h):
                run(h, progs["dve"])

            @block.gpsimd
            def _(h):
                run(h, progs["pool"])

            @block.sync
            def _(h):
                run(h, progs["sp"])


class _Stop(Exception):
    pass


DBG_STOP = None
DBG_OUT = False


def _cp(i):
    if DBG_STOP is not None and i >= DBG_STOP:
        raise _Stop()


def build(T=T_FULL, layers=DEPTH, phases="ABCD", sample=True):
    nc = bass.Bass("TRN2", target_bir_lowering=False)
    TT = T + NS
    NT = T // 128
    G = T // 512
    SG = G

    def din(name, shape):
        return nc.dram_tensor(name, list(shape), F32, kind="ExternalInput").ap()

    def dout(name, shape):
        return nc.dram_tensor(name, list(shape), F32, kind="ExternalOutput").ap()

    def dscr(name, shape, dt):
        return nc.dram_tensor(name, list(shape), dt, kind=("ExternalOutput" if (DBG_OUT and name in ("OA", "OG", "HT")) else "Internal")).ap()

    x_p = din("x_p", [T_FULL, D])
    x_s = din("x_s", [NS, D])
    ck = din("ck", [DEPTH, NSEQ, PAST, D])
    cv = din("cv", [DEPTH, NSEQ, PAST, D])
    sg = din("sg", [DEPTH, NSEQ, 4, 128, 256])
    sc = din("sc", [DEPTH, NSEQ, 2, DFF])
    w_in = din("w_in", [DEPTH, D, DIN])
    w_gk2 = din("w_gk2", [DEPTH, 16, 512])
    b_gk2 = din("b_gk2", [DEPTH, 512])
    lq1 = din("lq1", [DEPTH, 64]); lk1 = din("lk1", [DEPTH, 64])
    lq2 = din("lq2", [DEPTH, 64]); lk2 = din("lk2", [DEPTH, 64])
    da_norm_w = din("da_norm_w", [DEPTH, 128])
    gla_norm_w = din("gla_norm_w", [DEPTH, 256])
    w_o = din("w_o", [DEPTH, D, D])
    pre_mix_w = din("pre_mix_w", [DEPTH, D]); post_mix_w = din("post_mix_w", [DEPTH, D])
    pre_ffn_w = din("pre_ffn_w", [DEPTH, D]); post_ffn_w = din("post_ffn_w", [DEPTH, D])
    w_up = din("w_up", [DEPTH, D, 2 * DFF])
    conv_w = din("conv_w", [DEPTH, 3, DFF]); conv_b = din("conv_b", [DEPTH, DFF])
    w_down = din("w_down", [DEPTH, DFF, D])
    c_ident = din("c_ident", [128, 128])
    c_U = din("c_U", [128, 128]); c_L = din("c_L", [128, 128])
    c_Us = din("c_Us", [NS, NS]); c_Ls = din("c_Ls", [NS, NS])
    c_ind = din("c_ind", [NS, NSEQ])
    c_cos = din("c_cos", [T_FULL, 8]); c_sin = din("c_sin", [T_FULL, 8])
    c_cos_s = din("c_cos_s", [NS, 8]); c_sin_s = din("c_sin_s", [NS, 8])

    y_p = dout("y_p", [T_FULL, D]); y_s = dout("y_s", [NS, D])
    k_p = dout("k_p", [DEPTH, T_FULL, D]); v_p = dout("v_p", [DEPTH, T_FULL, D])
    gla_p = dout("gla_p", [DEPTH, 4, 128, 256]); conv_p = dout("conv_p", [DEPTH, 2, DFF])
    k_s = dout("k_s", [DEPTH, NS, D]); v_s = dout("v_s", [DEPTH, NS, D])
    gla_s = dout("gla_s", [DEPTH, NSEQ, 4, 128, 256]); conv_s = dout("conv_s", [DEPTH, NSEQ, 2, DFF])

    XT = dscr("XT", [8, 128, TT], F32)
    XN = dscr("XN", [8, 128, TT], BF16)
    QT = dscr("QT", [8, 128, TT], BF16)
    KT = dscr("KT", [8, 128, TT], BF16)
    V16 = dscr("V16", [TT, D], BF16)
    OA = dscr("OA", [8, 128, TT], BF16)
    OG = dscr("OG", [8, 128, TT], BF16)
    HT = dscr("HT", [8, 128, TT], F32)
    XMID = dscr("XMID", [TT, D], F32)

    def bufs(n):
        return [Buf() for _ in range(n)]

    B_XT = bufs(G + 1); B_XN = bufs(G + 1); B_QT = bufs(G + 1); B_KT = bufs(G + 1)
    B_V16 = bufs(G + 1); B_OG = bufs(G + 1); B_HT = bufs(G + 1)
    B_OA = [bufs(G + 1) for _ in range(8)]
    B_XMID = bufs(2 * G + 1)
    B_OUT = Buf()

    with ExitStack() as gst:
        fw = FW(nc, gst)

        def ACT(func, out, in_, R, W, **kw):
            fw.op("act", "activation", out=out, in_=in_, func=func, R=R, W=W, **kw)

        def DVE(meth, *a, R, W, **kw):
            fw.op("dve", meth, *a, R=R, W=W, **kw)

        def MM(out, lhsT, rhs, start, stop, R, W, signal=True):
            fw.op("pe", "matmul", out, lhsT, rhs, start=start, stop=stop, R=R, W=W, signal=signal)

        def TRP(out, in_, ident, R, W, signal=True):
            fw.op("pe", "transpose", out, in_, ident, R=R, W=W, signal=signal)

        SLOW = dict(allow_slow_non_contiguous=True)

        CB = Buf()
        id32 = fw.sb("id32", [128, 128], F32); id16 = fw.sb("id16", [128, 128], BF16)
        ones32 = fw.sb("ones32", [128, 128], F32); ones16 = fw.sb("ones16", [128, 128], BF16)
        U32 = fw.sb("U32", [128, 128], F32); L32 = fw.sb("L32", [128, 128], F32)
        Us32 = fw.sb("Us32", [NS, NS], F32); Ls32 = fw.sb("Ls32", [NS, NS], F32)
        ind32 = fw.sb("ind32", [NS, NSEQ], F32)
        cosT = fw.sb("cosT", [128, T_FULL // 128, 8], F32); sinT = fw.sb("sinT", [128, T_FULL // 128, 8], F32)
        cosS = fw.sb("cosS", [NS, 8], F32); sinS = fw.sb("sinS", [NS, 8], F32)
        fw.dma("sp", id32[:], c_ident, W=[CB]); fw.dma("pool", id16[:], c_ident, W=[CB])
        fw.dma("sp", U32[:], c_U, W=[CB]); fw.dma("sp", L32[:], c_L, W=[CB])
        fw.dma("sp", Us32[:], c_Us, W=[CB]); fw.dma("sp", Ls32[:], c_Ls, W=[CB])
        fw.dma("sp", ind32[:], c_ind, W=[CB])
        fw.dma("sp", cosT[:], c_cos.rearrange("(j p) e -> p j e", p=128), W=[CB])
        fw.dma("sp", sinT[:], c_sin.rearrange("(j p) e -> p j e", p=128), W=[CB])
        fw.dma("sp", cosS[:], c_cos_s, W=[CB]); fw.dma("sp", sinS[:], c_sin_s, W=[CB])
        fw.op("dve", "memset", ones32[:], 1.0, W=[CB]); fw.op("dve", "memset", ones16[:], 1.0, W=[CB])

        PRM = []
        for l in range(DEPTH):
            p = {}
            for nm, src in (("wpre", pre_mix_w), ("wpost", post_mix_w), ("wpreffn", pre_ffn_w), ("wpostffn", post_ffn_w)):
                p[nm] = fw.sb(nm, [128, 8], F32)
                fw.dma("sp", p[nm][:], src[l].rearrange("(k p) -> p k", p=128), W=[CB], **SLOW)
            p["wda"] = fw.sb("wda", [128, 1], F32)
            fw.dma("sp", p["wda"][:], da_norm_w[l].rearrange("(p o) -> p o", o=1), W=[CB], **SLOW)
            lam_init = 0.8 - 0.6 * math.exp(-0.3 * l)
            fw.op("dve", "tensor_scalar", p["wda"][:], p["wda"][:], 1.0 - lam_init, None, op0=ALU.mult, R=[CB], W=[CB])
            p["wgla"] = fw.sb("wgla", [128, 2], F32)
            fw.dma("sp", p["wgla"][:], gla_norm_w[l].rearrange("(e p) -> p e", p=128), W=[CB], **SLOW)
            p["cw"] = fw.sb("cw", [128, 3, NJ], F32)
            for j3 in range(3):
                fw.dma("sp", p["cw"][:, j3, :], conv_w[l, j3].rearrange("(c p) -> p c", p=128), W=[CB], **SLOW)
            p["cb"] = fw.sb("cb", [128, NJ], F32)
            fw.dma("sp", p["cb"][:], conv_b[l].rearrange("(c p) -> p c", p=128), W=[CB], **SLOW)
            p["bg2"] = fw.sb("bg2", [128, 512], F32)
            fw.dma("sp", p["bg2"][:], b_gk2[l].partition_broadcast(128), W=[CB])
            p["Wg2"] = fw.sb("Wg2", [16, 512], BF16)
            fw.dma("pool", p["Wg2"][:], w_gk2[l], W=[CB])
            lt = [fw.sb("lt%d" % i, [128, 64], F32) for i in range(4)]
            for t_, src in zip(lt, (lq1, lk1, lq2, lk2)):
                fw.dma("sp", t_[:], src[l].partition_broadcast(128), W=[CB])
            s1 = fw.sb("s1", [128, 2], F32)
            fw.op("dve", "memset", s1[:], 0.0, W=[CB])
            fw.op("dve", "tensor_tensor", lt[0][:], lt[0][:], lt[1][:], op=ALU.mult, R=[CB], W=[CB])
            fw.op("dve", "tensor_tensor", lt[2][:], lt[2][:], lt[3][:], op=ALU.mult, R=[CB], W=[CB])
            fw.op("dve", "reduce_sum", s1[:, 0:1], lt[0][:], axis=mybir.AxisListType.X, R=[CB], W=[CB])
            fw.op("dve", "reduce_sum", s1[:, 1:2], lt[2][:], axis=mybir.AxisListType.X, R=[CB], W=[CB])
            ACT(AF.Exp, s1[:], s1[:], R=[CB], W=[CB])
            p["neglam"] = fw.sb("neglam", [128, 1], F32)
            fw.op("dve", "tensor_tensor", p["neglam"][:], s1[:, 1:2], s1[:, 0:1], op=ALU.subtract, R=[CB], W=[CB])
            fw.op("dve", "tensor_scalar", p["neglam"][:], p["neglam"][:], -lam_init, None, op0=ALU.add, R=[CB], W=[CB])
            PRM.append(p)
        fw.flush()

        def rstd_from_ssq(out, ssq_ap, n_feat, lnv, R, W, WL):
            ACT(AF.Ln, lnv, ssq_ap, R=R, W=[WL], scale=1.0 / n_feat, bias=epsc[:lnv.shape[0], 0:1])
            ACT(AF.Exp, out, lnv, R=[WL], W=W, scale=-0.5)

        epsc = fw.sb("epsc", [128, 1], F32)
        fw.op("dve", "memset", epsc[:], EPS, W=[CB])

        def phase_A(l):
            P = PRM[l]
            with ExitStack() as ph:
                fw.stack = ph
                WB = Buf()
                Wa = fw.sb("Wa", [128, 8, 5120], BF16)
                Wlr = fw.sb("Wlr", [128, 8, 16], BF16)
                src = w_in[l].rearrange("(k p) n -> p k n", p=128)
                for k in range(8):
                    for hh in range(2):
                        fw.dma("pool", Wa[:, k, hh * 2560:(hh + 1) * 2560], src[:, k, hh * 2560:(hh + 1) * 2560], W=[WB])
                fw.dma("pool", Wlr[:], src[:, :, 6144:6160], W=[WB])
                xrot = fw.tiles("x", 2, [128, D], F32)
                junk = fw.sb("junk", [128, D], BF16); JB = Buf()
                ssq = fw.sb("ssq", [128, 1], F32); lnv = fw.sb("lnv", [128, 1], F32); rstd = fw.sb("rstd", [128, 1], F32)
                SB_ = Buf(); LB = Buf(); RB = Buf()
                xn = fw.sb("xn", [128, D], BF16); XNB = Buf()
                xTrot = fw.tiles("xTs", 2, [128, 8, 128], F32)
                xnTrot = fw.tiles("xnT", 2, [128, 8, 128], BF16)
                qrot16 = fw.sb("qrot16", [128, D], BF16); QRB = Buf()
                krot32 = fw.sb("krot32", [128, D], F32); KRB = Buf()
                krot16 = fw.sb("krot16", [128, D], BF16); KR16B = Buf()
                v32 = fw.sb("v32", [128, D], F32); V32B = Buf()
                v16 = fw.sb("v16", [128, D], BF16); V16B = Buf()
                QTs = fw.sb("QTs", [128, 8, 512], BF16); QTSB = Buf()
                KTs = fw.sb("KTs", [128, 8, 512], BF16); KTSB = Buf()
                OGs = fw.sb("OGs", [128, 8, 512], BF16); OGSB = Buf()
                rt = [fw.sb("rt%d" % i, [128, 16, 8], F32) for i in range(4)]; RTB = [Buf() for _ in range(4)]
                lrT = fw.sb("lrT", [16, 128], BF16); LRB = Buf()
                ge = fw.sb("ge", [128, 512], F32); GEB = Buf()
                g32 = fw.sb("g32", [128, 512], F32); G32B = Buf()
                E1 = fw.sb("E1", [128, 512], F32); E2 = fw.sb("E2", [128, 512], F32); E3 = fw.sb("E3", [128, 512], F32)
                E1B = Buf(); E2B = Buf(); E3B = Buf()
                ebl = fw.sb("ebl", [128, 16], F32); EBLB = Buf()
                qt16 = fw.sb("qt16", [128, 512], BF16); kt16 = fw.sb("kt16", [128, 512], BF16)
                kh16 = fw.sb("kh16", [128, 512], BF16); khm = fw.sb("khm", [128, 512], BF16)
                QT16B = Buf(); KT16B = Buf(); KH16B = Buf(); KHMB = Buf()
                vg16 = fw.sb("vg16", [128, D], BF16); VG16B = Buf()
                qkT = fw.sb("qkT", [128, 8, 128], BF16); QKTB = Buf()
                AT16 = fw.sb("AT16", [128, 4, 128], BF16); ATB = Buf()
                ogi = fw.sb("ogi", [128, 8, 128], F32); OGIB = Buf()
                S32 = fw.sb("S32", [128, 4, 256], F32); S32B = Buf()
                S16 = fw.sb("S16", [128, 4, 256], BF16); S16B = Buf()
                P2 = Rot([(fw.ps("P2_%d" % i, [128, 1024], F32), Buf()) for i in range(3)])
                P1 = Rot([(fw.ps("P1_%d" % i, [128, 512], F32), Buf()) for i in range(1)])
                P1b = Rot([(fw.ps("P1b_%d" % i, [128, 1024], BF16), Buf()) for i in range(1)])

                fw.op("dve", "memset", S32[:], 0.0, W=[S32B])
                fw.op("dve", "memset", S16[:], 0.0, W=[S16B])

                def tile(xsrc, XSB, n, nseq, t0, g, ti, cos_ap, sin_ap, kdst, vdst, last_in_group, ncols):
                    Lq = n // nseq
                    Uc = U32 if nseq == 1 else Us32
                    Lc = L32 if nseq == 1 else Ls32
                    xt, XB = xrot.next()
                    fw.dma("sp", xt[:n], xsrc, R=[XSB] if XSB is not None else [], W=[XB])
                    fw.op("dve", "memset", ssq[:n], 0.0, W=[SB_])
                    ACT(AF.Square, junk[:n], xt[:n], R=[XB], W=[JB, SB_], accum_out=ssq[:n, 0:1])
                    rstd_from_ssq(rstd[:n], ssq[:n], D, lnv[:n], R=[SB_], W=[RB], WL=LB)
                    DVE("tensor_scalar", xn[:n], xt[:n], rstd[:n, 0:1], None, op0=ALU.mult, R=[XB, RB], W=[XNB])
                    _cp(1)
                    pX, PXB = P2.next()
                    pXv = pX[:].rearrange("p (k t) -> p k t", t=128)
                    for k in range(8):
                        TRP(pXv[:, k, :n], xt[:n, k * 128:(k + 1) * 128], id32[:n, :n], R=[XB, CB], W=[PXB], signal=(k == 7))
                    xTs, XTSB = xTrot.next()
                    ACT(AF.Copy, xTs[:, :, :n], pXv[:, :, :n], R=[PXB], W=[XTSB])
                    fw.dma("sp", XT.rearrange("k p t -> p k t")[:, :, t0:t0 + n], xTs[:, :, :n], R=[XTSB], W=[B_XT[g]])
                    _cp(2)
                    pN, PNB = P1b.next()
                    pNv = pN[:].rearrange("p (k t) -> p k t", t=128)
                    for k in range(8):
                        TRP(pNv[:, k, :n], xn[:n, k * 128:(k + 1) * 128], id16[:n, :n], R=[XNB, CB], W=[PNB], signal=(k == 7))
                    xnT, XNTB = xnTrot.next()
                    DVE("tensor_tensor", xnT[:, :, :n], pNv[:, 0:8, :n], P["wpre"][:, :].unsqueeze(2).to_broadcast([128, 8, n]),
                        op=ALU.mult, R=[PNB, CB], W=[XNTB])
                    fw.dma("sp", XN.rearrange("k p t -> p k t")[:, :, t0:t0 + n], xnT[:, :, :n], R=[XNTB], W=[B_XN[g]])

                    _cp(3)
                    def proj(c0):
                        pt, PB = P2.next()
                        for half in range(2):
                            for k in range(8):
                                MM(pt[:n, half * 512:(half + 1) * 512], xnT[:, k, :n], Wa[:, k, c0 + half * 512:c0 + (half + 1) * 512],
                                   start=(k == 0), stop=(k == 7), R=[XNTB, WB], W=[PB], signal=(k == 7 and half == 1))
                        return pt, PB

                    def rope(pt, PB, out, OB):
                        v = pt[:n].rearrange("p (g d) -> p g d", d=64)
                        o = out[:n].rearrange("p (g d) -> p g d", d=64)
                        cb_ = cos_ap.unsqueeze(1).to_broadcast([n, 16, 8])
                        sb_ = sin_ap.unsqueeze(1).to_broadcast([n, 16, 8])
                        DVE("tensor_tensor", rt[0][:n], v[:, :, 0:8], cb_, op=ALU.mult, R=[PB, CB], W=[RTB[0]])
                        DVE("tensor_tensor", rt[1][:n], v[:, :, 8:16], sb_, op=ALU.mult, R=[PB, CB], W=[RTB[1]])
                        DVE("tensor_tensor", rt[2][:n], v[:, :, 8:16], cb_, op=ALU.mult, R=[PB, CB], W=[RTB[2]])
                        DVE("tensor_tensor", rt[3][:n], v[:, :, 0:8], sb_, op=ALU.mult, R=[PB, CB], W=[RTB[3]])
                        DVE("tensor_copy", o[:, :, 16:64], v[:, :, 16:64], R=[PB], W=[OB])
                        DVE("tensor_tensor", o[:, :, 0:8], rt[0][:n], rt[1][:n], op=ALU.subtract, R=[RTB[0], RTB[1]], W=[OB])
                        DVE("tensor_tensor", o[:, :, 8:16], rt[2][:n], rt[3][:n], op=ALU.add, R=[RTB[2], RTB[3]], W=[OB])

                    pq, PQB = proj(0)
                    _cp(3.2)
                    rope(pq, PQB, qrot16, QRB)
                    _cp(3.5)
                    pT, PTB = P1b.next()
                    pTv = pT[:].rearrange("p (k t) -> p k t", t=128)
                    for h in range(8):
                        TRP(pTv[:, h, :n], qrot16[:n, h * 128:(h + 1) * 128], id16[:n, :n], R=[QRB, CB], W=[PTB], signal=(h == 7))
                    _cp(3.8)
                    DVE("tensor_copy", QTs[:, :, ti * 128:ti * 128 + n], pTv[:, 0:8, :n], R=[PTB], W=[QTSB])
                    _cp(4)
                    pk, PKB = proj(1024)
                    rope(pk, PKB, krot32, KRB)
                    fw.dma("sp", kdst, krot32[:n], R=[KRB], W=[B_OUT])
                    fw.op("act", "copy", krot16[:n], krot32[:n], R=[KRB], W=[KR16B])
                    pT, PTB = P1b.next()
                    pTv = pT[:].rearrange("p (k t) -> p k t", t=128)
                    for h in range(8):
                        TRP(pTv[:, h, :n], krot16[:n, h * 128:(h + 1) * 128], id16[:n, :n], R=[KR16B, CB], W=[PTB], signal=(h == 7))
                    DVE("tensor_copy", KTs[:, :, ti * 128:ti * 128 + n], pTv[:, 0:8, :n], R=[PTB], W=[KTSB])
                    _cp(5)
                    pv, PVB = proj(2048)
                    _cp(5.2)
                    fw.op("act", "copy", v32[:n], pv[:n], R=[PVB], W=[V32B])
                    _cp(5.4)
                    DVE("tensor_copy", v16[:n], v32[:n], R=[V32B], W=[V16B])
                    _cp(5.6)
                    fw.dma("sp", vdst, v32[:n], R=[V32B], W=[B_OUT])
                    _cp(5.8)
                    fw.dma("sp", V16[t0:t0 + n, :], v16[:n], R=[V16B], W=[B_V16[g]])
                    _cp(6)
                    if last_in_group:
                        fw.dma("sp", QT.rearrange("h p t -> p h t")[:, :, g * 512:g * 512 + ncols], QTs[:, :, :ncols], R=[QTSB], W=[B_QT[g]])
                        fw.dma("sp", KT.rearrange("h p t -> p h t")[:, :, g * 512:g * 512 + ncols], KTs[:, :, :ncols], R=[KTSB], W=[B_KT[g]])

                    _cp(7)
                    pl, PLB = P1.next()
                    for k in range(8):
                        MM(pl[0:16, :n], Wlr[:, k, :], xnT[:, k, :n], start=(k == 0), stop=(k == 7), R=[XNTB, WB], W=[PLB], signal=(k == 7))
                    fw.op("act", "copy", lrT[:, :n], pl[0:16, :n], R=[PLB], W=[LRB])
                    pqk, PQKB = proj(3072)
                    pvg, PVGB = proj(4096)
                    fw.op("act", "copy", vg16[:n], pvg[:n], R=[PVGB], W=[VG16B])
                    pg, PGB = P1.next()
                    MM(pg[:n, :], lrT[:, :n], P["Wg2"][:, :], start=True, stop=True, R=[LRB, CB], W=[PGB])
                    DVE("tensor_tensor", ge[:n], pg[:n, :], P["bg2"][:n], op=ALU.add, R=[PGB, CB], W=[GEB])
                    ACT(AF.Exp, ge[:n], ge[:n], R=[GEB], W=[GEB], scale=-1.0)
                    ACT(AF.Ln, ge[:n], ge[:n], R=[GEB], W=[GEB], bias=1.0)
                    DVE("tensor_scalar", g32[:n], ge[:n], -1.0 / 16.0, None, op0=ALU.mult, R=[GEB], W=[G32B])
                    pb, PBB = P2.next()
                    MM(pb[:n, 0:512], Uc[:n, :n], g32[:n, :], start=True, stop=True, R=[CB, G32B], W=[PBB], signal=False)
                    MM(pb[:n, 512:1024], Lc[:n, :n], g32[:n, :], start=True, stop=True, R=[CB, G32B], W=[PBB])
                    ACT(AF.Exp, E1[:n], pb[:n, 0:512], R=[PBB], W=[E1B])
                    ACT(AF.Exp, E2[:n], pb[:n, 0:512], R=[PBB], W=[E2B], scale=-1.0)
                    ACT(AF.Exp, E3[:n], pb[:n, 512:1024], R=[PBB], W=[E3B])
                    pbl, PBLB = P1.next()
                    for h in range(4):
                        MM(pbl[:, h * nseq:(h + 1) * nseq], g32[:n, h * 128:(h + 1) * 128],
                           (ones32[:n, 0:1] if nseq == 1 else ind32[:n, :nseq]), start=True, stop=True, R=[G32B, CB], W=[PBLB], signal=(h == 3))
                    ACT(AF.Exp, ebl[:, :4 * nseq], pbl[:, :4 * nseq], R=[PBLB], W=[EBLB])
                    DVE("scalar_tensor_tensor", qt16[:n], pqk[:n, 0:512], 128.0 ** -0.5, E1[:n], op0=ALU.mult, op1=ALU.mult, R=[PQKB, E1B], W=[QT16B])
                    DVE("tensor_tensor", kt16[:n], pqk[:n, 512:1024], E2[:n], op=ALU.mult, R=[PQKB, E2B], W=[KT16B])
                    DVE("tensor_tensor", kh16[:n], pqk[:n, 512:1024], E3[:n], op=ALU.mult, R=[PQKB, E3B], W=[KH16B])
                    _cp(8)
                    pT2, PT2B = P1b.next()
                    pT2v = pT2[:].rearrange("p (k t) -> p k t", t=128)
                    for h in range(4):
                        TRP(pT2v[:, h, :n], qt16[:n, h * 128:(h + 1) * 128], id16[:n, :n], R=[QT16B, CB], W=[PT2B], signal=False)
                        TRP(pT2v[:, 4 + h, :n], kt16[:n, h * 128:(h + 1) * 128], id16[:n, :n], R=[KT16B, CB], W=[PT2B], signal=(h == 3))
                    DVE("tensor_copy", qkT[:, :, :n], pT2v[:, 0:8, :n], R=[PT2B], W=[QKTB])
                    pA, PAB = P1.next()
                    pAv = pA[:].rearrange("p (h t) -> p h t", t=128)
                    for h in range(4):
                        MM(pAv[:n, h, :n], qkT[:, 4 + h, :n], qkT[:, h, :n], start=True, stop=True, R=[QKTB], W=[PAB], signal=(h == 3))
                    DVE("tensor_tensor", AT16[:n, :, :n], pAv[:n, :, :n], Uc[:n, :n].unsqueeze(1).to_broadcast([n, 4, n]),
                        op=ALU.mult, R=[PAB, CB], W=[ATB])
                    _cp(9)
                    pO, POB = P2.next()
                    pOv = pO[:].rearrange("p (c t) -> p c t", t=128)
                    for h in range(4):
                        for e in range(2):
                            MM(pOv[:, h * 2 + e, :n], vg16[:n, h * 256 + e * 128:h * 256 + (e + 1) * 128], AT16[:n, h, :n],
                               start=True, stop=True, R=[VG16B, ATB], W=[POB], signal=(h == 3 and e == 1))
                    pI, PIB = P2.next()
                    pIv = pI[:].rearrange("p (c t) -> p c t", t=128)
                    for s in range(nseq):
                        if nseq > 1:
                            fw.dma("sp", S32[:], sg[l, s].rearrange("h p v -> p h v"), W=[S32B])
                            fw.op("act", "copy", S16[:], S32[:], R=[S32B], W=[S16B])
                        for h in range(4):
                            for e in range(2):
                                MM(pIv[:, h * 2 + e, s * Lq:(s + 1) * Lq], S16[:, h, e * 128:(e + 1) * 128], qkT[:, h, s * Lq:(s + 1) * Lq],
                                   start=True, stop=True, R=[S16B, QKTB], W=[PIB], signal=(h == 3 and e == 1))
                        if nseq > 1:
                            DVE("tensor_scalar", khm[:n], kh16[:n], ind32[:n, s:s + 1], None, op0=ALU.mult, R=[KH16B, CB], W=[KHMB])
                            khs, KHSB = khm, KHMB
                        else:
                            khs, KHSB = kh16, KH16B
                        for hp in range(2):
                            pS, PSB = P1.next()
                            pSv = pS[:].rearrange("p (h v) -> p h v", v=256)
                            for hh in range(2):
                                h = hp * 2 + hh
                                MM(pSv[:, hh, :], khs[:n, h * 128:(h + 1) * 128], vg16[:n, h * 256:(h + 1) * 256], start=True, stop=True,
                                   R=[KHSB, VG16B], W=[PSB], signal=(hh == 1))
                            for hh in range(2):
                                h = hp * 2 + hh
                                DVE("scalar_tensor_tensor", S32[:, h, :], S32[:, h, :], ebl[:, h * nseq + s:h * nseq + s + 1], pSv[:, hh, :],
                                    op0=ALU.mult, op1=ALU.add, R=[S32B, EBLB, PSB], W=[S32B])
                        fw.op("act", "copy", S16[:], S32[:], R=[S32B], W=[S16B])
                        if nseq > 1:
                            fw.dma("sp", gla_s[l, s].rearrange("h p v -> p h v"), S32[:], R=[S32B], W=[B_OUT])
                    _cp(11)
                    fw.op("act", "copy", ogi[:, :, :n], pIv[:, :, :n], R=[PIB], W=[OGIB])
                    DVE("tensor_tensor", OGs[:, :, ti * 128:ti * 128 + n], pOv[:, :, :n], ogi[:, :, :n], op=ALU.add, R=[POB, OGIB], W=[OGSB])
                    if last_in_group:
                        fw.dma("sp", OG.rearrange("c p t -> p c t")[:, :, g * 512:g * 512 + ncols], OGs[:, :, :ncols], R=[OGSB], W=[B_OG[g]])

                for i in range(NT):
                    g, ti = divmod(i, 4)
                    if l == 0:
                        xsrc, XSB = x_p[i * 128:(i + 1) * 128, :], None
                    else:
                        xsrc, XSB = XMID[i * 128:(i + 1) * 128, :], B_XMID[i // 2]
                    try:
                        tile(xsrc, XSB, 128, 1, i * 128, g, ti, cosT[:, i, :], sinT[:, i, :],
                             k_p[l, i * 128:(i + 1) * 128, :], v_p[l, i * 128:(i + 1) * 128, :], ti == 3, 512)
                    except _Stop:
                        pass
                fw.dma("sp", gla_p[l].rearrange("h p v -> p h v"), S32[:], R=[S32B], W=[B_OUT])
                if sample:
                    if l == 0:
                        xsrc, XSB = x_s[:, :], None
                    else:
                        xsrc, XSB = XMID[T:T + NS, :], B_XMID[2 * G]
                    tile(xsrc, XSB, NS, NSEQ, T, SG, 0, cosS[:, :], sinS[:, :], k_s[l], v_s[l], True, NS)
                fw.flush()
            fw.stack = fw.gstack

        def phase_B(l):
            P = PRM[l]
            with ExitStack() as ph:
                fw.stack = ph
                KTh = fw.tiles("KTh", 2, [128, T], BF16)
                QTh = fw.tiles("QTh", 2, [128, T], BF16)
                Vh = fw.tiles("Vh", 2, [128, NT, 128], BF16)
                PTr = fw.tiles("PT", 4, [128, 512], BF16)
                r1 = fw.sb("r1", [128, 512], F32); r2 = fw.sb("r2", [128, 512], F32)
                o1 = fw.sb("o1", [128, 512], F32); o2 = fw.sb("o2", [128, 512], F32)
                oa = fw.sb("oa", [128, 512], F32); sq = fw.sb("sq", [128, 512], F32)
                lnv = fw.sb("lnvB", [128, 512], F32); rstd = fw.sb("rstdB", [128, 512], F32)
                R1B = Buf(); R2B = Buf(); O1B = Buf(); O2B = Buf(); OAB = Buf(); SQB = Buf(); LNB = Buf(); RSB = Buf()
                oanr = fw.tiles("oan", 2, [128, 512], BF16)
                lacc = [(fw.sb("lacc%d" % i, [128, 512], F32), Buf()) for i in range(2)]
                STr = Rot([(fw.ps("ST%d" % i, [128, 512], F32), Buf()) for i in range(3)])
                PTb = Rot([(fw.ps("PTb%d" % i, [128, 1024], BF16), Buf()) for i in range(1)])
                Ops = [(fw.ps("O%d" % i, [128, 512], F32), Buf()) for i in range(2)]
                Lps = [(fw.ps("L%d" % i, [128, 512], F32), Buf()) for i in range(2)]

                def epilogue(h, g, N, c0):
                    DVE("reciprocal", r1[:, :N], Lps[0][0][:, :N], R=[Lps[0][1]], W=[R1B])
                    DVE("reciprocal", r2[:, :N], Lps[1][0][:, :N], R=[Lps[1][1]], W=[R2B])
                    DVE("tensor_tensor", o1[:, :N], Ops[0][0][:, :N], r1[:, :N], op=ALU.mult, R=[Ops[0][1], R1B], W=[O1B])
                    DVE("tensor_tensor", o2[:, :N], Ops[1][0][:, :N], r2[:, :N], op=ALU.mult, R=[Ops[1][1], R2B], W=[O2B])
                    DVE("scalar_tensor_tensor", oa[:, :N], o2[:, :N], P["neglam"][:, 0:1], o1[:, :N], op0=ALU.mult, op1=ALU.add,
                        R=[O1B, O2B, CB], W=[OAB])
                    ACT(AF.Square, sq[:, :N], oa[:, :N], R=[OAB], W=[SQB])
                    st, STB = STr.next()
                    MM(st[:, :N], ones32[:, :], sq[:, :N], start=True, stop=True, R=[CB, SQB], W=[STB])
                    rstd_from_ssq(rstd[:, :N], st[:, :N], 128, lnv[:, :N], R=[STB], W=[RSB], WL=LNB)
                    oan, OANB = oanr.next()
                    DVE("scalar_tensor_tensor", oan[:, :N], oa[:, :N], P["wda"][:, 0:1], rstd[:, :N], op0=ALU.mult, op1=ALU.mult,
                        R=[OAB, CB, RSB], W=[OANB])
                    fw.dma("sp", OA[h, :, c0:c0 + N], oan[:, :N], R=[OANB], W=[B_OA[h][g]])

                for h in range(8):
                    kth, KB = KTh.next(); qth, QB = QTh.next(); vh, VB = Vh.next()
                    fw.dma("sp", kth[:, :], KT[h, :, 0:T], R=B_KT[:G], W=[KB])
                    fw.dma("sp", qth[:, :], QT[h, :, 0:T], R=B_QT[:G], W=[QB])
                    fw.dma("sp", vh[:, :, :], V16[0:T, h * 128:(h + 1) * 128].rearrange("(j p) d -> p j d", p=128), R=B_V16[:G], W=[VB])
                    for g in range(G):
                        q0 = g * 512
                        items = [(j, c) for j in range(4 * (g + 1)) for c in range(2)]
                        nit = len(items)
                        pts = [None] * nit

                        def qk(i):
                            j, c = items[i]
                            off = max(j - 4 * g, 0) * 128
                            st, STB = STr.next()
                            MM(st[:, off:512], kth[c * 64:(c + 1) * 64, j * 128:(j + 1) * 128], qth[c * 64:(c + 1) * 64, q0 + off:q0 + 512],
                               start=True, stop=True, R=[KB, QB], W=[STB])
                            pt, PTB = PTr.next()
                            ACT(AF.Exp, pt[:, off:512], st[:, off:512], R=[STB], W=[PTB], scale=0.125)
                            if j >= 4 * g:
                                fw.op("pool", "memset", pt[64:128, off:off + 64], 0.0, W=[PTB])
                            pts[i] = (pt, PTB, off)

                        def pv(i):
                            j, c = items[i]
                            pt, PTB, off = pts[i]
                            first = (j == 0); last = (j == 4 * (g + 1) - 1)
                            MM(Ops[c][0][:, off:512], vh[:, j, :], pt[:, off:512], start=first, stop=last, R=[VB, PTB], W=[Ops[c][1]], signal=last)
                            la, LAB = lacc[c]
                            eng = "dve" if c == 0 else "pool"
                            if first:
                                fw.op(eng, "tensor_copy", la[:, :], pt[:, :], R=[PTB], W=[LAB])
                            else:
                                fw.op(eng, "tensor_tensor", la[:, off:512], la[:, off:512], pt[:, off:512], op=ALU.add, R=[LAB, PTB], W=[LAB])
                            if last:
                                MM(Lps[c][0][:, :], ones32[:, :], la[:, :], start=True, stop=True, R=[CB, LAB], W=[Lps[c][1]])

                        qk(0); qk(1)
                        for i in range(nit):
                            if i + 2 < nit:
                                qk(i + 2)
                            pv(i)
                        epilogue(h, g, 512, q0)

                if sample:
                    vs16 = fw.sb("vs16", [NS, D], BF16); VSB = Buf()
                    fw.dma("sp", vs16[:, :], V16[T:T + NS, :], R=[B_V16[SG]], W=[VSB])
                    kcr = fw.tiles("kc", 2, [128, 16, 128], BF16)
                    vcr = fw.tiles("vc", 2, [128, 16, 128], BF16)
                    ktc = fw.sb("ktc", [128, PAST], BF16); KTCB = Buf()
                    qs = fw.sb("qs", [128, NS], BF16); ks_ = fw.sb("ks", [128, NS], BF16); QSB = Buf(); KSB = Buf()
                    ptsr = fw.tiles("pts", 2, [128, 17 * 16], BF16)
                    for h in range(8):
                        fw.dma("sp", qs[:, :], QT[h, :, T:T + NS], R=[B_QT[SG]], W=[QSB])
                        fw.dma("sp", ks_[:, :], KT[h, :, T:T + NS], R=[B_KT[SG]], W=[KSB])
                        for s in range(NSEQ):
                            kc, KCB = kcr.next(); vc, VCB = vcr.next()
                            fw.dma("pool", kc[:, :, :], ck[l, s, :, h * 128:(h + 1) * 128].rearrange("(j p) d -> p j d", p=128), W=[KCB])
                            fw.dma("pool", vc[:, :, :], cv[l, s, :, h * 128:(h + 1) * 128].rearrange("(j p) d -> p j d", p=128), W=[VCB])
                            for half in range(2):
                                pT, PTB_ = PTb.next()
                                pTv = pT[:].rearrange("p (k t) -> p k t", t=128)
                                for jj in range(8):
                                    TRP(pTv[:, jj, :], kc[:, half * 8 + jj, :], id16[:, :], R=[KCB, CB], W=[PTB_], signal=(jj == 7))
                                DVE("tensor_copy", ktc[:, half * 1024:(half + 1) * 1024], pT[:], R=[PTB_], W=[KTCB])
                            for c in range(2):
                                st, STB = STr.next()
                                for j in range(16):
                                    MM(st[:, j * 16:(j + 1) * 16], ktc[c * 64:(c + 1) * 64, j * 128:(j + 1) * 128], qs[c * 64:(c + 1) * 64, s * 16:(s + 1) * 16],
                                       start=True, stop=True, R=[KTCB, QSB], W=[STB], signal=False)
                                MM(st[0:NS, 256:272], ks_[c * 64:(c + 1) * 64, :], qs[c * 64:(c + 1) * 64, s * 16:(s + 1) * 16],
                                   start=True, stop=True, R=[KSB, QSB], W=[STB])
                                pt, PTB = ptsr.next()
                                ACT(AF.Exp, pt[:, 0:256], st[:, 0:256], R=[STB], W=[PTB], scale=0.125)
                                ACT(AF.Exp, pt[0:NS, 256:272], st[0:NS, 256:272], R=[STB], W=[PTB], scale=0.125)
                                DVE("tensor_scalar", pt[0:NS, 256:272], pt[0:NS, 256:272], ind32[:, s:s + 1], None, op0=ALU.mult, R=[PTB, CB], W=[PTB])
                                oc = Ops[c][0][:, s * 16:(s + 1) * 16]; lc = Lps[c][0][:, s * 16:(s + 1) * 16]
                                for j in range(16):
                                    MM(oc, vc[:, j, :], pt[:, j * 16:(j + 1) * 16], start=(j == 0), stop=False, R=[VCB, PTB], W=[Ops[c][1]], signal=False)
                                MM(oc, vs16[:, h * 128:(h + 1) * 128], pt[0:NS, 256:272], start=False, stop=True, R=[VSB, PTB], W=[Ops[c][1]])
                                for j in range(16):
                                    MM(lc, ones16[:, :], pt[:, j * 16:(j + 1) * 16], start=(j == 0), stop=False, R=[CB, PTB], W=[Lps[c][1]], signal=False)
                                MM(lc, ones16[0:NS, :], pt[0:NS, 256:272], start=False, stop=True, R=[CB, PTB], W=[Lps[c][1]])
                        epilogue(h, SG, NS, T)
                fw.flush()
            fw.stack = fw.gstack

        def phase_D1(l):
            P = PRM[l]
            with ExitStack() as ph:
                fw.stack = ph
                WB = Buf()
                Wg = fw.sb("Wg", [128, 8, 3072], BF16)
                Wo = fw.sb("Wo", [128, 8, D], BF16)
                src = w_in[l].rearrange("(k p) n -> p k n", p=128)
                for k in range(8):
                    fw.dma("pool", Wg[:, k, 0:1024], src[:, k, 5120:6144], W=[WB])
                    fw.dma("pool", Wg[:, k, 1024:3072], src[:, k, 6160:8208], W=[WB])
                fw.dma("pool", Wo[:], w_o[l].rearrange("(k p) n -> p k n", p=128), W=[WB])
                xnr = fw.tiles("xnT", 1, [128, 8, 512], BF16)
                oar = fw.tiles("oaT", 1, [128, 8, 512], BF16)
                ogr = fw.tiles("ogT", 1, [128, 8, 512], BF16)
                xtr = fw.tiles("xT", 1, [128, 8, 512], F32)
                gate = fw.sb("gate", [128, 24, 512], BF16); GB = [Buf() for _ in range(24)]
                sqr = fw.tiles("sq", 3, [128, 512], BF16)
                lnv = fw.sb("lnvD", [128, 512], F32); LNB = Buf()
                rsr = fw.tiles("rs", 2, [128, 512], F32)
                t1r = fw.tiles("t1", 2, [128, 512], F32)
                t2r = fw.tiles("t2", 2, [128, 512], F32)
                merged = fw.sb("merged", [128, 8, 512], BF16); MB = [Buf() for _ in range(8)]
                AB = [Buf() for _ in range(8)]
                hT = fw.sb("hT", [128, 8, 512], F32)
                PS = Rot([(fw.ps("PD%d" % i, [128, 512], F32), Buf()) for i in range(6)])
                PSQ = Rot([(fw.ps("PQ%d" % i, [128, 512], F32), Buf()) for i in range(2)])

                def group(g, N, c0):
                    xn, XNB = xnr.next(); oaT, OAB_ = oar.next(); ogT, OGB_ = ogr.next(); xT, XTB = xtr.next()
                    fw.dma("sp", xn[:, :, :N], XN.rearrange("k p t -> p k t")[:, :, c0:c0 + N], R=[B_XN[g]], W=[XNB])
                    fw.dma("sp", oaT[:, :, :N], OA.rearrange("k p t -> p k t")[:, :, c0:c0 + N], R=[B_OA[h][g] for h in range(8)], W=[OAB_])
                    fw.dma("sp", ogT[:, :, :N], OG.rearrange("k p t -> p k t")[:, :, c0:c0 + N], R=[B_OG[g]], W=[OGB_])
                    fw.dma("sp", xT[:, :, :N], XT.rearrange("k p t -> p k t")[:, :, c0:c0 + N], R=[B_XT[g]], W=[XTB])
                    for c in range(24):
                        ps, PB = PS.next()
                        for k in range(8):
                            MM(ps[:, :N], Wg[:, k, c * 128:(c + 1) * 128], xn[:, k, :N], start=(k == 0), stop=(k == 7), R=[WB, XNB], W=[PB], signal=(k == 7))
                        ACT(AF.Silu if c < 8 else AF.Sigmoid, gate[:, c, :N], ps[:, :N], R=[PB], W=[GB[c]])
                    for hh in range(4):
                        pq, PQB = PSQ.next()
                        for e in range(2):
                            sq, SQB = sqr.next()
                            ACT(AF.Square, sq[:, :N], ogT[:, hh * 2 + e, :N], R=[OGB_], W=[SQB])
                            MM(pq[:, :N], ones16[:, :], sq[:, :N], start=(e == 0), stop=(e == 1), R=[CB, SQB], W=[PQB], signal=(e == 1))
                        rs, RSB = rsr.next()
                        rstd_from_ssq(rs[:, :N], pq[:, :N], 256, lnv[:, :N], R=[PQB], W=[RSB], WL=LNB)
                        for e in range(2):
                            c = hh * 2 + e
                            t1, T1B = t1r.next(); t2, T2B = t2r.next()
                            DVE("scalar_tensor_tensor", t1[:, :N], ogT[:, c, :N], P["wgla"][:, e:e + 1], rs[:, :N], op0=ALU.mult, op1=ALU.mult,
                                R=[OGB_, CB, RSB], W=[T1B])
                            DVE("tensor_tensor", t1[:, :N], t1[:, :N], gate[:, c, :N], op=ALU.mult, R=[T1B, GB[c]], W=[T1B])
                            DVE("tensor_tensor", t1[:, :N], t1[:, :N], gate[:, 16 + c, :N], op=ALU.mult, R=[T1B, GB[16 + c]], W=[T1B])
                            fw.op("pool", "tensor_tensor", t2[:, :N], oaT[:, c, :N], gate[:, 8 + c, :N], op=ALU.mult, R=[OAB_, GB[8 + c]], W=[T2B])
                            DVE("tensor_tensor", merged[:, c, :N], t1[:, :N], t2[:, :N], op=ALU.add, R=[T1B, T2B], W=[MB[c]])
                    pq, PQB = PSQ.next()
                    for c in range(8):
                        ps, PB = PS.next()
                        for k in range(8):
                            MM(ps[:, :N], Wo[:, k, c * 128:(c + 1) * 128], merged[:, k, :N], start=(k == 0), stop=(k == 7), R=[WB, MB[k]], W=[PB], signal=(k == 7))
                        fw.op("act", "copy", hT[:, c, :N], ps[:, :N], R=[PB], W=[AB[c]])
                        sq, SQB = sqr.next()
                        ACT(AF.Square, sq[:, :N], ps[:, :N], R=[PB], W=[SQB])
                        MM(pq[:, :N], ones16[:, :], sq[:, :N], start=(c == 0), stop=(c == 7), R=[CB, SQB], W=[PQB], signal=(c == 7))
                    rs, RSB = rsr.next()
                    rstd_from_ssq(rs[:, :N], pq[:, :N], D, lnv[:, :N], R=[PQB], W=[RSB], WL=LNB)
                    for c in range(8):
                        t1, T1B = t1r.next()
                        DVE("scalar_tensor_tensor", t1[:, :N], hT[:, c, :N], P["wpost"][:, c:c + 1], rs[:, :N], op0=ALU.mult, op1=ALU.mult,
                            R=[AB[c], CB, RSB], W=[T1B])
                        fw.op("pool", "tensor_tensor", hT[:, c, :N], t1[:, :N], xT[:, c, :N], op=ALU.add, R=[T1B, XTB], W=[AB[c]])
                    fw.dma("sp", HT.rearrange("k p t -> p k t")[:, :, c0:c0 + N], hT[:, :, :N], R=AB, W=[B_HT[g]])

                for g in range(G):
                    group(g, 512, g * 512)
                if sample:
                    group(SG, NS, T)
                fw.flush()
            fw.stack = fw.gstack

        def phase_D2(l):
            P = PRM[l]
            last = (l == DEPTH - 1)
            NG = 256
            with ExitStack() as ph:
                fw.stack = ph
                WB = Buf()
                Wu = fw.sb("Wu", [128, 8, 2 * DFF], BF16)
                Wd = fw.sb("Wd", [128, NJ, D], BF16)
                srcu = w_up[l].rearrange("(k p) n -> p k n", p=128)
                import os
                for k in range(8 if not os.environ.get("D2SKIPW") else 0):
                    for hh in range(2):
                        fw.dma("pool", Wu[:, k, hh * DFF:(hh + 1) * DFF], srcu[:, k, hh * DFF:(hh + 1) * DFF], W=[WB])
                srcd = w_down[l].rearrange("(j p) n -> p j n", p=128)
                for j in range(0, NJ if not os.environ.get("D2SKIPW") else 0, 2):
                    fw.dma("pool", Wd[:, j:j + 2, :], srcd[:, j:j + 2, :], W=[WB])
                hTr = fw.tiles("hT", 1, [128, 8, NG], F32)
                hn = fw.sb("hn", [128, 8, NG], BF16); HNB = Buf()
                sqr = fw.tiles("sq", 3, [128, NG], BF16)
                lnv = fw.sb("lnvE", [128, NG], F32); LNB = Buf()
                rsr = fw.tiles("rs", 2, [128, NG], F32)
                gbr = fw.tiles("gb", 2, [128, NG + 8], F32)
                gprev = fw.sb("gprev", [128, NJ, 2], F32); GPB = [Buf() for _ in range(NJ)]
                accr = fw.tiles("acc", 2, [128, NG], F32)
                ger = fw.tiles("gel", 2, [128, NG], F32)
                hid = fw.sb("hid", [128, NJ, NG], BF16); HB = [Buf() for _ in range(NJ)]
                oT = fw.sb("oT", [128, 8, NG], F32); OTB = [Buf() for _ in range(8)]
                t1r = fw.tiles("t1", 2, [128, NG], F32)
                ytr = fw.tiles("yt", 1, [128, D], F32)
                scs = fw.sb("scs", [128, NSEQ, 2, NJ], F32); SCB = Buf()
                PS = Rot([(fw.ps("UA%d" % i, [128, 512], F32), Buf()) for i in range(5)])
                PSQ = Rot([(fw.ps("UB%d" % i, [128, 512], F32), Buf()) for i in range(1)])
                PY = Rot([(fw.ps("UC%d" % i, [128, 1024], F32), Buf()) for i in range(1)])
                fw.op("dve", "memset", gprev[:], 0.0, W=GPB)

                def group(g512, N, c0, nseq, ydst_fn, xm_buf):
                    Lq = N // nseq
                    hT, HTB = hTr.next()
                    fw.dma("sp", hT[:, :, :N], HT.rearrange("k p t -> p k t")[:, :, c0:c0 + N], R=[B_HT[g512]], W=[HTB])
                    _cp(19.5)
                    pq, PQB = PSQ.next()
                    for c in range(8):
                        sq, SQB = sqr.next()
                        ACT(AF.Square, sq[:, :N], hT[:, c, :N], R=[HTB], W=[SQB])
                        MM(pq[:, :N], ones16[:, :], sq[:, :N], start=(c == 0), stop=(c == 7), R=[CB, SQB], W=[PQB], signal=(c == 7))
                    _cp(19.7)
                    rs, RSB = rsr.next()
                    rstd_from_ssq(rs[:, :N], pq[:, :N], D, lnv[:, :N], R=[PQB], W=[RSB], WL=LNB)
                    _cp(19.8)
                    for c in range(8):
                        DVE("scalar_tensor_tensor", hn[:, c, :N], hT[:, c, :N], P["wpreffn"][:, c:c + 1], rs[:, :N], op0=ALU.mult, op1=ALU.mult,
                            R=[HTB, CB, RSB], W=[HNB])
                    _cp(20)
                    if nseq > 1:
                        for s_ in range(NSEQ):
                            for t_ in range(2):
                                fw.dma("sp", scs[:, s_, t_, :], sc[l, s_, t_].rearrange("(c p) -> p c", p=128), W=[SCB], **SLOW)
                    for j in range(NJ):
                        pu, PUB = PS.next()
                        for k in range(8):
                            MM(pu[:, :N], Wu[:, k, j * 128:(j + 1) * 128], hn[:, k, :N], start=(k == 0), stop=(k == 7), R=[WB, HNB], W=[PUB], signal=(k == 7))
                        pg, PGB = PS.next()
                        for k in range(8):
                            MM(pg[:, :N], Wu[:, k, DFF + j * 128:DFF + (j + 1) * 128], hn[:, k, :N], start=(k == 0), stop=(k == 7), R=[WB, HNB], W=[PGB], signal=(k == 7))
                        gb, GBB = gbr.next()
                        gbv = gb[:, :nseq * (Lq + 2)].rearrange("p (s t) -> p s t", t=Lq + 2)
                        if nseq == 1:
                            fw.op("act", "copy", gbv[:, :, 0:2], gprev[:, j:j + 1, :], R=[GPB[j]], W=[GBB])
                        else:
                            fw.op("act", "copy", gbv[:, :, 0:2], scs[:, :, :, j], R=[SCB], W=[GBB])
                        fw.op("act", "copy", gbv[:, :, 2:Lq + 2], pg[:, :N].rearrange("p (s t) -> p s t", t=Lq), R=[PGB], W=[GBB])
                        acc, ACB = accr.next()
                        av = acc[:, :N].rearrange("p (s t) -> p s t", t=Lq)
                        DVE("tensor_scalar", av, gbv[:, :, 2:Lq + 2], P["cw"][:, 2, j:j + 1], P["cb"][:, j:j + 1], op0=ALU.mult, op1=ALU.add, R=[GBB, CB], W=[ACB])
                        DVE("scalar_tensor_tensor", av, gbv[:, :, 1:Lq + 1], P["cw"][:, 1, j:j + 1], av, op0=ALU.mult, op1=ALU.add, R=[GBB, CB, ACB], W=[ACB])
                        DVE("scalar_tensor_tensor", av, gbv[:, :, 0:Lq], P["cw"][:, 0, j:j + 1], av, op0=ALU.mult, op1=ALU.add, R=[GBB, CB, ACB], W=[ACB])
                        if nseq == 1:
                            fw.op("pool", "tensor_copy", gprev[:, j:j + 1, :], gbv[:, :, Lq:Lq + 2], R=[GBB], W=[GPB[j]])
                        else:
                            fw.op("pool", "tensor_copy", scs[:, :, :, j], gbv[:, :, Lq:Lq + 2], R=[GBB], W=[SCB])
                        gel, GLB = ger.next()
                        ACT(AF.Gelu_apprx_tanh, gel[:, :N], acc[:, :N], R=[ACB], W=[GLB])
                        DVE("tensor_tensor", hid[:, j, :N], gel[:, :N], pu[:, :N], op=ALU.mult, R=[GLB, PUB], W=[HB[j]])
                        _cp(21)
                    if nseq > 1:
                        for s_ in range(NSEQ):
                            for t_ in range(2):
                                fw.dma("sp", conv_s[l, s_, t_].rearrange("(c p) -> p c", p=128), scs[:, s_, t_, :], R=[SCB], W=[B_OUT], **SLOW)
                    _cp(22)
                    pq, PQB = PSQ.next()
                    for c in range(8):
                        ps, PB = PS.next()
                        for j in range(NJ):
                            MM(ps[:, :N], Wd[:, j, c * 128:(c + 1) * 128], hid[:, j, :N], start=(j == 0), stop=(j == NJ - 1), R=[WB, HB[j]], W=[PB], signal=(j == NJ - 1))
                        fw.op("act", "copy", oT[:, c, :N], ps[:, :N], R=[PB], W=[OTB[c]])
                        sq, SQB = sqr.next()
                        ACT(AF.Square, sq[:, :N], ps[:, :N], R=[PB], W=[SQB])
                        MM(pq[:, :N], ones16[:, :], sq[:, :N], start=(c == 0), stop=(c == 7), R=[CB, SQB], W=[PQB], signal=(c == 7))
                    rs, RSB = rsr.next()
                    rstd_from_ssq(rs[:, :N], pq[:, :N], D, lnv[:, :N], R=[PQB], W=[RSB], WL=LNB)
                    for c in range(8):
                        t1, T1B = t1r.next()
                        DVE("scalar_tensor_tensor", t1[:, :N], oT[:, c, :N], P["wpostffn"][:, c:c + 1], rs[:, :N], op0=ALU.mult, op1=ALU.mult,
                            R=[OTB[c], CB, RSB], W=[T1B])
                        fw.op("pool", "tensor_tensor", oT[:, c, :N], t1[:, :N], hT[:, c, :N], op=ALU.add, R=[T1B, HTB], W=[OTB[c]])
                    _cp(23)
                    for tt in range((N + 127) // 128):
                        n = min(128, N - tt * 128)
                        py, PYB = PY.next()
                        for c in range(8):
                            TRP(py[:n, c * 128:(c + 1) * 128], oT[:, c, tt * 128:tt * 128 + n], id32[:, :], R=[OTB[c], CB], W=[PYB], signal=(c == 7))
                        yt, YTB = ytr.next()
                        fw.op("act", "copy", yt[:n, :], py[:n, :], R=[PYB], W=[YTB])
                        dst, DB = ydst_fn(tt, n)
                        fw.dma("sp", dst, yt[:n, :], R=[YTB], W=[DB])

                for g in range(T // NG):
                    c0 = g * NG

                    def ydst(tt, n, c0=c0, g=g):
                        if last:
                            return y_p[c0 + tt * 128:c0 + tt * 128 + n, :], B_OUT
                        return XMID[c0 + tt * 128:c0 + tt * 128 + n, :], B_XMID[g]
                    try:
                        group(c0 // 512, NG, c0, 1, ydst, None)
                    except _Stop:
                        pass
                import os
                for t_ in range(2):
                    if os.environ.get("NOCONVP"):
                        break
                    fw.dma("sp", conv_p[l, t_].rearrange("(c p) -> p c", p=128), gprev[:, :, t_], R=GPB, W=[B_OUT], **SLOW)
                if sample:
                    def ydst_s(tt, n):
                        if last:
                            return y_s[0:n, :], B_OUT
                        return XMID[T:T + n, :], B_XMID[2 * G]
                    group(SG, NS, T, NSEQ, ydst_s, None)
                fw.flush()
            fw.stack = fw.gstack

        for l in range(layers):
            if "A" in phases:
                phase_A(l)
            if "B" in phases:
                phase_B(l)
            if "C" in phases:
                phase_D1(l)
            if "D" in phases:
                phase_D2(l)
        fw.flush(final=True)
        n_ops = fw.n_ops
    return nc, n_ops


def _consts():
    c = {}
    c["c_ident"] = np.eye(128, dtype=np.float32)
    s = np.arange(128)
    c["c_U"] = (s[:, None] <= s[None, :]).astype(np.float32)
    c["c_L"] = (s[:, None] > s[None, :]).astype(np.float32)
    r = np.arange(NS)
    same = (r[:, None] // LS) == (r[None, :] // LS)
    c["c_Us"] = (same & (r[:, None] <= r[None, :])).astype(np.float32)
    c["c_Ls"] = (same & (r[:, None] > r[None, :])).astype(np.float32)
    c["c_ind"] = ((r[:, None] // LS) == np.arange(NSEQ)[None, :]).astype(np.float32)
    half = 8
    inv = (np.float32(500000.0) ** (-np.arange(half, dtype=np.float32) * np.float32(2.0) / np.float32(16))).astype(np.float32)
    pos = np.arange(T_FULL, dtype=np.float32)
    ang = (pos[:, None] * inv[None, :]).astype(np.float32)
    c["c_cos"] = np.cos(ang).astype(np.float32); c["c_sin"] = np.sin(ang).astype(np.float32)
    pos_s = (PAST + (r % LS)).astype(np.float32)
    ang_s = (pos_s[:, None] * inv[None, :]).astype(np.float32)
    c["c_cos_s"] = np.cos(ang_s).astype(np.float32); c["c_sin_s"] = np.sin(ang_s).astype(np.float32)
    return c


def make_in_maps(inp, n_cores=8):
    f = lambda a: np.ascontiguousarray(np.asarray(a, dtype=np.float32))
    shared = {
        "w_in": f(inp["w_in"]), "w_gk2": f(inp["w_gk2"]), "b_gk2": f(inp["b_gk2"]),
        "lq1": f(inp["lambda_q1"]), "lk1": f(inp["lambda_k1"]), "lq2": f(inp["lambda_q2"]), "lk2": f(inp["lambda_k2"]),
        "da_norm_w": f(inp["da_norm_w"]), "gla_norm_w": f(inp["gla_norm_w"]), "w_o": f(inp["w_o"]),
        "pre_mix_w": f(inp["pre_mix_w"]), "post_mix_w": f(inp["post_mix_w"]),
        "pre_ffn_w": f(inp["pre_ffn_w"]), "post_ffn_w": f(inp["post_ffn_w"]),
        "w_up": f(inp["w_up"]), "conv_w": f(inp["conv_w"]), "conv_b": f(inp["conv_b"]), "w_down": f(inp["w_down"]),
    }
    shared.update(_consts())
    maps = []
    for b in range(n_cores):
        m = dict(shared)
        m["x_p"] = f(inp["x_prompt"][b])
        m["x_s"] = f(inp["x_sample"][4 * b:4 * b + 4]).reshape(NS, D)
        m["ck"] = f(inp["cache_k"][:, 4 * b:4 * b + 4]).reshape(DEPTH, NSEQ, PAST, D)
        m["cv"] = f(inp["cache_v"][:, 4 * b:4 * b + 4]).reshape(DEPTH, NSEQ, PAST, D)
        m["sg"] = f(inp["state_gla"][:, 4 * b:4 * b + 4])
        m["sc"] = f(inp["state_conv"][:, 4 * b:4 * b + 4])
        maps.append(m)
    return maps


_NC_CACHE = {}


def kernel(**inputs):
    n = 8
    if "nc" not in _NC_CACHE:
        _NC_CACHE["nc"] = build()[0]
    nc = _NC_CACHE["nc"]
    in_maps = make_in_maps(inputs, n)
    res = run_bass_kernel_spmd(nc, in_maps, core_ids=list(range(n))).results
    B = n
    y_prompt = np.stack([res[b]["y_p"] for b in range(B)]).astype(np.float32)
    y_sample = np.concatenate([res[b]["y_s"].reshape(NSEQ, LS, D) for b in range(B)], axis=0).astype(np.float32)
    k_prompt = np.stack([res[b]["k_p"].reshape(DEPTH, T_FULL, 8, 128) for b in range(B)], axis=1).astype(np.float32)
    v_prompt = np.stack([res[b]["v_p"].reshape(DEPTH, T_FULL, 8, 128) for b in range(B)], axis=1).astype(np.float32)
    gla_prompt = np.stack([res[b]["gla_p"] for b in range(B)], axis=1).astype(np.float32)
    conv_prompt = np.stack([res[b]["conv_p"] for b in range(B)], axis=1).astype(np.float32)
    k_sample = np.concatenate([res[b]["k_s"].reshape(DEPTH, NSEQ, LS, 8, 128) for b in range(B)], axis=1).astype(np.float32)
    v_sample = np.concatenate([res[b]["v_s"].reshape(DEPTH, NSEQ, LS, 8, 128) for b in range(B)], axis=1).astype(np.float32)
    gla_sample = np.concatenate([res[b]["gla_s"] for b in range(B)], axis=1).astype(np.float32)
    conv_sample = np.concatenate([res[b]["conv_s"] for b in range(B)], axis=1).astype(np.float32)
    return (y_prompt, y_sample, k_prompt, v_prompt, gla_prompt, conv_prompt, k_sample, v_sample, gla_sample, conv_sample)
```

```python
import math
import numpy as np
from contextlib import ExitStack
import concourse.bass as bass
import concourse.mybir as mybir
from concourse.bass_utils import run_bass_kernel_spmd

F32 = mybir.dt.float32
BF16 = mybir.dt.bfloat16
AF = mybir.ActivationFunctionType
ALU = mybir.AluOpType

D = 1024
T_FULL = 8192
NS = 64
NSEQ = 4
LS = 16
PAST = 2048
DIN = 8208
DFF = 2816
NJ = DFF // 128
EPS = 1e-6
DEPTH = 2


class Buf:
    __slots__ = ("w", "r")

    def __init__(self):
        self.w = None
        self.r = []


class Eng:
    def __init__(self, name, key):
        self.name = name
        self.key = key
        self.cnt = 0
        self.seen = {}
        self.prog = []
        self.dsems = []
        self.dnext = 0
        self.pending = None


class Rot:
    def __init__(self, items):
        self.items = items
        self.i = 0

    def next(self):
        it = self.items[self.i % len(self.items)]
        self.i += 1
        return it


class FW:
    def __init__(self, nc, stack, n_dma_sems=24):
        self.nc = nc
        self.gstack = stack
        self.stack = stack
        self.sems = {}
        self.semval = {}
        self.engs = {}
        for name in ("pe", "act", "dve", "pool", "sp"):
            key = None
            if name != "sp":
                key = "c_" + name
                self.sems[key] = stack.enter_context(nc.semaphore(key))
                self.semval[key] = 0
            self.engs[name] = Eng(name, key)
        for name in ("sp", "pool"):
            E = self.engs[name]
            for i in range(n_dma_sems if name == "sp" else 8):
                key = "d_%s_%d" % (name, i)
                self.sems[key] = stack.enter_context(nc.semaphore(key))
                self.semval[key] = 0
                E.dsems.append(key)
        self.n_ops = 0
        self.uid = 0

    def sb(self, name, shape, dt):
        self.uid += 1
        return self.stack.enter_context(self.nc.sbuf_tensor("%s_%d" % (name, self.uid), list(shape), dt))

    def ps(self, name, shape, dt=F32):
        self.uid += 1
        return self.stack.enter_context(self.nc.psum_tensor("%s_%d" % (name, self.uid), list(shape), dt))

    def tiles(self, name, n, shape, dt):
        return Rot([(self.sb(name + str(i), shape, dt), Buf()) for i in range(n)])

    def _deps(self, E, reads, writes):
        need = {}
        pe = E.name == "pe"

        def add(t, same_ok):
            if t is None:
                return
            k, v = t
            if k == E.key and not same_ok:
                return
            if need.get(k, 0) < v:
                need[k] = v

        for b in reads:
            add(b.w, not pe)
        for b in writes:
            add(b.w, not pe)
            for t in b.r:
                add(t, False)
        for k, v in need.items():
            if E.seen.get(k, 0) < v:
                E.seen[k] = v
                if k.startswith("c_"):
                    X = self.engs[k[2:]]
                    if v > X.cnt:
                        assert X.pending is not None and v == X.cnt + 1, (k, v, X.cnt)
                        X.pending["sig"] = True
                        X.pending = None
                        X.cnt += 1
                        self.semval[k] = X.cnt
                E.prog.append(lambda h, sem=self.sems[k], v=v: h.wait_ge(sem, v))

    def op(self, en, meth, *args, R=(), W=(), signal=True, **kw):
        E = self.engs[en]
        self._deps(E, R, W)
        ent = {"meth": meth, "args": args, "kw": kw, "sig": signal, "sem": self.sems[E.key]}
        E.prog.append(ent)
        if signal:
            E.cnt += 1
            self.semval[E.key] = E.cnt
            E.pending = None
            t = (E.key, E.cnt)
        else:
            E.pending = ent
            t = (E.key, E.cnt + 1)
        for b in R:
            b.r.append(t)
        for b in W:
            b.w = t
            b.r = []
        self.n_ops += 1

    def dma(self, en, out, in_, R=(), W=(), **kw):
        E = self.engs[en]
        self._deps(E, R, W)
        key = E.dsems[E.dnext % len(E.dsems)]
        E.dnext += 1
        prev = self.semval[key]
        if E.seen.get(key, 0) < prev:
            E.seen[key] = prev
            E.prog.append(lambda h, sem=self.sems[key], v=prev: h.wait_ge(sem, v))
        self.semval[key] = prev + 16
        sem = self.sems[key]
        E.prog.append(lambda h, out=out, in_=in_, sem=sem, kw=kw: h.dma_start(out=out, in_=in_, **kw).then_inc(sem, 16))
        t = (key, prev + 16)
        for b in R:
            b.r.append(t)
        for b in W:
            b.w = t
            b.r = []
        self.n_ops += 1

    def flush(self, final=False):
        for E in self.engs.values():
            if E.name == "sp" or final is False:
                for k, v in self.semval.items():
                    if v > 0 and k != E.key and E.seen.get(k, 0) < v:
                        E.seen[k] = v
                        E.prog.append(lambda h, sem=self.sems[k], v=v: h.wait_ge(sem, v))
        progs = {n: E.prog for n, E in self.engs.items()}
        for E in self.engs.values():
            E.prog = []
            E.pending = None

        def run(h, prog):
            for f in prog:
                if isinstance(f, dict):
                    ins = getattr(h, f["meth"])(*f["args"], **f["kw"])
                    if f["sig"]:
                        ins.then_inc(f["sem"], 1)
                else:
                    f(h)
        with self.nc.Block() as block:
            @block.tensor
            def _(h):
                run(h, progs["pe"])

            @block.scalar
            def _(h):
                run(h, progs["act"])

            @block.vector
            def _(h):
                run(h, progs["dve"])

            @block.gpsimd
            def _(h):
                run(h, progs["pool"])

            @block.sync
            def _(h):
                run(h, progs["sp"])


class _Stop(Exception):
    pass


DBG_STOP = None
DBG_OUT = False


def _cp(i):
    if DBG_STOP is not None and i >= DBG_STOP:
        raise _Stop()


def build(T=T_FULL, layers=DEPTH, phases="ABCD", sample=True):
    nc = bass.Bass("TRN2", target_bir_lowering=False)
    TT = T + NS
    NT = T // 128
    G = T // 512
    SG = G

    def din(name, shape):
        return nc.dram_tensor(name, list(shape), F32, kind="ExternalInput").ap()

    def dout(name, shape):
        return nc.dram_tensor(name, list(shape), F32, kind="ExternalOutput").ap()

    def dscr(name, shape, dt):
        return nc.dram_tensor(name, list(shape), dt, kind=("ExternalOutput" if (DBG_OUT and name in ("OA", "OG", "HT")) else "Internal")).ap()

    x_p = din("x_p", [T_FULL, D])
    x_s = din("x_s", [NS, D])
    ck = din("ck", [DEPTH, NSEQ, PAST, D])
    cv = din("cv", [DEPTH, NSEQ, PAST, D])
    sg = din("sg", [DEPTH, NSEQ, 4, 128, 256])
    sc = din("sc", [DEPTH, NSEQ, 2, DFF])
    w_in = din("w_in", [DEPTH, D, DIN])
    w_gk2 = din("w_gk2", [DEPTH, 16, 512])
    b_gk2 = din("b_gk2", [DEPTH, 512])
    lq1 = din("lq1", [DEPTH, 64]); lk1 = din("lk1", [DEPTH, 64])
    lq2 = din("lq2", [DEPTH, 64]); lk2 = din("lk2", [DEPTH, 64])
    da_norm_w = din("da_norm_w", [DEPTH, 128])
    gla_norm_w = din("gla_norm_w", [DEPTH, 256])
    w_o = din("w_o", [DEPTH, D, D])
    pre_mix_w = din("pre_mix_w", [DEPTH, D]); post_mix_w = din("post_mix_w", [DEPTH, D])
    pre_ffn_w = din("pre_ffn_w", [DEPTH, D]); post_ffn_w = din("post_ffn_w", [DEPTH, D])
    w_up = din("w_up", [DEPTH, D, 2 * DFF])
    conv_w = din("conv_w", [DEPTH, 3, DFF]); conv_b = din("conv_b", [DEPTH, DFF])
    w_down = din("w_down", [DEPTH, DFF, D])
    c_ident = din("c_ident", [128, 128])
    c_U = din("c_U", [128, 128]); c_L = din("c_L", [128, 128])
    c_Us = din("c_Us", [NS, NS]); c_Ls = din("c_Ls", [NS, NS])
    c_ind = din("c_ind", [NS, NSEQ])
    c_cos = din("c_cos", [T_FULL, 8]); c_sin = din("c_sin", [T_FULL, 8])
    c_cos_s = din("c_cos_s", [NS, 8]); c_sin_s = din("c_sin_s", [NS, 8])

    y_p = dout("y_p", [T_FULL, D]); y_s = dout("y_s", [NS, D])
    k_p = dout("k_p", [DEPTH, T_FULL, D]); v_p = dout("v_p", [DEPTH, T_FULL, D])
    gla_p = dout("gla_p", [DEPTH, 4, 128, 256]); conv_p = dout("conv_p", [DEPTH, 2, DFF])
    k_s = dout("k_s", [DEPTH, NS, D]); v_s = dout("v_s", [DEPTH, NS, D])
    gla_s = dout("gla_s", [DEPTH, NSEQ, 4, 128, 256]); conv_s = dout("conv_s", [DEPTH, NSEQ, 2, DFF])

    XT = dscr("XT", [8, 128, TT], F32)
    XN = dscr("XN", [8, 128, TT], BF16)
    QT = dscr("QT", [8, 128, TT], BF16)
    KT = dscr("KT", [8, 128, TT], BF16)
    V16 = dscr("V16", [TT, D], BF16)
    OA = dscr("OA", [8, 128, TT], BF16)
    OG = dscr("OG", [8, 128, TT], BF16)
    HT = dscr("HT", [8, 128, TT], F32)
    XMID = dscr("XMID", [TT, D], F32)

    def bufs(n):
        return [Buf() for _ in range(n)]

    B_XT = bufs(G + 1); B_XN = bufs(G + 1); B_QT = bufs(G + 1); B_KT = bufs(G + 1)
    B_V16 = bufs(G + 1); B_OG = bufs(G + 1); B_HT = bufs(G + 1)
    B_OA = [bufs(G + 1) for _ in range(8)]
    B_XMID = bufs(2 * G + 1)
    B_OUT = Buf()

    with ExitStack() as gst:
        fw = FW(nc, gst)

        def ACT(func, out, in_, R, W, **kw):
            fw.op("act", "activation", out=out, in_=in_, func=func, R=R, W=W, **kw)

        def DVE(meth, *a, R, W, **kw):
            fw.op("dve", meth, *a, R=R, W=W, **kw)

        def MM(out, lhsT, rhs, start, stop, R, W, signal=True):
            fw.op("pe", "matmul", out, lhsT, rhs, start=start, stop=stop, R=R, W=W, signal=signal)

        def TRP(out, in_, ident, R, W, signal=True):
            fw.op("pe", "transpose", out, in_, ident, R=R, W=W, signal=signal)

        SLOW = dict(allow_slow_non_contiguous=True)

        CB = Buf()
        id32 = fw.sb("id32", [128, 128], F32); id16 = fw.sb("id16", [128, 128], BF16)
        ones32 = fw.sb("ones32", [128, 128], F32); ones16 = fw.sb("ones16", [128, 128], BF16)
        U32 = fw.sb("U32", [128, 128], F32); L32 = fw.sb("L32", [128, 128], F32)
        Us32 = fw.sb("Us32", [NS, NS], F32); Ls32 = fw.sb("Ls32", [NS, NS], F32)
        ind32 = fw.sb("ind32", [NS, NSEQ], F32)
        cosT = fw.sb("cosT", [128, T_FULL // 128, 8], F32); sinT = fw.sb("sinT", [128, T_FULL // 128, 8], F32)
        cosS = fw.sb("cosS", [NS, 8], F32); sinS = fw.sb("sinS", [NS, 8], F32)
        fw.dma("sp", id32[:], c_ident, W=[CB]); fw.dma("pool", id16[:], c_ident, W=[CB])
        fw.dma("sp", U32[:], c_U, W=[CB]); fw.dma("sp", L32[:], c_L, W=[CB])
        fw.dma("sp", Us32[:], c_Us, W=[CB]); fw.dma("sp", Ls32[:], c_Ls, W=[CB])
        fw.dma("sp", ind32[:], c_ind, W=[CB])
        fw.dma("sp", cosT[:], c_cos.rearrange("(j p) e -> p j e", p=128), W=[CB])
        fw.dma("sp", sinT[:], c_sin.rearrange("(j p) e -> p j e", p=128), W=[CB])
        fw.dma("sp", cosS[:], c_cos_s, W=[CB]); fw.dma("sp", sinS[:], c_sin_s, W=[CB])
        fw.op("dve", "memset", ones32[:], 1.0, W=[CB]); fw.op("dve", "memset", ones16[:], 1.0, W=[CB])

        PRM = []
        for l in range(DEPTH):
            p = {}
            for nm, src in (("wpre", pre_mix_w), ("wpost", post_mix_w), ("wpreffn", pre_ffn_w), ("wpostffn", post_ffn_w)):
                p[nm] = fw.sb(nm, [128, 8], F32)
                fw.dma("sp", p[nm][:], src[l].rearrange("(k p) -> p k", p=128), W=[CB], **SLOW)
            p["wda"] = fw.sb("wda", [128, 1], F32)
            fw.dma("sp", p["wda"][:], da_norm_w[l].rearrange("(p o) -> p o", o=1), W=[CB], **SLOW)
            lam_init = 0.8 - 0.6 * math.exp(-0.3 * l)
            fw.op("dve", "tensor_scalar", p["wda"][:], p["wda"][:], 1.0 - lam_init, None, op0=ALU.mult, R=[CB], W=[CB])
            p["wgla"] = fw.sb("wgla", [128, 2], F32)
            fw.dma("sp", p["wgla"][:], gla_norm_w[l].rearrange("(e p) -> p e", p=128), W=[CB], **SLOW)
            p["cw"] = fw.sb("cw", [128, 3, NJ], F32)
            for j3 in range(3):
                fw.dma("sp", p["cw"][:, j3, :], conv_w[l, j3].rearrange("(c p) -> p c", p=128), W=[CB], **SLOW)
            p["cb"] = fw.sb("cb", [128, NJ], F32)
            fw.dma("sp", p["cb"][:], conv_b[l].rearrange("(c p) -> p c", p=128), W=[CB], **SLOW)
            p["bg2"] = fw.sb("bg2", [128, 512], F32)
            fw.dma("sp", p["bg2"][:], b_gk2[l].partition_broadcast(128), W=[CB])
            p["Wg2"] = fw.sb("Wg2", [16, 512], BF16)
            fw.dma("pool", p["Wg2"][:], w_gk2[l], W=[CB])
            lt = [fw.sb("lt%d" % i, [128, 64], F32) for i in range(4)]
            for t_, src in zip(lt, (lq1, lk1, lq2, lk2)):
                fw.dma("sp", t_[:], src[l].partition_broadcast(128), W=[CB])
            s1 = fw.sb("s1", [128, 2], F32)
            fw.op("dve", "memset", s1[:], 0.0, W=[CB])
            fw.op("dve", "tensor_tensor", lt[0][:], lt[0][:], lt[1][:], op=ALU.mult, R=[CB], W=[CB])
            fw.op("dve", "tensor_tensor", lt[2][:], lt[2][:], lt[3][:], op=ALU.mult, R=[CB], W=[CB])
            fw.op("dve", "reduce_sum", s1[:, 0:1], lt[0][:], axis=mybir.AxisListType.X, R=[CB], W=[CB])
            fw.op("dve", "reduce_sum", s1[:, 1:2], lt[2][:], axis=mybir.AxisListType.X, R=[CB], W=[CB])
            ACT(AF.Exp, s1[:], s1[:], R=[CB], W=[CB])
            p["neglam"] = fw.sb("neglam", [128, 1], F32)
            fw.op("dve", "tensor_tensor", p["neglam"][:], s1[:, 1:2], s1[:, 0:1], op=ALU.subtract, R=[CB], W=[CB])
            fw.op("dve", "tensor_scalar", p["neglam"][:], p["neglam"][:], -lam_init, None, op0=ALU.add, R=[CB], W=[CB])
            PRM.append(p)
        fw.flush()

        def rstd_from_ssq(out, ssq_ap, n_feat, lnv, R, W, WL):
            ACT(AF.Ln, lnv, ssq_ap, R=R, W=[WL], scale=1.0 / n_feat, bias=epsc[:lnv.shape[0], 0:1])
            ACT(AF.Exp, out, lnv, R=[WL], W=W, scale=-0.5)

        epsc = fw.sb("epsc", [128, 1], F32)
        fw.op("dve", "memset", epsc[:], EPS, W=[CB])

        def phase_A(l):
            P = PRM[l]
            with ExitStack() as ph:
                fw.stack = ph
                WB = Buf()
                Wa = fw.sb("Wa", [128, 8, 5120], BF16)
                Wlr = fw.sb("Wlr", [128, 8, 16], BF16)
                src = w_in[l].rearrange("(k p) n -> p k n", p=128)
                for k in range(8):
                    for hh in range(2):
                        fw.dma("pool", Wa[:, k, hh * 2560:(hh + 1) * 2560], src[:, k, hh * 2560:(hh + 1) * 2560], W=[WB])
                fw.dma("pool", Wlr[:], src[:, :, 6144:6160], W=[WB])
                xrot = fw.tiles("x", 2, [128, D], F32)
                junk = fw.sb("junk", [128, D], BF16); JB = Buf()
                ssq = fw.sb("ssq", [128, 1], F32); lnv = fw.sb("lnv", [128, 1], F32); rstd = fw.sb("rstd", [128, 1], F32)
                SB_ = Buf(); LB = Buf(); RB = Buf()
                xn = fw.sb("xn", [128, D], BF16); XNB = Buf()
                xTrot = fw.tiles("xTs", 2, [128, 8, 128], F32)
                xnTrot = fw.tiles("xnT", 2, [128, 8, 128], BF16)
                qrot16 = fw.sb("qrot16", [128, D], BF16); QRB = Buf()
                krot32 = fw.sb("krot32", [128, D], F32); KRB = Buf()
                krot16 = fw.sb("krot16", [128, D], BF16); KR16B = Buf()
                v32 = fw.sb("v32", [128, D], F32); V32B = Buf()
                v16 = fw.sb("v16", [128, D], BF16); V16B = Buf()
                QTs = fw.sb("QTs", [128, 8, 512], BF16); QTSB = Buf()
                KTs = fw.sb("KTs", [128, 8, 512], BF16); KTSB = Buf()
                OGs = fw.sb("OGs", [128, 8, 512], BF16); OGSB = Buf()
                rt = [fw.sb("rt%d" % i, [128, 16, 8], F32) for i in range(4)]; RTB = [Buf() for _ in range(4)]
                lrT = fw.sb("lrT", [16, 128], BF16); LRB = Buf()
                ge = fw.sb("ge", [128, 512], F32); GEB = Buf()
                g32 = fw.sb("g32", [128, 512], F32); G32B = Buf()
                E1 = fw.sb("E1", [128, 512], F32); E2 = fw.sb("E2", [128, 512], F32); E3 = fw.sb("E3", [128, 512], F32)
                E1B = Buf(); E2B = Buf(); E3B = Buf()
                ebl = fw.sb("ebl", [128, 16], F32); EBLB = Buf()
                qt16 = fw.sb("qt16", [128, 512], BF16); kt16 = fw.sb("kt16", [128, 512], BF16)
                kh16 = fw.sb("kh16", [128, 512], BF16); khm = fw.sb("khm", [128, 512], BF16)
                QT16B = Buf(); KT16B = Buf(); KH16B = Buf(); KHMB = Buf()
                vg16 = fw.sb("vg16", [128, D], BF16); VG16B = Buf()
                qkT = fw.sb("qkT", [128, 8, 128], BF16); QKTB = Buf()
                AT16 = fw.sb("AT16", [128, 4, 128], BF16); ATB = Buf()
                ogi = fw.sb("ogi", [128, 8, 128], F32); OGIB = Buf()
                S32 = fw.sb("S32", [128, 4, 256], F32); S32B = Buf()
                S16 = fw.sb("S16", [128, 4, 256], BF16); S16B = Buf()
                P2 = Rot([(fw.ps("P2_%d" % i, [128, 1024], F32), Buf()) for i in range(3)])
                P1 = Rot([(fw.ps("P1_%d" % i, [128, 512], F32), Buf()) for i in range(1)])
                P1b = Rot([(fw.ps("P1b_%d" % i, [128, 1024], BF16), Buf()) for i in range(1)])

                fw.op("dve", "memset", S32[:], 0.0, W=[S32B])
                fw.op("dve", "memset", S16[:], 0.0, W=[S16B])

                def tile(xsrc, XSB, n, nseq, t0, g, ti, cos_ap, sin_ap, kdst, vdst, last_in_group, ncols):
                    Lq = n // nseq
                    Uc = U32 if nseq == 1 else Us32
                    Lc = L32 if nseq == 1 else Ls32
                    xt, XB = xrot.next()
                    fw.dma("sp", xt[:n], xsrc, R=[XSB] if XSB is not None else [], W=[XB])
                    fw.op("dve", "memset", ssq[:n], 0.0, W=[SB_])
                    ACT(AF.Square, junk[:n], xt[:n], R=[XB], W=[JB, SB_], accum_out=ssq[:n, 0:1])
                    rstd_from_ssq(rstd[:n], ssq[:n], D, lnv[:n], R=[SB_], W=[RB], WL=LB)
                    DVE("tensor_scalar", xn[:n], xt[:n], rstd[:n, 0:1], None, op0=ALU.mult, R=[XB, RB], W=[XNB])
                    _cp(1)
                    pX, PXB = P2.next()
                    pXv = pX[:].rearrange("p (k t) -> p k t", t=128)
                    for k in range(8):
                        TRP(pXv[:, k, :n], xt[:n, k * 128:(k + 1) * 128], id32[:n, :n], R=[XB, CB], W=[PXB], signal=(k == 7))
                    xTs, XTSB = xTrot.next()
                    ACT(AF.Copy, xTs[:, :, :n], pXv[:, :, :n], R=[PXB], W=[XTSB])
                    fw.dma("sp", XT.rearrange("k p t -> p k t")[:, :, t0:t0 + n], xTs[:, :, :n], R=[XTSB], W=[B_XT[g]])
                    _cp(2)
                    pN, PNB = P1b.next()
                    pNv = pN[:].rearrange("p (k t) -> p k t", t=128)
                    for k in range(8):
                        TRP(pNv[:, k, :n], xn[:n, k * 128:(k + 1) * 128], id16[:n, :n], R=[XNB, CB], W=[PNB], signal=(k == 7))
                    xnT, XNTB = xnTrot.next()
                    DVE("tensor_tensor", xnT[:, :, :n], pNv[:, 0:8, :n], P["wpre"][:, :].unsqueeze(2).to_broadcast([128, 8, n]),
                        op=ALU.mult, R=[PNB, CB], W=[XNTB])
                    fw.dma("sp", XN.rearrange("k p t -> p k t")[:, :, t0:t0 + n], xnT[:, :, :n], R=[XNTB], W=[B_XN[g]])

                    _cp(3)
                    def proj(c0):
                        pt, PB = P2.next()
                        for half in range(2):
                            for k in range(8):
                                MM(pt[:n, half * 512:(half + 1) * 512], xnT[:, k, :n], Wa[:, k, c0 + half * 512:c0 + (half + 1) * 512],
                                   start=(k == 0), stop=(k == 7), R=[XNTB, WB], W=[PB], signal=(k == 7 and half == 1))
                        return pt, PB

                    def rope(pt, PB, out, OB):
                        v = pt[:n].rearrange("p (g d) -> p g d", d=64)
                        o = out[:n].rearrange("p (g d) -> p g d", d=64)
                        cb_ = cos_ap.unsqueeze(1).to_broadcast([n, 16, 8])
                        sb_ = sin_ap.unsqueeze(1).to_broadcast([n, 16, 8])
                        DVE("tensor_tensor", rt[0][:n], v[:, :, 0:8], cb_, op=ALU.mult, R=[PB, CB], W=[RTB[0]])
                        DVE("tensor_tensor", rt[1][:n], v[:, :, 8:16], sb_, op=ALU.mult, R=[PB, CB], W=[RTB[1]])
                        DVE("tensor_tensor", rt[2][:n], v[:, :, 8:16], cb_, op=ALU.mult, R=[PB, CB], W=[RTB[2]])
                        DVE("tensor_tensor", rt[3][:n], v[:, :, 0:8], sb_, op=ALU.mult, R=[PB, CB], W=[RTB[3]])
                        DVE("tensor_copy", o[:, :, 16:64], v[:, :, 16:64], R=[PB], W=[OB])
                        DVE("tensor_tensor", o[:, :, 0:8], rt[0][:n], rt[1][:n], op=ALU.subtract, R=[RTB[0], RTB[1]], W=[OB])
                        DVE("tensor_tensor", o[:, :, 8:16], rt[2][:n], rt[3][:n], op=ALU.add, R=[RTB[2], RTB[3]], W=[OB])

                    pq, PQB = proj(0)
                    _cp(3.2)
                    rope(pq, PQB, qrot16, QRB)
                    _cp(3.5)
                    pT, PTB = P1b.next()
                    pTv = pT[:].rearrange("p (k t) -> p k t", t=128)
                    for h in range(8):
                        TRP(pTv[:, h, :n], qrot16[:n, h * 128:(h + 1) * 128], id16[:n, :n], R=[QRB, CB], W=[PTB], signal=(h == 7))
                    _cp(3.8)
                    DVE("tensor_copy", QTs[:, :, ti * 128:ti * 128 + n], pTv[:, 0:8, :n], R=[PTB], W=[QTSB])
                    _cp(4)
                    pk, PKB = proj(1024)
                    rope(pk, PKB, krot32, KRB)
                    fw.dma("sp", kdst, krot32[:n], R=[KRB], W=[B_OUT])
                    fw.op("act", "copy", krot16[:n], krot32[:n], R=[KRB], W=[KR16B])
                    pT, PTB = P1b.next()
                    pTv = pT[:].rearrange("p (k t) -> p k t", t=128)
                    for h in range(8):
                        TRP(pTv[:, h, :n], krot16[:n, h * 128:(h + 1) * 128], id16[:n, :n], R=[KR16B, CB], W=[PTB], signal=(h == 7))
                    DVE("tensor_copy", KTs[:, :, ti * 128:ti * 128 + n], pTv[:, 0:8, :n], R=[PTB], W=[KTSB])
                    _cp(5)
                    pv, PVB = proj(2048)
                    _cp(5.2)
                    fw.op("act", "copy", v32[:n], pv[:n], R=[PVB], W=[V32B])
                    _cp(5.4)
                    DVE("tensor_copy", v16[:n], v32[:n], R=[V32B], W=[V16B])
                    _cp(5.6)
                    fw.dma("sp", vdst, v32[:n], R=[V32B], W=[B_OUT])
                    _cp(5.8)
                    fw.dma("sp", V16[t0:t0 + n, :], v16[:n], R=[V16B], W=[B_V16[g]])
                    _cp(6)
                    if last_in_group:
                        fw.dma("sp", QT.rearrange("h p t -> p h t")[:, :, g * 512:g * 512 + ncols], QTs[:, :, :ncols], R=[QTSB], W=[B_QT[g]])
                        fw.dma("sp", KT.rearrange("h p t -> p h t")[:, :, g * 512:g * 512 + ncols], KTs[:, :, :ncols], R=[KTSB], W=[B_KT[g]])

                    _cp(7)
                    pl, PLB = P1.next()
                    for k in range(8):
                        MM(pl[0:16, :n], Wlr[:, k, :], xnT[:, k, :n], start=(k == 0), stop=(k == 7), R=[XNTB, WB], W=[PLB], signal=(k == 7))
                    fw.op("act", "copy", lrT[:, :n], pl[0:16, :n], R=[PLB], W=[LRB])
                    pqk, PQKB = proj(3072)
                    pvg, PVGB = proj(4096)
                    fw.op("act", "copy", vg16[:n], pvg[:n], R=[PVGB], W=[VG16B])
                    pg, PGB = P1.next()
                    MM(pg[:n, :], lrT[:, :n], P["Wg2"][:, :], start=True, stop=True, R=[LRB, CB], W=[PGB])
                    DVE("tensor_tensor", ge[:n], pg[:n, :], P["bg2"][:n], op=ALU.add, R=[PGB, CB], W=[GEB])
                    ACT(AF.Exp, ge[:n], ge[:n], R=[GEB], W=[GEB], scale=-1.0)
                    ACT(AF.Ln, ge[:n], ge[:n], R=[GEB], W=[GEB], bias=1.0)
                    DVE("tensor_scalar", g32[:n], ge[:n], -1.0 / 16.0, None, op0=ALU.mult, R=[GEB], W=[G32B])
                    pb, PBB = P2.next()
                    MM(pb[:n, 0:512], Uc[:n, :n], g32[:n, :], start=True, stop=True, R=[CB, G32B], W=[PBB], signal=False)
                    MM(pb[:n, 512:1024], Lc[:n, :n], g32[:n, :], start=True, stop=True, R=[CB, G32B], W=[PBB])
                    ACT(AF.Exp, E1[:n], pb[:n, 0:512], R=[PBB], W=[E1B])
                    ACT(AF.Exp, E2[:n], pb[:n, 0:512], R=[PBB], W=[E2B], scale=-1.0)
                    ACT(AF.Exp, E3[:n], pb[:n, 512:1024], R=[PBB], W=[E3B])
                    pbl, PBLB = P1.next()
                    for h in range(4):
                        MM(pbl[:, h * nseq:(h + 1) * nseq], g32[:n, h * 128:(h + 1) * 128],
                           (ones32[:n, 0:1] if nseq == 1 else ind32[:n, :nseq]), start=True, stop=True, R=[G32B, CB], W=[PBLB], signal=(h == 3))
                    ACT(AF.Exp, ebl[:, :4 * nseq], pbl[:, :4 * nseq], R=[PBLB], W=[EBLB])
                    DVE("scalar_tensor_tensor", qt16[:n], pqk[:n, 0:512], 128.0 ** -0.5, E1[:n], op0=ALU.mult, op1=ALU.mult, R=[PQKB, E1B], W=[QT16B])
                    DVE("tensor_tensor", kt16[:n], pqk[:n, 512:1024], E2[:n], op=ALU.mult, R=[PQKB, E2B], W=[KT16B])
                    DVE("tensor_tensor", kh16[:n], pqk[:n, 512:1024], E3[:n], op=ALU.mult, R=[PQKB, E3B], W=[KH16B])
                    _cp(8)
                    pT2, PT2B = P1b.next()
                    pT2v = pT2[:].rearrange("p (k t) -> p k t", t=128)
                    for h in range(4):
                        TRP(pT2v[:, h, :n], qt16[:n, h * 128:(h + 1) * 128], id16[:n, :n], R=[QT16B, CB], W=[PT2B], signal=False)
                        TRP(pT2v[:, 4 + h, :n], kt16[:n, h * 128:(h + 1) * 128], id16[:n, :n], R=[KT16B, CB], W=[PT2B], signal=(h == 3))
                    DVE("tensor_copy", qkT[:, :, :n], pT2v[:, 0:8, :n], R=[PT2B], W=[QKTB])
                    pA, PAB = P1.next()
                    pAv = pA[:].rearrange("p (h t) -> p h t", t=128)
                    for h in range(4):
                        MM(pAv[:n, h, :n], qkT[:, 4 + h, :n], qkT[:, h, :n], start=True, stop=True, R=[QKTB], W=[PAB], signal=(h == 3))
                    DVE("tensor_tensor", AT16[:n, :, :n], pAv[:n, :, :n], Uc[:n, :n].unsqueeze(1).to_broadcast([n, 4, n]),
                        op=ALU.mult, R=[PAB, CB], W=[ATB])
                    _cp(9)
                    pO, POB = P2.next()
                    pOv = pO[:].rearrange("p (c t) -> p c t", t=128)
                    for h in range(4):
                        for e in range(2):
                            MM(pOv[:, h * 2 + e, :n], vg16[:n, h * 256 + e * 128:h * 256 + (e + 1) * 128], AT16[:n, h, :n],
                               start=True, stop=True, R=[VG16B, ATB], W=[POB], signal=(h == 3 and e == 1))
                    pI, PIB = P2.next()
                    pIv = pI[:].rearrange("p (c t) -> p c t", t=128)
                    for s in range(nseq):
                        if nseq > 1:
                            fw.dma("sp", S32[:], sg[l, s].rearrange("h p v -> p h v"), W=[S32B])
                            fw.op("act", "copy", S16[:], S32[:], R=[S32B], W=[S16B])
                        for h in range(4):
                            for e in range(2):
                                MM(pIv[:, h * 2 + e, s * Lq:(s + 1) * Lq], S16[:, h, e * 128:(e + 1) * 128], qkT[:, h, s * Lq:(s + 1) * Lq],
                                   start=True, stop=True, R=[S16B, QKTB], W=[PIB], signal=(h == 3 and e == 1))
                        if nseq > 1:
                            DVE("tensor_scalar", khm[:n], kh16[:n], ind32[:n, s:s + 1], None, op0=ALU.mult, R=[KH16B, CB], W=[KHMB])
                            khs, KHSB = khm, KHMB
                        else:
                            khs, KHSB = kh16, KH16B
                        for hp in range(2):
                            pS, PSB = P1.next()
                            pSv = pS[:].rearrange("p (h v) -> p h v", v=256)
                            for hh in range(2):
                                h = hp * 2 + hh
                                MM(pSv[:, hh, :], khs[:n, h * 128:(h + 1) * 128], vg16[:n, h * 256:(h + 1) * 256], start=True, stop=True,
                                   R=[KHSB, VG16B], W=[PSB], signal=(hh == 1))
                            for hh in range(2):
                                h = hp * 2 + hh
                                DVE("scalar_tensor_tensor", S32[:, h, :], S32[:, h, :], ebl[:, h * nseq + s:h * nseq + s + 1], pSv[:, hh, :],
                                    op0=ALU.mult, op1=ALU.add, R=[S32B, EBLB, PSB], W=[S32B])
                        fw.op("act", "copy", S16[:], S32[:], R=[S32B], W=[S16B])
                        if nseq > 1:
                            fw.dma("sp", gla_s[l, s].rearrange("h p v -> p h v"), S32[:], R=[S32B], W=[B_OUT])
                    _cp(11)
                    fw.op("act", "copy", ogi[:, :, :n], pIv[:, :, :n], R=[PIB], W=[OGIB])
                    DVE("tensor_tensor", OGs[:, :, ti * 128:ti * 128 + n], pOv[:, :, :n], ogi[:, :, :n], op=ALU.add, R=[POB, OGIB], W=[OGSB])
                    if last_in_group:
                        fw.dma("sp", OG.rearrange("c p t -> p c t")[:, :, g * 512:g * 512 + ncols], OGs[:, :, :ncols], R=[OGSB], W=[B_OG[g]])

                for i in range(NT):
                    g, ti = divmod(i, 4)
                    if l == 0:
                        xsrc, XSB = x_p[i * 128:(i + 1) * 128, :], None
                    else:
                        xsrc, XSB = XMID[i * 128:(i + 1) * 128, :], B_XMID[i // 2]
                    try:
                        tile(xsrc, XSB, 128, 1, i * 128, g, ti, cosT[:, i, :], sinT[:, i, :],
                             k_p[l, i * 128:(i + 1) * 128, :], v_p[l, i * 128:(i + 1) * 128, :], ti == 3, 512)
                    except _Stop:
                        pass
                fw.dma("sp", gla_p[l].rearrange("h p v -> p h v"), S32[:], R=[S32B], W=[B_OUT])
                if sample:
                    if l == 0:
                        xsrc, XSB = x_s[:, :], None
                    else:
                        xsrc, XSB = XMID[T:T + NS, :], B_XMID[2 * G]
                    tile(xsrc, XSB, NS, NSEQ, T, SG, 0, cosS[:, :], sinS[:, :], k_s[l], v_s[l], True, NS)
                fw.flush()
            fw.stack = fw.gstack

        def phase_B(l):
            P = PRM[l]
            with ExitStack() as ph:
                fw.stack = ph
                KTh = fw.tiles("KTh", 2, [128, T], BF16)
                QTh = fw.tiles("QTh", 2, [128, T], BF16)
                Vh = fw.tiles("Vh", 2, [128, NT, 128], BF16)
                PTr = fw.tiles("PT", 3, [128, 1024], BF16)
                r1 = fw.sb("r1", [128, 512], F32); r2 = fw.sb("r2", [128, 512], F32)
                o1 = fw.sb("o1", [128, 512], F32); o2 = fw.sb("o2", [128, 512], F32)
                oa = fw.sb("oa", [128, 512], F32); sq = fw.sb("sq", [128, 512], F32)
                lnv = fw.sb("lnvB", [128, 512], F32); rstd = fw.sb("rstdB", [128, 512], F32)
                R1B = Buf(); R2B = Buf(); O1B = Buf(); O2B = Buf(); OAB = Buf(); SQB = Buf(); LNB = Buf(); RSB = Buf()
                oanr = fw.tiles("oan", 2, [128, 512], BF16)
                lacc = [(fw.sb("lacc%d" % i, [128, 512], F32), Buf()) for i in range(2)]
                STr = Rot([(fw.ps("ST%d" % i, [128, 1024], F32), Buf()) for i in range(2)])
                PTb = Rot([(fw.ps("PTb%d" % i, [128, 1024], BF16), Buf()) for i in range(1)])
                Ops = [(fw.ps("O%d" % i, [128, 512], F32), Buf()) for i in range(2)]
                Lbank = fw.ps("Lb", [128, 512], F32); LBB = Buf()

                def epilogue(h, g, N, c0, prompt):
                    if prompt:
                        MM(Lbank[:, :], ones32[:, :], lacc[0][0][:, :], start=True, stop=True, R=[CB, lacc[0][1]], W=[LBB])
                        DVE("reciprocal", r1[:, :N], Lbank[:, :N], R=[LBB], W=[R1B])
                        MM(Lbank[:, :], ones32[:, :], lacc[1][0][:, :], start=True, stop=True, R=[CB, lacc[1][1]], W=[LBB])
                        DVE("reciprocal", r2[:, :N], Lbank[:, :N], R=[LBB], W=[R2B])
                    else:
                        DVE("reciprocal", r1[:, :N], Lbank[:, 0:N], R=[LBB], W=[R1B])
                        DVE("reciprocal", r2[:, :N], Lbank[:, 64:64 + N], R=[LBB], W=[R2B])
                    DVE("tensor_tensor", o1[:, :N], Ops[0][0][:, :N], r1[:, :N], op=ALU.mult, R=[Ops[0][1], R1B], W=[O1B])
                    DVE("tensor_tensor", o2[:, :N], Ops[1][0][:, :N], r2[:, :N], op=ALU.mult, R=[Ops[1][1], R2B], W=[O2B])
                    DVE("scalar_tensor_tensor", oa[:, :N], o2[:, :N], P["neglam"][:, 0:1], o1[:, :N], op0=ALU.mult, op1=ALU.add,
                        R=[O1B, O2B, CB], W=[OAB])
                    ACT(AF.Square, sq[:, :N], oa[:, :N], R=[OAB], W=[SQB])
                    st, STB = STr.next()
                    MM(st[:, :N], ones32[:, :], sq[:, :N], start=True, stop=True, R=[CB, SQB], W=[STB])
                    rstd_from_ssq(rstd[:, :N], st[:, :N], 128, lnv[:, :N], R=[STB], W=[RSB], WL=LNB)
                    oan, OANB = oanr.next()
                    DVE("scalar_tensor_tensor", oan[:, :N], oa[:, :N], P["wda"][:, 0:1], rstd[:, :N], op0=ALU.mult, op1=ALU.mult,
                        R=[OAB, CB, RSB], W=[OANB])
                    fw.dma("sp", OA[h, :, c0:c0 + N], oan[:, :N], R=[OANB], W=[B_OA[h][g]])

                for h in range(8):
                    kth, KB = KTh.next(); qth, QB = QTh.next(); vh, VB = Vh.next()
                    fw.dma("sp", kth[:, :], KT[h, :, 0:T], R=B_KT[:G], W=[KB])
                    fw.dma("sp", qth[:, :], QT[h, :, 0:T], R=B_QT[:G], W=[QB])
                    fw.dma("sp", vh[:, :, :], V16[0:T, h * 128:(h + 1) * 128].rearrange("(j p) d -> p j d", p=128), R=B_V16[:G], W=[VB])
                    for g in range(G):
                        q0 = g * 512
                        nkt = 4 * (g + 1)
                        pts = [None] * nkt

                        def qk(j):
                            off = max(j - 4 * g, 0) * 128
                            st, STB = STr.next()
                            for c in range(2):
                                MM(st[:, c * 512 + off:(c + 1) * 512], kth[c * 64:(c + 1) * 64, j * 128:(j + 1) * 128],
                                   qth[c * 64:(c + 1) * 64, q0 + off:q0 + 512], start=True, stop=True, R=[KB, QB], W=[STB], signal=(c == 1))
                            pt, PTB = PTr.next()
                            stv = st[:].rearrange("p (c n) -> p c n", n=512)
                            ptv = pt[:].rearrange("p (c n) -> p c n", n=512)
                            ACT(AF.Exp, ptv[:, :, off:512], stv[:, :, off:512], R=[STB], W=[PTB], scale=0.125)
                            if j >= 4 * g:
                                fw.op("pool", "memset", ptv[64:128, :, off:off + 64], 0.0, W=[PTB])
                            pts[j] = (pt, PTB, off)

                        def pv(j):
                            pt, PTB, off = pts[j]
                            first = (j == 0); last = (j == nkt - 1)
                            for c in range(2):
                                MM(Ops[c][0][:, off:512], vh[:, j, :], pt[:, c * 512 + off:(c + 1) * 512], start=first, stop=last,
                                   R=[VB, PTB], W=[Ops[c][1]], signal=last)
                            for c in range(2):
                                la, LAB = lacc[c]
                                eng = "dve" if c == 0 else "pool"
                                if first:
                                    fw.op(eng, "tensor_copy", la[:, :], pt[:, c * 512:(c + 1) * 512], R=[PTB], W=[LAB])
                                else:
                                    fw.op(eng, "tensor_tensor", la[:, off:512], la[:, off:512], pt[:, c * 512 + off:(c + 1) * 512], op=ALU.add,
                                          R=[LAB, PTB], W=[LAB])

                        qk(0)
                        for j in range(nkt):
                            if j + 1 < nkt:
                                qk(j + 1)
                            pv(j)
                        epilogue(h, g, 512, q0, True)

                if sample:
                    vs16 = fw.sb("vs16", [NS, D], BF16); VSB = Buf()
                    fw.dma("sp", vs16[:, :], V16[T:T + NS, :], R=[B_V16[SG]], W=[VSB])
                    kcr = fw.tiles("kc", 2, [128, 16, 128], BF16)
                    vcr = fw.tiles("vc", 2, [128, 16, 128], BF16)
                    ktc = fw.sb("ktc", [128, PAST], BF16); KTCB = Buf()
                    qs = fw.sb("qs", [128, NS], BF16); ks_ = fw.sb("ks", [128, NS], BF16); QSB = Buf(); KSB = Buf()
                    ptsr = fw.tiles("pts", 2, [128, 17 * 16], BF16)
                    for h in range(8):
                        fw.dma("sp", qs[:, :], QT[h, :, T:T + NS], R=[B_QT[SG]], W=[QSB])
                        fw.dma("sp", ks_[:, :], KT[h, :, T:T + NS], R=[B_KT[SG]], W=[KSB])
                        for s in range(NSEQ):
                            kc, KCB = kcr.next(); vc, VCB = vcr.next()
                            fw.dma("pool", kc[:, :, :], ck[l, s, :, h * 128:(h + 1) * 128].rearrange("(j p) d -> p j d", p=128), W=[KCB])
                            fw.dma("pool", vc[:, :, :], cv[l, s, :, h * 128:(h + 1) * 128].rearrange("(j p) d -> p j d", p=128), W=[VCB])
                            for half in range(2):
                                pT, PTB_ = PTb.next()
                                pTv = pT[:].rearrange("p (k t) -> p k t", t=128)
                                for jj in range(8):
                                    TRP(pTv[:, jj, :], kc[:, half * 8 + jj, :], id16[:, :], R=[KCB, CB], W=[PTB_], signal=(jj == 7))
                                DVE("tensor_copy", ktc[:, half * 1024:(half + 1) * 1024], pT[:], R=[PTB_], W=[KTCB])
                            for c in range(2):
                                st, STB = STr.next()
                                for j in range(16):
                                    MM(st[:, j * 16:(j + 1) * 16], ktc[c * 64:(c + 1) * 64, j * 128:(j + 1) * 128], qs[c * 64:(c + 1) * 64, s * 16:(s + 1) * 16],
                                       start=True, stop=True, R=[KTCB, QSB], W=[STB], signal=False)
                                MM(st[0:NS, 256:272], ks_[c * 64:(c + 1) * 64, :], qs[c * 64:(c + 1) * 64, s * 16:(s + 1) * 16],
                                   start=True, stop=True, R=[KSB, QSB], W=[STB])
                                pt, PTB = ptsr.next()
                                ACT(AF.Exp, pt[:, 0:256], st[:, 0:256], R=[STB], W=[PTB], scale=0.125)
                                ACT(AF.Exp, pt[0:NS, 256:272], st[0:NS, 256:272], R=[STB], W=[PTB], scale=0.125)
                                DVE("tensor_scalar", pt[0:NS, 256:272], pt[0:NS, 256:272], ind32[:, s:s + 1], None, op0=ALU.mult, R=[PTB, CB], W=[PTB])
                                oc = Ops[c][0][:, s * 16:(s + 1) * 16]; lc = Lbank[:, c * 64 + s * 16:c * 64 + (s + 1) * 16]
                                for j in range(16):
                                    MM(oc, vc[:, j, :], pt[:, j * 16:(j + 1) * 16], start=(j == 0), stop=False, R=[VCB, PTB], W=[Ops[c][1]], signal=False)
                                MM(oc, vs16[:, h * 128:(h + 1) * 128], pt[0:NS, 256:272], start=False, stop=True, R=[VSB, PTB], W=[Ops[c][1]])
                                for j in range(16):
                                    MM(lc, ones16[:, :], pt[:, j * 16:(j + 1) * 16], start=(j == 0), stop=False, R=[CB, PTB], W=[LBB], signal=False)
                                MM(lc, ones16[0:NS, :], pt[0:NS, 256:272], start=False, stop=True, R=[CB, PTB], W=[LBB])
                        epilogue(h, SG, NS, T, False)
                fw.flush()
            fw.stack = fw.gstack

        def phase_D1(l):
            P = PRM[l]
            with ExitStack() as ph:
                fw.stack = ph
                WB = Buf()
                Wg = fw.sb("Wg", [128, 8, 3072], BF16)
                Wo = fw.sb("Wo", [128, 8, D], BF16)
                src = w_in[l].rearrange("(k p) n -> p k n", p=128)
                for k in range(8):
                    fw.dma("pool", Wg[:, k, 0:1024], src[:, k, 5120:6144], W=[WB])
                    fw.dma("pool", Wg[:, k, 1024:3072], src[:, k, 6160:8208], W=[WB])
                fw.dma("pool", Wo[:], w_o[l].rearrange("(k p) n -> p k n", p=128), W=[WB])
                xnr = fw.tiles("xnT", 1, [128, 8, 512], BF16)
                oar = fw.tiles("oaT", 1, [128, 8, 512], BF16)
                ogr = fw.tiles("ogT", 1, [128, 8, 512], BF16)
                xtr = fw.tiles("xT", 1, [128, 8, 512], F32)
                gate = fw.sb("gate", [128, 24, 512], BF16); GB = [Buf() for _ in range(24)]
                sqr = fw.tiles("sq", 3, [128, 512], BF16)
                lnv = fw.sb("lnvD", [128, 512], F32); LNB = Buf()
                rsr = fw.tiles("rs", 2, [128, 512], F32)
                t1r = fw.tiles("t1", 2, [128, 512], F32)
                t2r = fw.tiles("t2", 2, [128, 512], F32)
                merged = fw.sb("merged", [128, 8, 512], BF16); MB = [Buf() for _ in range(8)]
                AB = [Buf() for _ in range(8)]
                hT = fw.sb("hT", [128, 8, 512], F32)
                PS = Rot([(fw.ps("PD%d" % i, [128, 512], F32), Buf()) for i in range(6)])
                PSQ = Rot([(fw.ps("PQ%d" % i, [128, 512], F32), Buf()) for i in range(2)])

                def group(g, N, c0):
                    xn, XNB = xnr.next(); oaT, OAB_ = oar.next(); ogT, OGB_ = ogr.next(); xT, XTB = xtr.next()
                    fw.dma("sp", xn[:, :, :N], XN.rearrange("k p t -> p k t")[:, :, c0:c0 + N], R=[B_XN[g]], W=[XNB])
                    fw.dma("sp", oaT[:, :, :N], OA.rearrange("k p t -> p k t")[:, :, c0:c0 + N], R=[B_OA[h][g] for h in range(8)], W=[OAB_])
                    fw.dma("sp", ogT[:, :, :N], OG.rearrange("k p t -> p k t")[:, :, c0:c0 + N], R=[B_OG[g]], W=[OGB_])
                    fw.dma("sp", xT[:, :, :N], XT.rearrange("k p t -> p k t")[:, :, c0:c0 + N], R=[B_XT[g]], W=[XTB])
                    for c in range(24):
                        ps, PB = PS.next()
                        for k in range(8):
                            MM(ps[:, :N], Wg[:, k, c * 128:(c + 1) * 128], xn[:, k, :N], start=(k == 0), stop=(k == 7), R=[WB, XNB], W=[PB], signal=(k == 7))
                        ACT(AF.Silu if c < 8 else AF.Sigmoid, gate[:, c, :N], ps[:, :N], R=[PB], W=[GB[c]])
                    for hh in range(4):
                        pq, PQB = PSQ.next()
                        for e in range(2):
                            sq, SQB = sqr.next()
                            ACT(AF.Square, sq[:, :N], ogT[:, hh * 2 + e, :N], R=[OGB_], W=[SQB])
                            MM(pq[:, :N], ones16[:, :], sq[:, :N], start=(e == 0), stop=(e == 1), R=[CB, SQB], W=[PQB], signal=(e == 1))
                        rs, RSB = rsr.next()
                        rstd_from_ssq(rs[:, :N], pq[:, :N], 256, lnv[:, :N], R=[PQB], W=[RSB], WL=LNB)
                        for e in range(2):
                            c = hh * 2 + e
                            t1, T1B = t1r.next(); t2, T2B = t2r.next()
                            DVE("scalar_tensor_tensor", t1[:, :N], ogT[:, c, :N], P["wgla"][:, e:e + 1], rs[:, :N], op0=ALU.mult, op1=ALU.mult,
                                R=[OGB_, CB, RSB], W=[T1B])
                            DVE("tensor_tensor", t1[:, :N], t1[:, :N], gate[:, c, :N], op=ALU.mult, R=[T1B, GB[c]], W=[T1B])
                            DVE("tensor_tensor", t1[:, :N], t1[:, :N], gate[:, 16 + c, :N], op=ALU.mult, R=[T1B, GB[16 + c]], W=[T1B])
                            fw.op("pool", "tensor_tensor", t2[:, :N], oaT[:, c, :N], gate[:, 8 + c, :N], op=ALU.mult, R=[OAB_, GB[8 + c]], W=[T2B])
                            DVE("tensor_tensor", merged[:, c, :N], t1[:, :N], t2[:, :N], op=ALU.add, R=[T1B, T2B], W=[MB[c]])
                    pq, PQB = PSQ.next()
                    for c in range(8):
                        ps, PB = PS.next()
                        for k in range(8):
                            MM(ps[:, :N], Wo[:, k, c * 128:(c + 1) * 128], merged[:, k, :N], start=(k == 0), stop=(k == 7), R=[WB, MB[k]], W=[PB], signal=(k == 7))
                        fw.op("act", "copy", hT[:, c, :N], ps[:, :N], R=[PB], W=[AB[c]])
                        sq, SQB = sqr.next()
                        ACT(AF.Square, sq[:, :N], ps[:, :N], R=[PB], W=[SQB])
                        MM(pq[:, :N], ones16[:, :], sq[:, :N], start=(c == 0), stop=(c == 7), R=[CB, SQB], W=[PQB], signal=(c == 7))
                    rs, RSB = rsr.next()
                    rstd_from_ssq(rs[:, :N], pq[:, :N], D, lnv[:, :N], R=[PQB], W=[RSB], WL=LNB)
                    for c in range(8):
                        t1, T1B = t1r.next()
                        DVE("scalar_tensor_tensor", t1[:, :N], hT[:, c, :N], P["wpost"][:, c:c + 1], rs[:, :N], op0=ALU.mult, op1=ALU.mult,
                            R=[AB[c], CB, RSB], W=[T1B])
                        fw.op("pool", "tensor_tensor", hT[:, c, :N], t1[:, :N], xT[:, c, :N], op=ALU.add, R=[T1B, XTB], W=[AB[c]])
                    fw.dma("sp", HT.rearrange("k p t -> p k t")[:, :, c0:c0 + N], hT[:, :, :N], R=AB, W=[B_HT[g]])

                for g in range(G):
                    group(g, 512, g * 512)
                if sample:
                    group(SG, NS, T)
                fw.flush()
            fw.stack = fw.gstack

        def phase_D2(l):
            P = PRM[l]
            last = (l == DEPTH - 1)
            NG = 256
            with ExitStack() as ph:
                fw.stack = ph
                WB = Buf()
                Wu = fw.sb("Wu", [128, 8, 2 * DFF], BF16)
                Wd = fw.sb("Wd", [128, NJ, D], BF16)
                srcu = w_up[l].rearrange("(k p) n -> p k n", p=128)
                import os
                for k in range(8 if not os.environ.get("D2SKIPW") else 0):
                    for hh in range(2):
                        fw.dma("pool", Wu[:, k, hh * DFF:(hh + 1) * DFF], srcu[:, k, hh * DFF:(hh + 1) * DFF], W=[WB])
                srcd = w_down[l].rearrange("(j p) n -> p j n", p=128)
                for j in range(0, NJ if not os.environ.get("D2SKIPW") else 0, 2):
                    fw.dma("pool", Wd[:, j:j + 2, :], srcd[:, j:j + 2, :], W=[WB])
                hTr = fw.tiles("hT", 1, [128, 8, NG], F32)
                hn = fw.sb("hn", [128, 8, NG], BF16); HNB = Buf()
                sqr = fw.tiles("sq", 3, [128, NG], BF16)
                lnv = fw.sb("lnvE", [128, NG], F32); LNB = Buf()
                rsr = fw.tiles("rs", 2, [128, NG], F32)
                gbr = fw.tiles("gb", 2, [128, NG + 8], F32)
                gprev = fw.sb("gprev", [128, NJ, 2], F32); GPB = [Buf() for _ in range(NJ)]
                accr = fw.tiles("acc", 2, [128, NG], F32)
                ger = fw.tiles("gel", 2, [128, NG], F32)
                hid = fw.sb("hid", [128, NJ, NG], BF16); HB = [Buf() for _ in range(NJ)]
                oT = fw.sb("oT", [128, 8, NG], F32); OTB = [Buf() for _ in range(8)]
                t1r = fw.tiles("t1", 2, [128, NG], F32)
                ytr = fw.tiles("yt", 1, [128, D], F32)
                scs = fw.sb("scs", [128, NSEQ, 2, NJ], F32); SCB = Buf()
                PS = Rot([(fw.ps("UA%d" % i, [128, 512], F32), Buf()) for i in range(5)])
                PSQ = Rot([(fw.ps("UB%d" % i, [128, 512], F32), Buf()) for i in range(1)])
                PY = Rot([(fw.ps("UC%d" % i, [128, 1024], F32), Buf()) for i in range(1)])
                fw.op("dve", "memset", gprev[:], 0.0, W=GPB)

                def group(g512, N, c0, nseq, ydst_fn, xm_buf):
                    Lq = N // nseq
                    hT, HTB = hTr.next()
                    fw.dma("sp", hT[:, :, :N], HT.rearrange("k p t -> p k t")[:, :, c0:c0 + N], R=[B_HT[g512]], W=[HTB])
                    _cp(19.5)
                    pq, PQB = PSQ.next()
                    for c in range(8):
                        sq, SQB = sqr.next()
                        ACT(AF.Square, sq[:, :N], hT[:, c, :N], R=[HTB], W=[SQB])
                        MM(pq[:, :N], ones16[:, :], sq[:, :N], start=(c == 0), stop=(c == 7), R=[CB, SQB], W=[PQB], signal=(c == 7))
                    _cp(19.7)
                    rs, RSB = rsr.next()
                    rstd_from_ssq(rs[:, :N], pq[:, :N], D, lnv[:, :N], R=[PQB], W=[RSB], WL=LNB)
                    _cp(19.8)
                    for c in range(8):
                        DVE("scalar_tensor_tensor", hn[:, c, :N], hT[:, c, :N], P["wpreffn"][:, c:c + 1], rs[:, :N], op0=ALU.mult, op1=ALU.mult,
                            R=[HTB, CB, RSB], W=[HNB])
                    _cp(20)
                    if nseq > 1:
                        for s_ in range(NSEQ):
                            for t_ in range(2):
                                fw.dma("sp", scs[:, s_, t_, :], sc[l, s_, t_].rearrange("(c p) -> p c", p=128), W=[SCB], **SLOW)
                    for j in range(NJ):
                        pu, PUB = PS.next()
                        for k in range(8):
                            MM(pu[:, :N], Wu[:, k, j * 128:(j + 1) * 128], hn[:, k, :N], start=(k == 0), stop=(k == 7), R=[WB, HNB], W=[PUB], signal=(k == 7))
                        pg, PGB = PS.next()
                        for k in range(8):
                            MM(pg[:, :N], Wu[:, k, DFF + j * 128:DFF + (j + 1) * 128], hn[:, k, :N], start=(k == 0), stop=(k == 7), R=[WB, HNB], W=[PGB], signal=(k == 7))
                        gb, GBB = gbr.next()
                        gbv = gb[:, :nseq * (Lq + 2)].rearrange("p (s t) -> p s t", t=Lq + 2)
                        if nseq == 1:
                            fw.op("act", "copy", gbv[:, :, 0:2], gprev[:, j:j + 1, :], R=[GPB[j]], W=[GBB])
                        else:
                            fw.op("act", "copy", gbv[:, :, 0:2], scs[:, :, :, j], R=[SCB], W=[GBB])
                        fw.op("act", "copy", gbv[:, :, 2:Lq + 2], pg[:, :N].rearrange("p (s t) -> p s t", t=Lq), R=[PGB], W=[GBB])
                        acc, ACB = accr.next()
                        av = acc[:, :N].rearrange("p (s t) -> p s t", t=Lq)
                        DVE("tensor_scalar", av, gbv[:, :, 2:Lq + 2], P["cw"][:, 2, j:j + 1], P["cb"][:, j:j + 1], op0=ALU.mult, op1=ALU.add, R=[GBB, CB], W=[ACB])
                        DVE("scalar_tensor_tensor", av, gbv[:, :, 1:Lq + 1], P["cw"][:, 1, j:j + 1], av, op0=ALU.mult, op1=ALU.add, R=[GBB, CB, ACB], W=[ACB])
                        DVE("scalar_tensor_tensor", av, gbv[:, :, 0:Lq], P["cw"][:, 0, j:j + 1], av, op0=ALU.mult, op1=ALU.add, R=[GBB, CB, ACB], W=[ACB])
                        if nseq == 1:
                            fw.op("pool", "tensor_copy", gprev[:, j:j + 1, :], gbv[:, :, Lq:Lq + 2], R=[GBB], W=[GPB[j]])
                        else:
                            fw.op("pool", "tensor_copy", scs[:, :, :, j], gbv[:, :, Lq:Lq + 2], R=[GBB], W=[SCB])
                        gel, GLB = ger.next()
                        ACT(AF.Gelu_apprx_tanh, gel[:, :N], acc[:, :N], R=[ACB], W=[GLB])
                        DVE("tensor_tensor", hid[:, j, :N], gel[:, :N], pu[:, :N], op=ALU.mult, R=[GLB, PUB], W=[HB[j]])
                        _cp(21)
                    if nseq > 1:
                        for s_ in range(NSEQ):
                            for t_ in range(2):
                                fw.dma("sp", conv_s[l, s_, t_].rearrange("(c p) -> p c", p=128), scs[:, s_, t_, :], R=[SCB], W=[B_OUT], **SLOW)
                    _cp(22)
                    pq, PQB = PSQ.next()
                    for c in range(8):
                        ps, PB = PS.next()
                        for j in range(NJ):
                            MM(ps[:, :N], Wd[:, j, c * 128:(c + 1) * 128], hid[:, j, :N], start=(j == 0), stop=(j == NJ - 1), R=[WB, HB[j]], W=[PB], signal=(j == NJ - 1))
                        fw.op("act", "copy", oT[:, c, :N], ps[:, :N], R=[PB], W=[OTB[c]])
                        sq, SQB = sqr.next()
                        ACT(AF.Square, sq[:, :N], ps[:, :N], R=[PB], W=[SQB])
                        MM(pq[:, :N], ones16[:, :], sq[:, :N], start=(c == 0), stop=(c == 7), R=[CB, SQB], W=[PQB], signal=(c == 7))
                    rs, RSB = rsr.next()
                    rstd_from_ssq(rs[:, :N], pq[:, :N], D, lnv[:, :N], R=[PQB], W=[RSB], WL=LNB)
                    for c in range(8):
                        t1, T1B = t1r.next()
                        DVE("scalar_tensor_tensor", t1[:, :N], oT[:, c, :N], P["wpostffn"][:, c:c + 1], rs[:, :N], op0=ALU.mult, op1=ALU.mult,
                            R=[OTB[c], CB, RSB], W=[T1B])
                        fw.op("pool", "tensor_tensor", oT[:, c, :N], t1[:, :N], hT[:, c, :N], op=ALU.add, R=[T1B, HTB], W=[OTB[c]])
                    _cp(23)
                    for tt in range((N + 127) // 128):
                        n = min(128, N - tt * 128)
                        py, PYB = PY.next()
                        for c in range(8):
                            TRP(py[:n, c * 128:(c + 1) * 128], oT[:, c, tt * 128:tt * 128 + n], id32[:, :], R=[OTB[c], CB], W=[PYB], signal=(c == 7))
                        yt, YTB = ytr.next()
                        fw.op("act", "copy", yt[:n, :], py[:n, :], R=[PYB], W=[YTB])
                        dst, DB = ydst_fn(tt, n)
                        fw.dma("sp", dst, yt[:n, :], R=[YTB], W=[DB])

                for g in range(T // NG):
                    c0 = g * NG

                    def ydst(tt, n, c0=c0, g=g):
                        if last:
                            return y_p[c0 + tt * 128:c0 + tt * 128 + n, :], B_OUT
                        return XMID[c0 + tt * 128:c0 + tt * 128 + n, :], B_XMID[g]
                    try:
                        group(c0 // 512, NG, c0, 1, ydst, None)
                    except _Stop:
                        pass
                import os
                for t_ in range(2):
                    if os.environ.get("NOCONVP"):
                        break
                    fw.dma("sp", conv_p[l, t_].rearrange("(c p) -> p c", p=128), gprev[:, :, t_], R=GPB, W=[B_OUT], **SLOW)
                if sample:
                    def ydst_s(tt, n):
                        if last:
                            return y_s[0:n, :], B_OUT
                        return XMID[T:T + n, :], B_XMID[2 * G]
                    group(SG, NS, T, NSEQ, ydst_s, None)
                fw.flush()
            fw.stack = fw.gstack

        for l in range(layers):
            if "A" in phases:
                phase_A(l)
            if "B" in phases:
                phase_B(l)
            if "C" in phases:
                phase_D1(l)
            if "D" in phases:
                phase_D2(l)
        fw.flush(final=True)
        n_ops = fw.n_ops
    return nc, n_ops


def _consts():
    c = {}
    c["c_ident"] = np.eye(128, dtype=np.float32)
    s = np.arange(128)
    c["c_U"] = (s[:, None] <= s[None, :]).astype(np.float32)
    c["c_L"] = (s[:, None] > s[None, :]).astype(np.float32)
    r = np.arange(NS)
    same = (r[:, None] // LS) == (r[None, :] // LS)
    c["c_Us"] = (same & (r[:, None] <= r[None, :])).astype(np.float32)
    c["c_Ls"] = (same & (r[:, None] > r[None, :])).astype(np.float32)
    c["c_ind"] = ((r[:, None] // LS) == np.arange(NSEQ)[None, :]).astype(np.float32)
    half = 8
    inv = (np.float32(500000.0) ** (-np.arange(half, dtype=np.float32) * np.float32(2.0) / np.float32(16))).astype(np.float32)
    pos = np.arange(T_FULL, dtype=np.float32)
    ang = (pos[:, None] * inv[None, :]).astype(np.float32)
    c["c_cos"] = np.cos(ang).astype(np.float32); c["c_sin"] = np.sin(ang).astype(np.float32)
    pos_s = (PAST + (r % LS)).astype(np.float32)
    ang_s = (pos_s[:, None] * inv[None, :]).astype(np.float32)
    c["c_cos_s"] = np.cos(ang_s).astype(np.float32); c["c_sin_s"] = np.sin(ang_s).astype(np.float32)
    return c


def make_in_maps(inp, n_cores=8):
    f = lambda a: np.ascontiguousarray(np.asarray(a, dtype=np.float32))
    shared = {
        "w_in": f(inp["w_in"]), "w_gk2": f(inp["w_gk2"]), "b_gk2": f(inp["b_gk2"]),
        "lq1": f(inp["lambda_q1"]), "lk1": f(inp["lambda_k1"]), "lq2": f(inp["lambda_q2"]), "lk2": f(inp["lambda_k2"]),
        "da_norm_w": f(inp["da_norm_w"]), "gla_norm_w": f(inp["gla_norm_w"]), "w_o": f(inp["w_o"]),
        "pre_mix_w": f(inp["pre_mix_w"]), "post_mix_w": f(inp["post_mix_w"]),
        "pre_ffn_w": f(inp["pre_ffn_w"]), "post_ffn_w": f(inp["post_ffn_w"]),
        "w_up": f(inp["w_up"]), "conv_w": f(inp["conv_w"]), "conv_b": f(inp["conv_b"]), "w_down": f(inp["w_down"]),
    }
    shared.update(_consts())
    maps = []
    for b in range(n_cores):
        m = dict(shared)
        m["x_p"] = f(inp["x_prompt"][b])
        m["x_s"] = f(inp["x_sample"][4 * b:4 * b + 4]).reshape(NS, D)
        m["ck"] = f(inp["cache_k"][:, 4 * b:4 * b + 4]).reshape(DEPTH, NSEQ, PAST, D)
        m["cv"] = f(inp["cache_v"][:, 4 * b:4 * b + 4]).reshape(DEPTH, NSEQ, PAST, D)
        m["sg"] = f(inp["state_gla"][:, 4 * b:4 * b + 4])
        m["sc"] = f(inp["state_conv"][:, 4 * b:4 * b + 4])
        maps.append(m)
    return maps


_NC_CACHE = {}


def kernel(**inputs):
    n = 8
    if "nc" not in _NC_CACHE:
        _NC_CACHE["nc"] = build()[0]
    nc = _NC_CACHE["nc"]
    in_maps = make_in_maps(inputs, n)
    res = run_bass_kernel_spmd(nc, in_maps, core_ids=list(range(n))).results
    B = n
    y_prompt = np.stack([res[b]["y_p"] for b in range(B)]).astype(np.float32)
    y_sample = np.concatenate([res[b]["y_s"].reshape(NSEQ, LS, D) for b in range(B)], axis=0).astype(np.float32)
    k_prompt = np.stack([res[b]["k_p"].reshape(DEPTH, T_FULL, 8, 128) for b in range(B)], axis=1).astype(np.float32)
    v_prompt = np.stack([res[b]["v_p"].reshape(DEPTH, T_FULL, 8, 128) for b in range(B)], axis=1).astype(np.float32)
    gla_prompt = np.stack([res[b]["gla_p"] for b in range(B)], axis=1).astype(np.float32)
    conv_prompt = np.stack([res[b]["conv_p"] for b in range(B)], axis=1).astype(np.float32)
    k_sample = np.concatenate([res[b]["k_s"].reshape(DEPTH, NSEQ, LS, 8, 128) for b in range(B)], axis=1).astype(np.float32)
    v_sample = np.concatenate([res[b]["v_s"].reshape(DEPTH, NSEQ, LS, 8, 128) for b in range(B)], axis=1).astype(np.float32)
    gla_sample = np.concatenate([res[b]["gla_s"] for b in range(B)], axis=1).astype(np.float32)
    conv_sample = np.concatenate([res[b]["conv_s"] for b in range(B)], axis=1).astype(np.float32)
    return (y_prompt, y_sample, k_prompt, v_prompt, gla_prompt, conv_prompt, k_sample, v_sample, gla_sample, conv_sample)
```
